# Optimizing a Trainium2 kernel written in Bass

```python
import math
import jax, jax.numpy as jnp
from jax import lax
import numpy as np

D_MODEL = 1024
BATCH = 2
SEQ = 8192
DEPTH = 2
DEC_BATCH = 32
DEC_SEQ = 8
PAST_LEN = 8192
PAGE_SIZE = 128

N_MIXERS = 2
N_NSA_LAYERS = (DEPTH + 1) // 2
N_DIFF_LAYERS = DEPTH // 2
NSA_HEADS = 16
NSA_KV_HEADS = 2
NSA_HEAD_DIM = D_MODEL // NSA_HEADS
NSA_GROUP = NSA_HEADS // NSA_KV_HEADS
CMP_BLOCK = 32
CMP_STRIDE = 16
CMP_HIDDEN = 4 * NSA_HEAD_DIM
SEL_BLOCK = 64
SEL_TOPK = 16
WINDOW = 512
NSA_IN = NSA_HEADS * NSA_HEAD_DIM + 6 * NSA_KV_HEADS * NSA_HEAD_DIM + 3 * NSA_HEADS
DIFF_HEADS = 8
DIFF_DIM = D_MODEL // (2 * DIFF_HEADS)
DIFF_IN = 3 * D_MODEL
D_FF = 4 * D_MODEL
Q_BLOCK = 128
ROPE_THETA = 10000.0
EPS = 1e-6
NEG = -1e30
FORCE = 1e6

kernel_name = "nsa_diffattn_hybrid_step"


def rms_norm(x, g):
    xf = x.astype(jnp.float32)
    y = xf * lax.rsqrt(jnp.mean(xf * xf, axis=-1, keepdims=True) + EPS)
    return (y * g.astype(jnp.float32)).astype(x.dtype)


def rope(x, pos):
    half = x.shape[-1] // 2
    inv = ROPE_THETA ** (-jnp.arange(half, dtype=jnp.float32) / half)
    ang = pos.astype(jnp.float32)[:, None] * inv
    ang = ang.reshape(ang.shape[:1] + (1,) * (x.ndim - 3) + (half,))
    cos, sin = jnp.cos(ang), jnp.sin(ang)
    xf = x.astype(jnp.float32)
    x1, x2 = xf[..., :half], xf[..., half:]
    return jnp.concatenate([x1 * cos - x2 * sin, x2 * cos + x1 * sin], axis=-1).astype(x.dtype)


def gather_pages(cache, page_table):
    pages = cache[page_table]
    return pages.reshape((page_table.shape[0], -1) + cache.shape[2:])


def pad_rows(x, n):
    return jnp.pad(x, ((0, 0), (0, n)) + ((0, 0),) * (x.ndim - 2))


def sq_relu_mlp(h, w_up, w_down):
    u = jax.nn.relu(h @ w_up)
    return (u * u) @ w_down


def nsa_project(h, w_in, pos):
    B, T, _ = h.shape
    H, Hk, dh = NSA_HEADS, NSA_KV_HEADS, NSA_HEAD_DIM
    z = h @ w_in
    nq, nkv = H * dh, 6 * Hk * dh
    q = rope(z[..., :nq].reshape(B, T, H, dh), pos)
    kv = z[..., nq:nq + nkv].reshape(B, T, 3, 2, Hk, dh)
    gates = jax.nn.sigmoid(z[..., nq + nkv:].astype(jnp.float32)).reshape(B, T, H, 3)
    cmp_kv = kv[:, :, 0]
    sel_kv = jnp.stack([rope(kv[:, :, 1, 0], pos), kv[:, :, 1, 1]], axis=2)
    win_kv = jnp.stack([rope(kv[:, :, 2, 0], pos), kv[:, :, 2, 1]], axis=2)
    return q, gates, cmp_kv, sel_kv, win_kv


def compress(rows, pe, w1, w2):
    B, T, Hk, dh = rows.shape
    r = CMP_BLOCK // CMP_STRIDE
    chunks = rows.reshape(B, T // CMP_STRIDE, CMP_STRIDE, Hk, dh)
    n_cmp = T // CMP_STRIDE - r + 1
    blocks = jnp.concatenate([chunks[:, s:s + n_cmp] for s in range(r)], axis=2)
    blocks = blocks + pe[:, None, :]
    flat = blocks.transpose(0, 1, 3, 2, 4).reshape(B, n_cmp, Hk, CMP_BLOCK * dh)
    return jax.nn.silu(flat @ w1) @ w2


def nsa_summaries(cmp_kv, sel_kv, pe, w1, w2):
    B, T, _, Hk, dh = cmp_kv.shape
    kc = compress(cmp_kv[:, :, 0], pe[0], w1[0], w2[0])
    vc = compress(cmp_kv[:, :, 1], pe[1], w1[1], w2[1])
    kc = rope(kc, jnp.arange(kc.shape[1]) * CMP_STRIDE)
    ks_blk = sel_kv[:, :, 0].reshape(B, T // SEL_BLOCK, SEL_BLOCK, Hk, dh)
    vs_blk = sel_kv[:, :, 1].reshape(B, T // SEL_BLOCK, SEL_BLOCK, Hk, dh)
    return kc, vc, ks_blk, vs_blk


def nsa_branches(q, q_pos, gates, kc, vc, ks_blk, vs_blk, kw, vw, w_pos):
    f32 = jnp.float32
    B, Tq, H, dh = q.shape
    Hk = kc.shape[2]
    G = H // Hk
    qg = q.reshape(B, Tq, Hk, G, dh).astype(f32) * (dh ** -0.5)

    n_cmp = kc.shape[1]
    c_end = jnp.arange(n_cmp) * CMP_STRIDE + CMP_BLOCK - 1
    c_ok = c_end[None, :] <= q_pos[:, None]
    s = jnp.einsum('btkgd,bckd->bkgtc', qg, kc.astype(f32))
    p_cmp = jax.nn.softmax(jnp.where(c_ok, s, NEG), axis=-1)
    p_cmp = jnp.where(jnp.any(c_ok, axis=-1)[:, None], p_cmp, 0.0)
    o_cmp = jnp.einsum('bkgtc,bckd->btkgd', p_cmp, vc.astype(f32))

    n_sel = ks_blk.shape[1]
    ci = jnp.arange(n_cmp)[:, None]
    sj = jnp.arange(n_sel)[None, :]
    overlap = ((ci * CMP_STRIDE < (sj + 1) * SEL_BLOCK) & (ci * CMP_STRIDE + CMP_BLOCK > sj * SEL_BLOCK)).astype(f32)
    imp = jnp.einsum('bkgtc,cs->bkts', p_cmp, overlap)
    cur = q_pos // SEL_BLOCK
    j = jnp.arange(n_sel)[None, :]
    forced = (j == 0) | (j == cur[:, None]) | (j == cur[:, None] - 1)
    causal_blk = j * SEL_BLOCK <= q_pos[:, None]
    imp = jnp.where(forced, FORCE, jnp.where(causal_blk, imp, -1.0))
    _, idx = lax.top_k(imp, min(SEL_TOPK, n_sel))
    ksb = ks_blk.transpose(0, 3, 1, 2, 4)
    vsb = vs_blk.transpose(0, 3, 1, 2, 4)
    bi = jnp.arange(B)[:, None, None, None]
    hi = jnp.arange(Hk)[None, :, None, None]
    kg = ksb[bi, hi, idx].astype(f32)
    vg = vsb[bi, hi, idx].astype(f32)
    key_pos = idx[..., None] * SEL_BLOCK + jnp.arange(SEL_BLOCK)
    ok = key_pos <= q_pos[:, None, None]
    s = jnp.einsum('btkgd,bktnjd->bkgtnj', qg, kg)
    s = jnp.where(ok[:, :, None], s, NEG)
    p = jax.nn.softmax(s.reshape(s.shape[:4] + (-1,)), axis=-1).reshape(s.shape)
    o_sel = jnp.einsum('bkgtnj,bktnjd->btkgd', p, vg)

    ok_w = (w_pos[None, :] <= q_pos[:, None]) & (w_pos[None, :] > q_pos[:, None] - WINDOW) & (w_pos[None, :] >= 0)
    s = jnp.einsum('btkgd,bckd->bkgtc', qg, kw.astype(f32))
    p = jax.nn.softmax(jnp.where(ok_w, s, NEG), axis=-1)
    o_win = jnp.einsum('bkgtc,bckd->btkgd', p, vw.astype(f32))

    g = gates.reshape(B, Tq, Hk, G, 3)
    out = o_cmp * g[..., 0:1] + o_sel * g[..., 1:2] + o_win * g[..., 2:3]
    return out.reshape(B, Tq, H * dh)


def nsa_prompt(h, w_in, pe, w1, w2, w_out):
    B, T, _ = h.shape
    pos = jnp.arange(T)
    q, gates, cmp_kv, sel_kv, win_kv = nsa_project(h, w_in, pos)
    kc, vc, ks_blk, vs_blk = nsa_summaries(cmp_kv, sel_kv, pe, w1, w2)
    win_pad = jnp.pad(win_kv, ((0, 0), (WINDOW, 0), (0, 0), (0, 0), (0, 0)))

    def block(i):
        q0 = i * Q_BLOCK
        qb = lax.dynamic_slice_in_dim(q, q0, Q_BLOCK, axis=1)
        gb = lax.dynamic_slice_in_dim(gates, q0, Q_BLOCK, axis=1)
        wb = lax.dynamic_slice_in_dim(win_pad, q0, WINDOW + Q_BLOCK, axis=1)
        q_pos = q0 + jnp.arange(Q_BLOCK)
        w_pos = q0 - WINDOW + jnp.arange(WINDOW + Q_BLOCK)
        return nsa_branches(qb, q_pos, gb, kc, vc, ks_blk, vs_blk, wb[:, :, 0], wb[:, :, 1], w_pos)

    o = lax.map(block, jnp.arange(T // Q_BLOCK))
    o = o.transpose(1, 0, 2, 3).reshape(B, T, -1).astype(h.dtype)
    return o @ w_out, cmp_kv, sel_kv, win_kv[:, T - min(WINDOW, T):]


def nsa_sample(h, cache_cmp, cache_sel, win_buf, page_table, w_in, pe, w1, w2, w_out):
    B, Tn, _ = h.shape
    past = page_table.shape[1] * cache_cmp.shape[1]
    pos = past + jnp.arange(Tn)
    q, gates, cmp_kv, sel_kv, win_kv = nsa_project(h, w_in, pos)
    t_full = past + Tn
    n_pad = -(-t_full // SEL_BLOCK) * SEL_BLOCK - t_full
    full_cmp = pad_rows(jnp.concatenate([gather_pages(cache_cmp, page_table), cmp_kv], axis=1), n_pad)
    full_sel = pad_rows(jnp.concatenate([gather_pages(cache_sel, page_table), sel_kv], axis=1), n_pad)
    kc, vc, ks_blk, vs_blk = nsa_summaries(full_cmp, full_sel, pe, w1, w2)
    w_buf = win_buf.shape[1]
    w_all = jnp.concatenate([win_buf, win_kv], axis=1)
    w_pos = past - w_buf + jnp.arange(w_buf + Tn)
    o = nsa_branches(q, pos, gates, kc, vc, ks_blk, vs_blk, w_all[:, :, 0], w_all[:, :, 1], w_pos)
    return o.astype(h.dtype) @ w_out, cmp_kv, sel_kv, w_all[:, Tn:]


def diff_project(h, w_in, pos):
    B, T, _ = h.shape
    z = h @ w_in
    q = rope(z[..., :D_MODEL].reshape(B, T, DIFF_HEADS, 2, DIFF_DIM), pos)
    k = rope(z[..., D_MODEL:2 * D_MODEL].reshape(B, T, DIFF_HEADS, 2, DIFF_DIM), pos)
    v = z[..., 2 * D_MODEL:].reshape(B, T, DIFF_HEADS, 2 * DIFF_DIM)
    kv = jnp.stack([k.reshape(B, T, DIFF_HEADS, 2 * DIFF_DIM), v], axis=2)
    return q, kv


def diff_lambda_value(lam, lam_init):
    lf = lam.astype(jnp.float32)
    return jnp.exp(jnp.sum(lf[0] * lf[1])) - jnp.exp(jnp.sum(lf[2] * lf[3])) + lam_init


def diff_attend(q, q_pos, kv, k_pos, lam):
    f32 = jnp.float32
    B, Tk = kv.shape[:2]
    k = kv[:, :, 0].reshape(B, Tk, DIFF_HEADS, 2, DIFF_DIM).astype(f32)
    v = kv[:, :, 1].astype(f32)
    s = jnp.einsum('bthcd,bshcd->bhcts', q.astype(f32) * (DIFF_DIM ** -0.5), k)
    s = jnp.where(k_pos[None, :] <= q_pos[:, None], s, NEG)
    a = jax.nn.softmax(s, axis=-1)
    w = a[:, :, 0] - lam * a[:, :, 1]
    return jnp.einsum('bhts,bshe->bthe', w, v)


def diff_out(o, subnorm, lam_init, w_out, dtype):
    B, T = o.shape[:2]
    o = rms_norm(o, subnorm) * (1.0 - lam_init)
    return o.reshape(B, T, -1).astype(dtype) @ w_out


def diff_prompt(h, w_in, lam_p, subnorm, w_out, lam_init):
    B, T, _ = h.shape
    pos = jnp.arange(T)
    q, kv = diff_project(h, w_in, pos)
    lam = diff_lambda_value(lam_p, lam_init)

    def block(i):
        q0 = i * Q_BLOCK
        qb = lax.dynamic_slice_in_dim(q, q0, Q_BLOCK, axis=1)
        return diff_attend(qb, q0 + jnp.arange(Q_BLOCK), kv, pos, lam)

    o = lax.map(block, jnp.arange(T // Q_BLOCK))
    o = o.transpose(1, 0, 2, 3, 4).reshape(B, T, DIFF_HEADS, 2 * DIFF_DIM)
    return diff_out(o, subnorm, lam_init, w_out, h.dtype), kv


def diff_sample(h, cache_kv, page_table, w_in, lam_p, subnorm, w_out, lam_init):
    B, Tn, _ = h.shape
    past = page_table.shape[1] * cache_kv.shape[1]
    pos = past + jnp.arange(Tn)
    q, kv = diff_project(h, w_in, pos)
    lam = diff_lambda_value(lam_p, lam_init)
    kv_full = jnp.concatenate([gather_pages(cache_kv, page_table), kv], axis=1)
    o = diff_attend(q, pos, kv_full, jnp.arange(past + Tn), lam)
    return diff_out(o, subnorm, lam_init, w_out, h.dtype), kv


def setup_inputs(seed: int = 0) -> dict:
    key = jax.random.key(seed)
    ks = jax.random.split(key, 24)
    nrm = jax.random.normal
    f32 = jnp.float32
    n_pages = PAST_LEN // PAGE_SIZE
    n_used = DEC_BATCH * n_pages
    n_pool = n_used + n_used // 4
    perm = jax.random.permutation(ks[0], n_pool)
    page_table = perm[:n_used].reshape(DEC_BATCH, n_pages).astype(jnp.int32)
    Hk, dh = NSA_KV_HEADS, NSA_HEAD_DIM
    return {
        "x_prompt": nrm(ks[1], (BATCH, SEQ, D_MODEL), f32),
        "x_sample": nrm(ks[2], (DEC_BATCH, DEC_SEQ, D_MODEL), f32),
        "cache_nsa_cmp": nrm(ks[3], (N_NSA_LAYERS, n_pool, PAGE_SIZE, 2, Hk, dh), f32),
        "cache_nsa_sel": nrm(ks[4], (N_NSA_LAYERS, n_pool, PAGE_SIZE, 2, Hk, dh), f32),
        "state_nsa_win": nrm(ks[5], (N_NSA_LAYERS, DEC_BATCH, min(WINDOW, PAST_LEN), 2, Hk, dh), f32),
        "cache_diff_kv": nrm(ks[6], (N_DIFF_LAYERS, n_pool, PAGE_SIZE, 2, DIFF_HEADS, 2 * DIFF_DIM), f32),
        "page_table": page_table,
        "norm_mix": 1.0 + 0.02 * nrm(ks[7], (DEPTH, D_MODEL), f32),
        "norm_mlp": 1.0 + 0.02 * nrm(ks[8], (DEPTH, D_MODEL), f32),
        "norm_final": 1.0 + 0.02 * nrm(ks[9], (D_MODEL,), f32),
        "nsa_w_in": nrm(ks[10], (N_NSA_LAYERS, D_MODEL, NSA_IN), f32) * D_MODEL ** -0.5,
        "nsa_cmp_pe": 0.1 * nrm(ks[11], (N_NSA_LAYERS, 2, CMP_BLOCK, dh), f32),
        "nsa_cmp_w1": nrm(ks[12], (N_NSA_LAYERS, 2, CMP_BLOCK * dh, CMP_HIDDEN), f32) * (CMP_BLOCK * dh) ** -0.5,
        "nsa_cmp_w2": nrm(ks[13], (N_NSA_LAYERS, 2, CMP_HIDDEN, dh), f32) * CMP_HIDDEN ** -0.5,
        "nsa_w_out": nrm(ks[14], (N_NSA_LAYERS, NSA_HEADS * dh, D_MODEL), f32) * (NSA_HEADS * dh) ** -0.5,
        "diff_w_in": nrm(ks[15], (N_DIFF_LAYERS, D_MODEL, DIFF_IN), f32) * D_MODEL ** -0.5,
        "diff_lambda": 0.1 * nrm(ks[16], (N_DIFF_LAYERS, 4, DIFF_DIM), f32),
        "diff_subnorm": 1.0 + 0.02 * nrm(ks[17], (N_DIFF_LAYERS, 2 * DIFF_DIM), f32),
        "diff_w_out": nrm(ks[18], (N_DIFF_LAYERS, D_MODEL, D_MODEL), f32) * D_MODEL ** -0.5,
        "mlp_w_up": nrm(ks[19], (DEPTH, D_MODEL, D_FF), f32) * D_MODEL ** -0.5,
        "mlp_w_down": nrm(ks[20], (DEPTH, D_FF, D_MODEL), f32) * D_FF ** -0.5,
    }


def reference(x_prompt, x_sample, cache_nsa_cmp, cache_nsa_sel, state_nsa_win, cache_diff_kv, page_table,
              norm_mix, norm_mlp, norm_final, nsa_w_in, nsa_cmp_pe, nsa_cmp_w1, nsa_cmp_w2, nsa_w_out,
              diff_w_in, diff_lambda, diff_subnorm, diff_w_out, mlp_w_up, mlp_w_down):
    hp, hs = x_prompt, x_sample
    cmp_p, cmp_s, sel_p, sel_s, win_p, win_s, dkv_p, dkv_s = [], [], [], [], [], [], [], []
    for i in range(DEPTH):
        li = i // N_MIXERS
        ap = rms_norm(hp, norm_mix[i])
        a_s = rms_norm(hs, norm_mix[i])
        if i % N_MIXERS == 0:
            mp, c_p, s_p, w_p = nsa_prompt(ap, nsa_w_in[li], nsa_cmp_pe[li], nsa_cmp_w1[li], nsa_cmp_w2[li], nsa_w_out[li])
            ms, c_s, s_s, w_s = nsa_sample(a_s, cache_nsa_cmp[li], cache_nsa_sel[li], state_nsa_win[li], page_table,
                                           nsa_w_in[li], nsa_cmp_pe[li], nsa_cmp_w1[li], nsa_cmp_w2[li], nsa_w_out[li])
            cmp_p.append(c_p); cmp_s.append(c_s); sel_p.append(s_p); sel_s.append(s_s)
            win_p.append(w_p); win_s.append(w_s)
        else:
            lam_init = 0.8 - 0.6 * math.exp(-0.3 * i)
            mp, kv_p = diff_prompt(ap, diff_w_in[li], diff_lambda[li], diff_subnorm[li], diff_w_out[li], lam_init)
            ms, kv_s = diff_sample(a_s, cache_diff_kv[li], page_table, diff_w_in[li], diff_lambda[li],
                                   diff_subnorm[li], diff_w_out[li], lam_init)
            dkv_p.append(kv_p); dkv_s.append(kv_s)
        hp = hp + mp
        hs = hs + ms
        hp = hp + sq_relu_mlp(rms_norm(hp, norm_mlp[i]), mlp_w_up[i], mlp_w_down[i])
        hs = hs + sq_relu_mlp(rms_norm(hs, norm_mlp[i]), mlp_w_up[i], mlp_w_down[i])
    y_prompt = rms_norm(hp, norm_final)
    y_sample = rms_norm(hs, norm_final)
    new_nsa_cmp_prompt = jnp.stack(cmp_p)
    new_nsa_cmp_sample = jnp.stack(cmp_s)
    new_nsa_sel_prompt = jnp.stack(sel_p)
    new_nsa_sel_sample = jnp.stack(sel_s)
    new_nsa_win_prompt = jnp.stack(win_p)
    new_nsa_win_sample = jnp.stack(win_s)
    new_diff_kv_prompt = jnp.stack(dkv_p)
    new_diff_kv_sample = jnp.stack(dkv_s)
    return (y_prompt, y_sample, new_nsa_cmp_prompt, new_nsa_cmp_sample, new_nsa_sel_prompt, new_nsa_sel_sample,
            new_nsa_win_prompt, new_nsa_win_sample, new_diff_kv_prompt, new_diff_kv_sample)
```

```python
import numpy as np
from contextlib import ExitStack
import concourse.bass as bass
import concourse.mybir as mybir
from concourse.bass_utils import run_bass_kernel_spmd

F32 = mybir.dt.float32
BF16 = mybir.dt.bfloat16
I32 = mybir.dt.int32
AF = mybir.ActivationFunctionType
ALU = mybir.AluOpType

NCORES = 8
D = 1024
SEQ = 8192
PAST = 8192
DEC_B, DEC_T = 32, 8
NSA_IN = 1840
DIFF_IN = 3072
DFF = 4096
WIN = 512
EPS = 1e-6
NPT = 16
ROWS = NPT * 128 + 32
THETA = 10000.0
NEGB = -30000.0


def core_tiles(c):
    out = []
    for b in range(2):
        for u in sorted([c, 15 - c, 16 + c, 31 - c]):
            for s in range(2):
                out.append((b, u * 256 + s * 128))
    return out


def tile_rows():
    return [(i * 128, 128) for i in range(NPT)] + [(NPT * 128, 32)]


class Prog:
    ENG = ("sync", "scalar", "vector", "tensor", "gpsimd")

    def __init__(self):
        self.items = []
        self.cnt = {}
        self.lastw = {}
        self.readers = {}
        self.waited = {e: {} for e in self.ENG}

    def emit(self, eng, fn, reads=(), writes=(), inc=1, chan=None):
        key = (eng, chan)
        deps = {}

        def need(tok):
            if tok is None:
                return
            k, c = tok
            if k == ("tensor", None) and eng == "tensor":
                return
            if self.waited[eng].get(k, 0) >= c:
                return
            deps[k] = max(deps.get(k, 0), c)

        for b in reads:
            need(self.lastw.get(b))
        for b in writes:
            need(self.lastw.get(b))
            for k, c in self.readers.get(b, {}).items():
                need((k, c))
        for k, c in deps.items():
            self.waited[eng][k] = c
        c = self.cnt.get(key, 0) + inc
        self.cnt[key] = c
        for b in reads:
            self.readers.setdefault(b, {})[key] = c
        for b in writes:
            self.lastw[b] = (key, c)
            self.readers[b] = {}
        self.items.append((eng, deps, fn, key, inc))

    def dma(self, eng, fn, reads=(), writes=(), chan=None):
        assert chan is not None
        self.emit(eng, fn, reads=reads, writes=writes, inc=16, chan="d_" + chan)

    def replay(self, nc, es):
        sems = {k: es.enter_context(nc.semaphore("s%d" % i)) for i, k in enumerate(self.cnt)}
        block = es.enter_context(nc.Block())

        def make(name):
            def body(e):
                for eng, deps, fn, key, inc in self.items:
                    if eng != name:
                        continue
                    for k, c in deps.items():
                        e.wait_ge(sems[k], c)
                    fn(e).then_inc(sems[key], inc)
                if name == "sync":
                    for k, c in self.cnt.items():
                        e.wait_ge(sems[k], c)
            return body
        for name in self.ENG:
            getattr(block, name)(make(name))


def bc(ap, shape, axis):
    return ap.unsqueeze(axis).to_broadcast(list(shape))


def make_ident(P, nc, es):
    identf = es.enter_context(nc.sbuf_tensor("identf", [128, 128], F32))
    ident = es.enter_context(nc.sbuf_tensor("ident", [128, 128], BF16))
    P.emit("gpsimd", lambda e: e.memset(identf[:], 0.0), writes=["identf"])
    P.emit("gpsimd", lambda e: e.affine_select(out=identf[:], in_=identf[:], pattern=[[-1, 128]],
                                               compare_op=ALU.not_equal, fill=1.0, base=0, channel_multiplier=1),
           reads=["identf"], writes=["identf"])
    P.emit("vector", lambda e: e.tensor_copy(out=ident[:], in_=identf[:]), reads=["identf"], writes=["ident"])
    return ident


def emit_rmsnorm(P, xt, rows, gt, hn, ss, rstd, junk, tag):
    P.emit("vector", lambda e: e.memset(ss[rows, :], 0.0), writes=["ss" + tag])
    P.emit("scalar", lambda e: e.activation(out=junk[rows, :], in_=xt[rows, :], func=AF.Square, accum_out=ss[rows, 0:1]),
           reads=["xt" + tag, "ss" + tag], writes=["junk" + tag, "ss" + tag])
    P.emit("vector", lambda e: e.tensor_scalar(out=rstd[rows, :], in0=ss[rows, :], scalar1=1.0 / D, scalar2=EPS,
                                               op0=ALU.mult, op1=ALU.add), reads=["ss" + tag], writes=["rstd" + tag])
    P.emit("scalar", lambda e: e.activation(out=rstd[rows, :], in_=rstd[rows, :], func=AF.Sqrt),
           reads=["rstd" + tag], writes=["rstd" + tag])
    P.emit("vector", lambda e: e.reciprocal(out=rstd[rows, :], in_=rstd[rows, :]), reads=["rstd" + tag], writes=["rstd" + tag])
    P.emit("vector", lambda e: e.scalar_tensor_tensor(out=hn[rows, :], in0=xt[rows, :], scalar=rstd[rows, 0:1],
                                                      in1=gt[rows, :], op0=ALU.mult, op1=ALU.mult),
           reads=["xt" + tag, "rstd" + tag, "gt"], writes=["hn" + tag])


def build_proj(ncols, rope_sets, sig_cols):
    nc = bass.Bass("TRN2", target_bir_lowering=False)
    x = nc.dram_tensor("x", [ROWS, D], F32, kind="ExternalInput").ap()
    cs = nc.dram_tensor("cs", [ROWS, 32], F32, kind="ExternalInput").ap()
    sn = nc.dram_tensor("sn", [ROWS, 32], F32, kind="ExternalInput").ap()
    w = nc.dram_tensor("w", [D, ncols], F32, kind="ExternalInput").ap()
    g_rep = nc.dram_tensor("g_rep", [128, D], F32, kind="ExternalInput").ap()
    z_out = nc.dram_tensor("z_out", [ROWS, ncols], F32, kind="ExternalOutput").ap()
    P = Prog()
    with ExitStack() as es:
        sb = lambda n, s, d: es.enter_context(nc.sbuf_tensor(n, s, d))
        wb = sb("wb", [128, 8, ncols], BF16)
        gt = sb("gt", [128, D], F32)
        NB = 2
        xt = [sb(f"xt{i}", [128, D], F32) for i in range(NB)]
        junk = [sb(f"junk{i}", [128, D], F32) for i in range(NB)]
        hn = [sb(f"hn{i}", [128, D], BF16) for i in range(NB)]
        hnT = [sb(f"hnT{i}", [128, 8, 128], BF16) for i in range(NB)]
        cst = [sb(f"cst{i}", [128, 32], F32) for i in range(NB)]
        snt = [sb(f"snt{i}", [128, 32], F32) for i in range(NB)]
        ss = [sb(f"ss{i}", [128, 1], F32) for i in range(NB)]
        rstd = [sb(f"rstd{i}", [128, 1], F32) for i in range(NB)]
        zsb = [sb(f"zsb{i}", [128, ncols], F32) for i in range(NB)]
        tmp = [sb(f"tmp{i}", [128, 1024], F32) for i in range(4)]
        tp = [es.enter_context(nc.psum_tensor(f"tp{i}", [128, 8, 128], BF16)) for i in range(NB)]
        zp = [es.enter_context(nc.psum_tensor(f"zp{i}", [128, 512], F32)) for i in range(4)]
        ident = make_ident(P, nc, es)
        for k in range(8):
            P.dma("gpsimd", lambda e, k=k: e.dma_start(out=wb[:, k, :], in_=w[k * 128:(k + 1) * 128, :]), writes=["wb"], chan="wb")
        P.dma("sync", lambda e: e.dma_start(out=gt[:], in_=g_rep[:, :]), writes=["gt"], chan="gt")
        chunks = [(c0, min(512, ncols - c0)) for c0 in range(0, ncols, 512)]
        zi = 0
        for ti, (r0, R) in enumerate(tile_rows()):
            bi = ti % NB
            t = str(bi)
            rows = slice(0, R)
            P.dma("sync", lambda e, bi=bi, r0=r0, R=R: e.dma_start(out=xt[bi][0:R, :], in_=x[r0:r0 + R, :]), writes=["xt" + t], chan="xt" + t)
            P.dma("sync", lambda e, bi=bi, r0=r0, R=R: e.dma_start(out=cst[bi][0:R, :], in_=cs[r0:r0 + R, :]), writes=["cst" + t], chan="cst" + t)
            P.dma("sync", lambda e, bi=bi, r0=r0, R=R: e.dma_start(out=snt[bi][0:R, :], in_=sn[r0:r0 + R, :]), writes=["snt" + t], chan="snt" + t)
            emit_rmsnorm(P, xt[bi], rows, gt, hn[bi], ss[bi], rstd[bi], junk[bi], t)
            for k in range(8):
                P.emit("tensor", lambda e, k=k, R=R, bi=bi: e.transpose(out=tp[bi][:, k, 0:R], in_=hn[bi][0:R, k * 128:(k + 1) * 128],
                                                                        identity=ident[0:R, 0:R]),
                       reads=["hn" + t, "ident"], writes=["tp" + t])
            P.emit("scalar", lambda e, R=R, bi=bi: e.copy(out=hnT[bi][:, :, 0:R], in_=tp[bi][:, :, 0:R]), reads=["tp" + t], writes=["hnT" + t])
            for (c0, wd) in chunks:
                zb = zi % 4
                zi += 1
                for k in range(8):
                    P.emit("tensor", lambda e, zb=zb, k=k, c0=c0, wd=wd, R=R, bi=bi: e.matmul(
                        zp[zb][0:R, 0:wd], lhsT=hnT[bi][:, k, 0:R], rhs=wb[:, k, c0:c0 + wd], start=(k == 0), stop=(k == 7)),
                        reads=["hnT" + t, "wb"], writes=[f"zp{zb}"])
                P.emit("scalar", lambda e, zb=zb, c0=c0, wd=wd, rows=rows, bi=bi: e.copy(out=zsb[bi][rows, c0:c0 + wd], in_=zp[zb][rows, 0:wd]),
                       reads=[f"zp{zb}"], writes=["zsb" + t])
            for (col0, no, ostride, ni) in rope_sets:
                def views(bi=bi, col0=col0, no=no, ostride=ostride, ni=ni, R=R):
                    a1 = zsb[bi][0:R, col0:col0 + 1]
                    a2 = zsb[bi][0:R, col0 + 32:col0 + 33]
                    pat = [[ostride if no > 1 else 64 * ni, no], [64, ni], [1, 32]]
                    x1 = bass.AP(a1.tensor, a1.offset, [list(a1.ap[0])] + pat)
                    x2 = bass.AP(a2.tensor, a2.offset, [list(a2.ap[0])] + pat)
                    return x1, x2
                x1, x2 = views()
                ng = no * ni
                shp = [R, no, ni, 32]
                c4 = cst[bi][0:R, :].unsqueeze(1).unsqueeze(1).to_broadcast(shp)
                s4 = snt[bi][0:R, :].unsqueeze(1).unsqueeze(1).to_broadcast(shp)
                a = [tmp[i][0:R, 0:ng * 32].rearrange("p (o i f) -> p o i f", o=no, i=ni, f=32) for i in range(4)]
                P.emit("vector", lambda e, a=a, x1=x1, c4=c4: e.tensor_tensor(out=a[0], in0=x1, in1=c4, op=ALU.mult), reads=["zsb" + t, "cst" + t], writes=["tmp0"])
                P.emit("gpsimd", lambda e, a=a, x2=x2, s4=s4: e.tensor_tensor(out=a[1], in0=x2, in1=s4, op=ALU.mult), reads=["zsb" + t, "snt" + t], writes=["tmp1"])
                P.emit("vector", lambda e, a=a, x2=x2, c4=c4: e.tensor_tensor(out=a[2], in0=x2, in1=c4, op=ALU.mult), reads=["zsb" + t, "cst" + t], writes=["tmp2"])
                P.emit("gpsimd", lambda e, a=a, x1=x1, s4=s4: e.tensor_tensor(out=a[3], in0=x1, in1=s4, op=ALU.mult), reads=["zsb" + t, "snt" + t], writes=["tmp3"])
                P.emit("vector", lambda e, a=a, x1=x1: e.tensor_tensor(out=x1, in0=a[0], in1=a[1], op=ALU.subtract), reads=["tmp0", "tmp1"], writes=["zsb" + t])
                P.emit("gpsimd", lambda e, a=a, x2=x2: e.tensor_tensor(out=x2, in0=a[2], in1=a[3], op=ALU.add), reads=["tmp2", "tmp3"], writes=["zsb" + t])
            if sig_cols is not None:
                s0, s1 = sig_cols
                P.emit("scalar", lambda e, bi=bi, rows=rows, s0=s0, s1=s1: e.activation(out=zsb[bi][rows, s0:s1], in_=zsb[bi][rows, s0:s1], func=AF.Sigmoid),
                       reads=["zsb" + t], writes=["zsb" + t])
            P.dma("sync", lambda e, bi=bi, r0=r0, R=R: e.dma_start(out=z_out[r0:r0 + R, :], in_=zsb[bi][0:R, :]), reads=["zsb" + t], chan="zsb" + t)
        P.replay(nc, es)
    return nc


def build_post(final):
    nc = bass.Bass("TRN2", target_bir_lowering=False)
    x = nc.dram_tensor("x", [ROWS, D], F32, kind="ExternalInput").ap()
    o = nc.dram_tensor("o", [ROWS, D], F32, kind="ExternalInput").ap()
    w_out = nc.dram_tensor("w_out", [D, D], F32, kind="ExternalInput").ap()
    w_up = nc.dram_tensor("w_up", [D, DFF], F32, kind="ExternalInput").ap()
    w_down = nc.dram_tensor("w_down", [DFF, D], F32, kind="ExternalInput").ap()
    g_rep = nc.dram_tensor("g_rep", [128, D], F32, kind="ExternalInput").ap()
    gf_rep = nc.dram_tensor("gf_rep", [128, D], F32, kind="ExternalInput").ap()
    h_out = nc.dram_tensor("h_out", [ROWS, D], F32, kind="ExternalOutput").ap()
    P = Prog()
    with ExitStack() as es:
        sb = lambda n, s, d: es.enter_context(nc.sbuf_tensor(n, s, d))
        wo = sb("wo", [128, 8, D], BF16)
        wu = sb("wu", [128, 8, DFF], BF16)
        wd_ = sb("wd", [128, 32, D], BF16)
        gt = sb("gt", [128, D], F32)
        gf = sb("gf", [128, D], F32)
        xh = sb("xh", [128, 2, D], F32)
        ost = sb("ost", [128, D], F32)
        ob = sb("ob", [128, 2, D], BF16)
        aT = sb("aT", [128, 8, 256], BF16)
        uT = sb("uT", [128, 32, 256], BF16)
        rt = [sb(f"rt{i}", [128, 512], F32) for i in range(2)]
        ss = sb("ss", [128, 1], F32)
        rstd = sb("rstd", [128, 1], F32)
        tp = [es.enter_context(nc.psum_tensor(f"tp{i}", [128, 8, 128], BF16)) for i in range(2)]
        mp = [es.enter_context(nc.psum_tensor(f"mp{i}", [128, 512], F32)) for i in range(4)]
        ident = make_ident(P, nc, es)
        for k in range(8):
            P.dma("gpsimd", lambda e, k=k: e.dma_start(out=wo[:, k, :], in_=w_out[k * 128:(k + 1) * 128, :]), writes=["wo"], chan="wo")
        for k in range(8):
            for hh in range(2):
                P.dma("gpsimd", lambda e, k=k, hh=hh: e.dma_start(out=wu[:, k, hh * 2048:(hh + 1) * 2048], in_=w_up[k * 128:(k + 1) * 128, hh * 2048:(hh + 1) * 2048]),
                      writes=["wu"], chan="wu")
        for f in range(32):
            P.dma("gpsimd", lambda e, f=f: e.dma_start(out=wd_[:, f, :], in_=w_down[f * 128:(f + 1) * 128, :]), writes=["wd"], chan="wd")
        P.dma("sync", lambda e: e.dma_start(out=gt[:], in_=g_rep[:, :]), writes=["gt"], chan="gt")
        P.dma("sync", lambda e: e.dma_start(out=gf[:], in_=gf_rep[:, :]), writes=["gf"], chan="gf")
        groups = [[(2 * gi) * 128, 128, (2 * gi + 1) * 128, 128] for gi in range(NPT // 2)] + [[NPT * 128, 32]]
        mi = 0
        for grp in groups:
            tl = [(grp[i], grp[i + 1]) for i in range(0, len(grp), 2)]
            NT = sum(R for _, R in tl)
            for j, (r0, R) in enumerate(tl):
                P.dma("sync", lambda e, j=j, r0=r0, R=R: e.dma_start(out=xh[0:R, j, :], in_=x[r0:r0 + R, :]), writes=["xh"], chan="xh")
                P.dma("sync", lambda e, r0=r0, R=R: e.dma_start(out=ost[0:R, :], in_=o[r0:r0 + R, :]), writes=["ost"], chan="ost")
                P.emit("scalar", lambda e, j=j, R=R: e.copy(out=ob[0:R, j, :], in_=ost[0:R, :]), reads=["ost"], writes=["ob"])
                tb = j % 2
                for k in range(8):
                    P.emit("tensor", lambda e, k=k, R=R, j=j, tb=tb: e.transpose(out=tp[tb][:, k, 0:R], in_=ob[0:R, j, k * 128:(k + 1) * 128], identity=ident[0:R, 0:R]),
                           reads=["ob", "ident"], writes=[f"tp{tb}"])
                P.emit("scalar", lambda e, j=j, R=R, tb=tb: e.copy(out=aT[:, :, j * 128:j * 128 + R], in_=tp[tb][:, :, 0:R]), reads=[f"tp{tb}"], writes=["aT"])
            for j, (r0, R) in enumerate(tl):
                for hf in range(2):
                    mb = mi % 4
                    mi += 1
                    for k in range(8):
                        P.emit("tensor", lambda e, mb=mb, k=k, j=j, R=R, hf=hf: e.matmul(mp[mb][0:R, :], lhsT=aT[:, k, j * 128:j * 128 + R], rhs=wo[:, k, hf * 512:(hf + 1) * 512],
                                                                                     start=(k == 0), stop=(k == 7)), reads=["aT", "wo"], writes=[f"mp{mb}"])
                    P.emit("vector", lambda e, mb=mb, j=j, R=R, hf=hf: e.tensor_tensor(out=xh[0:R, j, hf * 512:(hf + 1) * 512], in0=mp[mb][0:R, :], in1=xh[0:R, j, hf * 512:(hf + 1) * 512], op=ALU.add),
                           reads=[f"mp{mb}", "xh"], writes=["xh"])
            for j, (r0, R) in enumerate(tl):
                rows = slice(0, R)
                P.emit("vector", lambda e, rows=rows: e.memset(ss[rows, :], 0.0), writes=["ss"])
                P.emit("scalar", lambda e, rows=rows, j=j: e.activation(out=ost[rows, :], in_=xh[rows, j, :], func=AF.Square, accum_out=ss[rows, 0:1]),
                       reads=["xh", "ss"], writes=["ost", "ss"])
                P.emit("vector", lambda e, rows=rows: e.tensor_scalar(out=rstd[rows, :], in0=ss[rows, :], scalar1=1.0 / D, scalar2=EPS, op0=ALU.mult, op1=ALU.add),
                       reads=["ss"], writes=["rstd"])
                P.emit("scalar", lambda e, rows=rows: e.activation(out=rstd[rows, :], in_=rstd[rows, :], func=AF.Sqrt), reads=["rstd"], writes=["rstd"])
                P.emit("vector", lambda e, rows=rows: e.reciprocal(out=rstd[rows, :], in_=rstd[rows, :]), reads=["rstd"], writes=["rstd"])
                P.emit("vector", lambda e, rows=rows, j=j: e.scalar_tensor_tensor(out=ob[rows, j, :], in0=xh[rows, j, :], scalar=rstd[rows, 0:1], in1=gt[rows, :], op0=ALU.mult, op1=ALU.mult),
                       reads=["xh", "rstd", "gt"], writes=["ob"])
                tb = j % 2
                for k in range(8):
                    P.emit("tensor", lambda e, k=k, R=R, j=j, tb=tb: e.transpose(out=tp[tb][:, k, 0:R], in_=ob[0:R, j, k * 128:(k + 1) * 128], identity=ident[0:R, 0:R]),
                           reads=["ob", "ident"], writes=[f"tp{tb}"])
                P.emit("scalar", lambda e, j=j, R=R, tb=tb: e.copy(out=aT[:, :, j * 128:j * 128 + R], in_=tp[tb][:, :, 0:R]), reads=[f"tp{tb}"], writes=["aT"])
            for fc in range(32):
                mb = mi % 4
                mi += 1
                for k in range(8):
                    P.emit("tensor", lambda e, mb=mb, k=k, fc=fc, NT=NT: e.matmul(mp[mb][:, 0:NT], lhsT=wu[:, k, fc * 128:(fc + 1) * 128], rhs=aT[:, k, 0:NT], start=(k == 0), stop=(k == 7)),
                           reads=["aT", "wu"], writes=[f"mp{mb}"])
                rb = fc % 2
                P.emit("scalar", lambda e, mb=mb, rb=rb, NT=NT: e.activation(out=rt[rb][:, 0:NT], in_=mp[mb][:, 0:NT], func=AF.Relu), reads=[f"mp{mb}"], writes=[f"rt{rb}"])
                P.emit("vector", lambda e, rb=rb, fc=fc, NT=NT: e.tensor_tensor(out=uT[:, fc, 0:NT], in0=rt[rb][:, 0:NT], in1=rt[rb][:, 0:NT], op=ALU.mult), reads=[f"rt{rb}"], writes=["uT"])
            for j, (r0, R) in enumerate(tl):
                for hf in range(2):
                    mb = mi % 4
                    mi += 1
                    for fc in range(32):
                        P.emit("tensor", lambda e, mb=mb, fc=fc, j=j, R=R, hf=hf: e.matmul(mp[mb][0:R, :], lhsT=uT[:, fc, j * 128:j * 128 + R], rhs=wd_[:, fc, hf * 512:(hf + 1) * 512],
                                                                                       start=(fc == 0), stop=(fc == 31)), reads=["uT", "wd"], writes=[f"mp{mb}"])
                    P.emit("vector", lambda e, mb=mb, j=j, R=R, hf=hf: e.tensor_tensor(out=xh[0:R, j, hf * 512:(hf + 1) * 512], in0=mp[mb][0:R, :], in1=xh[0:R, j, hf * 512:(hf + 1) * 512], op=ALU.add),
                           reads=[f"mp{mb}", "xh"], writes=["xh"])
                if final:
                    rows = slice(0, R)
                    P.emit("vector", lambda e, rows=rows: e.memset(ss[rows, :], 0.0), writes=["ss"])
                    P.emit("scalar", lambda e, rows=rows, j=j: e.activation(out=ost[rows, :], in_=xh[rows, j, :], func=AF.Square, accum_out=ss[rows, 0:1]),
                           reads=["xh", "ss"], writes=["ost", "ss"])
                    P.emit("vector", lambda e, rows=rows: e.tensor_scalar(out=rstd[rows, :], in0=ss[rows, :], scalar1=1.0 / D, scalar2=EPS, op0=ALU.mult, op1=ALU.add),
                           reads=["ss"], writes=["rstd"])
                    P.emit("scalar", lambda e, rows=rows: e.activation(out=rstd[rows, :], in_=rstd[rows, :], func=AF.Sqrt), reads=["rstd"], writes=["rstd"])
                    P.emit("vector", lambda e, rows=rows: e.reciprocal(out=rstd[rows, :], in_=rstd[rows, :]), reads=["rstd"], writes=["rstd"])
                    P.emit("vector", lambda e, rows=rows, j=j: e.scalar_tensor_tensor(out=xh[rows, j, :], in0=xh[rows, j, :], scalar=rstd[rows, 0:1], in1=gf[rows, :], op0=ALU.mult, op1=ALU.mult),
                           reads=["xh", "rstd", "gf"], writes=["xh"])
                P.dma("sync", lambda e, j=j, r0=r0, R=R: e.dma_start(out=h_out[r0:r0 + R, :], in_=xh[0:R, j, :]), reads=["xh"], chan="xh")
        P.replay(nc, es)
    return nc


def fview(t, rows, start, step, count):
    a = t[rows, start:start + 1]
    return bass.AP(a.tensor, a.offset, [list(a.ap[0]), [step, count]])


def slot_range(slot):
    k, s = (slot % 8) // 2, slot % 2
    return slot // 8, 16 * k + s, 16 * k + 14 + s


def build_nsa():
    nc = bass.Bass("TRN2", target_bir_lowering=False)
    dt_in = lambda n, s: nc.dram_tensor(n, s, F32, kind="ExternalInput").ap()
    qT = dt_in("qT", [NPT, 2, 64, 1024])
    gates = dt_in("gates", [ROWS, 48])
    selKT = dt_in("selKT", [2, 2, 64, SEQ])
    selV = dt_in("selV", [2, 2, SEQ, 64])
    winKT = dt_in("winKT", [NPT, 2, 64, 640])
    winV = dt_in("winV", [NPT, 2, 640, 64])
    kbias = dt_in("kbias", [NPT, 128, 15, 128])
    wbias = dt_in("wbias", [NPT, 128, 5, 128])
    cmpA = dt_in("cmpA", [2, 2, 2, 128, SEQ])
    EK = dt_in("EK", [64, SEQ])
    topA = dt_in("topA", [NPT, 128, 128])
    topB = dt_in("topB", [NPT, 128, 128])
    cmpbias = dt_in("cmpbias", [NPT, 4, 128, 128])
    ovl = dt_in("ovl", [4, 128, 128])
    ropeC = dt_in("ropeC", [64, 512])
    ropeS = dt_in("ropeS", [64, 512])
    w1 = dt_in("w1", [2, 2048, 256])
    w2 = dt_in("w2", [2, 256, 64])
    peT = dt_in("peT", [2, 128, 16])
    o_out = nc.dram_tensor("o_out", [NPT * 128, D], F32, kind="ExternalOutput").ap()
    P = Prog()
    with ExitStack() as es:
        sb = lambda n, s, d: es.enter_context(nc.sbuf_tensor(n, s, d))
        ps = lambda n, s, d: es.enter_context(nc.psum_tensor(n, s, d))
        ident = make_ident(P, nc, es)
        w1b = sb("w1b", [128, 2, 16, 256], BF16)
        w2b = sb("w2b", [128, 2, 2, 64], BF16)
        w2R = sb("w2R", [128, 2, 64], BF16)
        peb = sb("peb", [128, 2, 16], BF16)
        rC = sb("rC", [64, 512], F32)
        rS = sb("rS", [64, 512], F32)
        kb1 = sb("kb1", [128, 15, 128], BF16)
        wb1 = sb("wb1", [128, 5, 128], BF16)
        kbr = [sb(f"kbr{i}", [128, 15, 4, 128], BF16) for i in range(2)]
        wbr = [sb(f"wbr{i}", [128, 5, 4, 128], BF16) for i in range(2)]
        b1 = sb("b1", [128, 2, 2], F32)
        kcT = sb("kcT", [64, 4, 512], BF16)
        rhsC = sb("rhsC", [128, 4, 4, 193], BF16)
        Acmp = sb("Acmp", [128, SEQ], BF16)
        hid = sb("hid", [128, 2, 512], BF16)
        t64 = [sb(f"t64_{i}", [64, 512], F32) for i in range(2)]
        KTa = [sb(f"KTa{i}", [128, SEQ], BF16) for i in range(2)]
        KwT = [sb(f"KwT{i}", [64, 640], BF16) for i in range(2)]
        Vs = [sb(f"Vs{i}", [128, 64, 65], BF16) for i in range(2)]
        Vw = [sb(f"Vw{i}", [128, 5, 65], BF16) for i in range(2)]
        QTa = [sb(f"QTa{i}", [128, 2, 1024], BF16) for i in range(2)]
        gts = [sb(f"gts{i}", [128, 48], F32) for i in range(2)]
        tA = [sb(f"tA{i}", [128, 128], F32) for i in range(2)]
        tB = [sb(f"tB{i}", [128, 128], F32) for i in range(2)]
        cb1 = [sb(f"cb1_{i}", [128, 128], BF16) for i in range(2)]
        cbr = [sb(f"cbr{i}", [128, 8, 128], BF16) for i in range(2)]
        pT = [sb(f"pT{i}", [128, 512], BF16) for i in range(3)]
        den = sb("den", [128, 8], F32)
        rden = sb("rden", [128, 8], F32)
        fsc = sb("fsc", [128, 8], F32)
        imp = sb("imp", [128, 128], F32)
        wk = sb("wk", [128, 128], F32)
        mx1 = sb("mx1", [128, 8], F32)
        mx2 = sb("mx2", [128, 8], F32)
        selb2 = sb("selb2", [128, 2, 128], BF16)
        otmp = sb("otmp", [128, 8, 64], F32)
        oacc = [sb(f"oacc{i}", [128, 8, 64], F32) for i in range(2)]
        sp = [ps(f"sp{i}", [128, 512], F32) for i in range(3)]
        acc = [ps(f"acc{i}", [128, 512], F32) for i in range(4)]
        tps = ps("tps", [128, 128], BF16)
        cp = ps

        for kv in range(2):
            for j in range(16):
                P.dma("gpsimd", lambda e, kv=kv, j=j: e.dma_start(out=w1b[:, kv, j, :], in_=w1[kv, j * 128:(j + 1) * 128, :]), writes=["w1b"], chan="w1b")
            for hc in range(2):
                P.dma("gpsimd", lambda e, kv=kv, hc=hc: e.dma_start(out=w2b[:, kv, hc, :], in_=w2[kv, hc * 128:(hc + 1) * 128, :]), writes=["w2b"], chan="w2b")
            P.dma("gpsimd", lambda e, kv=kv: e.dma_start(out=peb[:, kv, :], in_=peT[kv, :, :]), writes=["peb"], chan="peb")
        P.dma("sync", lambda e: e.dma_start(out=rC[:], in_=ropeC[:, :]), writes=["rC"], chan="rC")
        P.dma("sync", lambda e: e.dma_start(out=rS[:], in_=ropeS[:, :]), writes=["rS"], chan="rS")
        P.emit("vector", lambda e: e.tensor_scalar(out=w2R[:, :, 0:32], in0=w2b[:, 0, :, 32:64], scalar1=-1.0, scalar2=None, op0=ALU.mult), reads=["w2b"], writes=["w2R"])
        P.emit("vector", lambda e: e.tensor_copy(out=w2R[:, :, 32:64], in_=w2b[:, 0, :, 0:32]), reads=["w2b"], writes=["w2R"])
        P.emit("vector", lambda e: e.memset(kcT[:], 0.0), writes=["kcT"])
        P.emit("gpsimd", lambda e: e.memset(rhsC[:], 0.0), writes=["rhsC"])
        P.emit("vector", lambda e: e.memset(selb2[:], 0.0), writes=["selb2"])
        for i in range(2):
            P.emit("gpsimd", lambda e, i=i: e.memset(Vs[i][:, :, 64:65], 1.0), writes=[f"Vs{i}"])
            P.emit("gpsimd", lambda e, i=i: e.memset(Vw[i][:, :, 64:65], 1.0), writes=[f"Vw{i}"])
            for q4 in range(4):
                P.dma("gpsimd", lambda e, i=i, q4=q4: e.dma_start(out=KTa[i][64:128, q4 * 2048:(q4 + 1) * 2048], in_=EK[:, q4 * 2048:(q4 + 1) * 2048]), writes=[f"KTa{i}"], chan=f"KTa{i}")

        NCMP = 511
        spi = [0]

        def nsp():
            spi[0] += 1
            return spi[0] % 3

        for b in range(2):
            for kvh in range(2):
                combo = b * 2 + kvh
                for kv in range(2):
                    for q4 in range(4):
                        P.dma("gpsimd", lambda e, b=b, kvh=kvh, kv=kv, q4=q4: e.dma_start(out=Acmp[:, q4 * 2048:(q4 + 1) * 2048], in_=cmpA[b, kvh, kv, :, q4 * 2048:(q4 + 1) * 2048]),
                              writes=["Acmp"], chan="Acmp")
                    for hc in range(2):
                        s = nsp()
                        for j in range(16):
                            P.emit("tensor", lambda e, s=s, kv=kv, j=j, hc=hc: e.matmul(sp[s][:, 0:1], lhsT=w1b[:, kv, j, hc * 128:(hc + 1) * 128], rhs=peb[:, kv, j:j + 1], start=(j == 0), stop=(j == 15)),
                                   reads=["w1b", "peb"], writes=[f"sp{s}"])
                        P.emit("vector", lambda e, s=s, kv=kv, hc=hc: e.tensor_copy(out=b1[:, kv, hc:hc + 1], in_=sp[s][:, 0:1]), reads=[f"sp{s}"], writes=["b1"])
                        s = nsp()
                        for j in range(16):
                            P.emit("tensor", lambda e, s=s, kv=kv, j=j, hc=hc: e.matmul(sp[s][:, 0:NCMP], lhsT=w1b[:, kv, j, hc * 128:(hc + 1) * 128], rhs=fview(Acmp, slice(0, 128), 2 * j, 16, NCMP),
                                                                                     start=(j == 0), stop=(j == 15)), reads=["w1b", "Acmp"], writes=[f"sp{s}"])
                        P.emit("scalar", lambda e, s=s, kv=kv, hc=hc: e.activation(out=hid[:, hc, 0:NCMP], in_=sp[s][:, 0:NCMP], func=AF.Silu, bias=b1[:, kv, hc:hc + 1]),
                               reads=[f"sp{s}", "b1"], writes=["hid"])
                    if kv == 0:
                        s0, s1 = nsp(), nsp()
                        for hc in range(2):
                            P.emit("tensor", lambda e, s0=s0, hc=hc: e.matmul(sp[s0][0:64, 0:NCMP], lhsT=w2b[:, 0, hc, :], rhs=hid[:, hc, 0:NCMP], start=(hc == 0), stop=(hc == 1)),
                                   reads=["w2b", "hid"], writes=[f"sp{s0}"])
                        for hc in range(2):
                            P.emit("tensor", lambda e, s1=s1, hc=hc: e.matmul(sp[s1][0:64, 0:NCMP], lhsT=w2R[:, hc, :], rhs=hid[:, hc, 0:NCMP], start=(hc == 0), stop=(hc == 1)),
                                   reads=["w2R", "hid"], writes=[f"sp{s1}"])
                        P.emit("vector", lambda e, s0=s0: e.tensor_tensor(out=t64[0][:, 0:NCMP], in0=sp[s0][0:64, 0:NCMP], in1=rC[:, 0:NCMP], op=ALU.mult), reads=[f"sp{s0}", "rC"], writes=["t64_0"])
                        P.emit("vector", lambda e, s1=s1: e.tensor_tensor(out=t64[1][:, 0:NCMP], in0=sp[s1][0:64, 0:NCMP], in1=rS[:, 0:NCMP], op=ALU.mult), reads=[f"sp{s1}", "rS"], writes=["t64_1"])
                        P.emit("vector", lambda e, combo=combo: e.tensor_tensor(out=kcT[:, combo, 0:NCMP], in0=t64[0][:, 0:NCMP], in1=t64[1][:, 0:NCMP], op=ALU.add), reads=["t64_0", "t64_1"], writes=["kcT"])
                    else:
                        for ct in range(4):
                            nb = min(128, NCMP - ct * 128)
                            s = nsp()
                            for hc in range(2):
                                P.emit("tensor", lambda e, s=s, hc=hc, ct=ct, nb=nb: e.matmul(sp[s][0:nb, 0:64], lhsT=hid[:, hc, ct * 128:ct * 128 + nb], rhs=w2b[:, 1, hc, :], start=(hc == 0), stop=(hc == 1)),
                                       reads=["w2b", "hid"], writes=[f"sp{s}"])
                            P.emit("vector", lambda e, s=s, ct=ct, nb=nb, combo=combo: e.tensor_copy(out=rhsC[0:nb, combo, ct, 0:64], in_=sp[s][0:nb, 0:64]), reads=[f"sp{s}"], writes=["rhsC"])
                            P.emit("gpsimd", lambda e, ct=ct, nb=nb, combo=combo: e.memset(rhsC[0:nb, combo, ct, 64:65], 1.0), reads=[], writes=["rhsC"])
                            P.dma("gpsimd", lambda e, ct=ct, combo=combo: e.dma_start(out=rhsC[:, combo, ct, 65:193], in_=ovl[ct, :, :]), writes=["rhsC"], chan="rhsC")

        pti = [0]

        def npt():
            pti[0] += 1
            return pti[0] % 3

        def score_unit(lhsT, rhs, kparts, bias, kbuf, qbuf, biasname):
            s = nsp()
            P.emit("tensor", lambda e: e.matmul(sp[s][:, :], lhsT=lhsT, rhs=rhs, start=True, stop=(bias is None)), reads=[kbuf, qbuf], writes=[f"sp{s}"])
            if bias is not None:
                P.emit("tensor", lambda e: e.matmul(sp[s][:, :], lhsT=ident[:], rhs=bias, start=False, stop=True), reads=["ident", biasname], writes=[f"sp{s}"])
            p = npt()
            P.emit("scalar", lambda e: e.activation(out=pT[p][:], in_=sp[s][:], func=AF.Exp, scale=0.125), reads=[f"sp{s}"], writes=[f"pT{p}"])
            return p

        it = 0
        for b in range(2):
            for kvh in range(2):
                combo = b * 2 + kvh
                cb = combo % 2
                for q4 in range(4):
                    cs_ = slice(q4 * 2048, (q4 + 1) * 2048)
                    P.dma("gpsimd", lambda e, cb=cb, b=b, kvh=kvh, cs_=cs_: e.dma_start(out=KTa[cb][0:64, cs_], in_=selKT[b, kvh, :, cs_]), writes=[f"KTa{cb}"], chan=f"KTa{cb}")
                    ks = slice(q4 * 16, (q4 + 1) * 16)
                    P.dma("gpsimd", lambda e, cb=cb, b=b, kvh=kvh, ks=ks, q4=q4: e.dma_start(out=Vs[cb][:, ks, 0:64], in_=selV[b, kvh, q4 * 2048:(q4 + 1) * 2048, :].rearrange("(k p) d -> p k d", p=128)),
                          writes=[f"Vs{cb}"], chan=f"Vs{cb}")
                for ti in range(NPT):
                    tb_, qmin, qmax = slot_range(ti)
                    if tb_ != b:
                        continue
                    ib = it % 2
                    it += 1
                    r0 = ti * 128
                    for h in range(2):
                        P.dma("gpsimd", lambda e, ib=ib, ti=ti, kvh=kvh, h=h: e.dma_start(out=QTa[ib][0:64, h, :], in_=qT[ti, kvh, :, :]), writes=[f"QTa{ib}"], chan=f"QTa{ib}")
                    P.dma("sync", lambda e, ib=ib, r0=r0: e.dma_start(out=gts[ib][:], in_=gates[r0:r0 + 128, :]), writes=[f"gts{ib}"], chan=f"gts{ib}")
                    P.dma("sync", lambda e, ib=ib, ti=ti: e.dma_start(out=tA[ib][:], in_=topA[ti, :, :]), writes=[f"tA{ib}"], chan=f"tA{ib}")
                    P.dma("sync", lambda e, ib=ib, ti=ti: e.dma_start(out=tB[ib][:], in_=topB[ti, :, :]), writes=[f"tB{ib}"], chan=f"tB{ib}")
                    P.dma("gpsimd", lambda e, ti=ti: e.dma_start(out=kb1[:], in_=kbias[ti, :, :, :]), writes=["kb1"], chan="kb1")
                    P.emit("gpsimd", lambda e, ib=ib: e.tensor_copy(out=kbr[ib][:], in_=bc(kb1[:], [128, 15, 4, 128], 2)), reads=["kb1"], writes=[f"kbr{ib}"])
                    P.dma("gpsimd", lambda e, ti=ti: e.dma_start(out=wb1[:], in_=wbias[ti, :, :, :]), writes=["wb1"], chan="wb1")
                    P.emit("gpsimd", lambda e, ib=ib: e.tensor_copy(out=wbr[ib][:], in_=bc(wb1[:], [128, 5, 4, 128], 2)), reads=["wb1"], writes=[f"wbr{ib}"])
                    P.dma("gpsimd", lambda e, ib=ib, ti=ti, kvh=kvh: e.dma_start(out=KwT[ib][:, :], in_=winKT[ti, kvh, :, :]), writes=[f"KwT{ib}"], chan=f"KwT{ib}")
                    P.dma("gpsimd", lambda e, ib=ib, ti=ti, kvh=kvh: e.dma_start(out=Vw[ib][:, :, 0:64], in_=winV[ti, kvh, :, :].rearrange("(k p) d -> p k d", p=128)),
                          writes=[f"Vw{ib}"], chan=f"Vw{ib}")
                    gv = gts[ib][:, kvh * 24:(kvh + 1) * 24].rearrange("p (g c) -> p g c", c=3)
                    nct = 4
                    pend = None

                    def pv_cmp(p, ct, hf, combo=combo, nct=nct):
                        for g4 in range(4):
                            g = hf * 4 + g4
                            P.emit("tensor", lambda e, p=p, g=g, g4=g4, ct=ct: e.matmul(
                                acc[g // 2][:, (g % 2) * 193:(g % 2) * 193 + 193], lhsT=pT[p][:, g4 * 128:(g4 + 1) * 128], rhs=rhsC[:, combo, ct, :],
                                start=(ct == 0 and g % 2 == 0), stop=(ct == nct - 1)), reads=[f"pT{p}", "rhsC"], writes=[f"acc{g // 2}"])

                    for ct in range(nct):
                        cbi = ct % 2
                        P.dma("gpsimd", lambda e, cbi=cbi, ti=ti, ct=ct: e.dma_start(out=cb1[cbi][:], in_=cmpbias[ti, ct, :, :]), writes=[f"cb1_{cbi}"], chan=f"cb1_{cbi}")
                        P.emit("gpsimd", lambda e, cbi=cbi: e.tensor_copy(out=cbr[cbi][:], in_=bc(cb1[cbi][:], [128, 8, 128], 1)), reads=[f"cb1_{cbi}"], writes=[f"cbr{cbi}"])
                        for hf in range(2):
                            p = score_unit(kcT[:, combo, ct * 128:(ct + 1) * 128], QTa[ib][0:64, 0, hf * 512:(hf + 1) * 512], 64,
                                           cbr[cbi][:, hf * 4:(hf + 1) * 4, :].rearrange("p g t -> p (g t)"), "kcT", f"QTa{ib}", f"cbr{cbi}")
                            if pend is not None:
                                pv_cmp(*pend)
                            pend = (p, ct, hf)
                    pv_cmp(*pend)
                    for a in range(4):
                        P.emit("vector", lambda e, a=a: e.tensor_copy(out=den[:, 2 * a:2 * a + 2], in_=fview(acc[a], slice(0, 128), 64, 193, 2)), reads=[f"acc{a}"], writes=["den"])
                    P.emit("vector", lambda e: e.tensor_scalar(out=rden[:], in0=den[:], scalar1=1e-30, scalar2=None, op0=ALU.max), reads=["den"], writes=["rden"])
                    P.emit("vector", lambda e: e.reciprocal(out=rden[:], in_=rden[:]), reads=["rden"], writes=["rden"])
                    for g in range(8):
                        U = acc[g // 2][:, (g % 2) * 193 + 65:(g % 2) * 193 + 193]
                        if g == 0:
                            P.emit("vector", lambda e, U=U: e.tensor_scalar(out=imp[:], in0=U, scalar1=rden[:, 0:1], scalar2=None, op0=ALU.mult), reads=["acc0", "rden"], writes=["imp"])
                        else:
                            P.emit("vector", lambda e, U=U, g=g: e.scalar_tensor_tensor(out=imp[:], in0=U, scalar=rden[:, g:g + 1], in1=imp[:], op0=ALU.mult, op1=ALU.add),
                                   reads=[f"acc{g // 2}", "rden", "imp"], writes=["imp"])
                    ob_ = it % 2
                    P.emit("vector", lambda e, gv=gv: e.tensor_tensor(out=fsc[:], in0=rden[:], in1=gv[:, :, 0], op=ALU.mult), reads=["rden", f"gts{ib}"], writes=["fsc"])
                    for a in range(4):
                        ov = acc[a][:, 0:386].rearrange("p (g c) -> p g c", c=193)[:, :, 0:64]
                        P.emit("vector", lambda e, a=a, ov=ov, ob_=ob_: e.tensor_tensor(out=oacc[ob_][:, 2 * a:2 * a + 2, :], in0=ov, in1=bc(fsc[:, 2 * a:2 * a + 2], [128, 2, 64], 2), op=ALU.mult),
                               reads=[f"acc{a}", "fsc"], writes=[f"oacc{ob_}"])
                    pend = None

                    def pv_win(p, kt, hf, ib=ib):
                        for g4 in range(4):
                            P.emit("tensor", lambda e, p=p, g4=g4, hf=hf, kt=kt: e.matmul(acc[2 + hf][:, g4 * 65:(g4 + 1) * 65], lhsT=pT[p][:, g4 * 128:(g4 + 1) * 128], rhs=Vw[ib][:, kt, :],
                                                                                    start=(kt == 0 and g4 == 0), stop=(kt == 4)), reads=[f"pT{p}", f"Vw{ib}"], writes=[f"acc{2 + hf}"])

                    for kt in range(5):
                        for hf in range(2):
                            bias = wbr[ib][:, kt, :, :].rearrange("p g t -> p (g t)")
                            p = score_unit(KwT[ib][:, kt * 128:(kt + 1) * 128], QTa[ib][0:64, 0, hf * 512:(hf + 1) * 512], 64, bias, f"KwT{ib}", f"QTa{ib}", f"wbr{ib}")
                            if pend is not None:
                                pv_win(*pend)
                            pend = (p, kt, hf)
                    pv_win(*pend)
                    P.emit("vector", lambda e, ib=ib: e.tensor_tensor(out=imp[:], in0=imp[:], in1=tA[ib][:], op=ALU.mult), reads=["imp", f"tA{ib}"], writes=["imp"])
                    P.emit("vector", lambda e, ib=ib: e.tensor_tensor(out=imp[:], in0=imp[:], in1=tB[ib][:], op=ALU.add), reads=["imp", f"tB{ib}"], writes=["imp"])
                    P.emit("vector", lambda e: e.max(out=mx1[:], in_=imp[:]), reads=["imp"], writes=["mx1"])
                    P.emit("vector", lambda e: e.match_replace(out=wk[:], in_to_replace=mx1[:], in_values=imp[:], imm_value=-1e30), reads=["imp", "mx1"], writes=["wk"])
                    P.emit("vector", lambda e: e.max(out=mx2[:], in_=wk[:]), reads=["wk"], writes=["mx2"])
                    P.emit("vector", lambda e: e.tensor_scalar(out=selb2[:, :, 64:128], in0=imp[:].rearrange("p (h j) -> p h j", h=2), scalar1=mx2[:, 7:8], scalar2=NEGB,
                                                               op0=ALU.is_lt, op1=ALU.mult), reads=["imp", "mx2"], writes=["selb2"])
                    for h in range(2):
                        P.emit("tensor", lambda e, h=h: e.transpose(out=tps[:], in_=selb2[:, h, :], identity=ident[:]), reads=["selb2", "ident"], writes=["tps"])
                        P.emit("scalar", lambda e, h=h, ib=ib: e.copy(out=QTa[ib][64:128, h, :].rearrange("p (g t) -> p g t", g=8), in_=bc(tps[64:128, :], [64, 8, 128], 1)),
                               reads=["tps"], writes=[f"QTa{ib}"])
                    pend = None

                    def pv_sel(p, kt, hf, qmax=qmax, cb=cb):
                        for g4 in range(4):
                            P.emit("tensor", lambda e, p=p, g4=g4, hf=hf, kt=kt: e.matmul(acc[hf][:, g4 * 65:(g4 + 1) * 65], lhsT=pT[p][:, g4 * 128:(g4 + 1) * 128], rhs=Vs[cb][:, kt, :],
                                                                                    start=(kt == 0 and g4 == 0), stop=(kt == qmax)), reads=[f"pT{p}", f"Vs{cb}"], writes=[f"acc{hf}"])

                    for kt in range(qmax + 1):
                        half = kt // 32
                        for hf in range(2):
                            bias = kbr[ib][:, kt - qmin, :, :].rearrange("p g t -> p (g t)") if kt >= qmin else None
                            p = score_unit(KTa[cb][:, kt * 128:(kt + 1) * 128], QTa[ib][:, half, hf * 512:(hf + 1) * 512], 128, bias, f"KTa{cb}", f"QTa{ib}", f"kbr{ib}")
                            if pend is not None:
                                pv_sel(*pend)
                            pend = (p, kt, hf)
                    pv_sel(*pend)
                    for br, a0 in ((1, 0), (2, 2)):
                        for hf in range(2):
                            a = a0 + hf
                            P.emit("vector", lambda e, a=a, hf=hf: e.tensor_copy(out=den[:, hf * 4:(hf + 1) * 4], in_=fview(acc[a], slice(0, 128), 64, 65, 4)), reads=[f"acc{a}"], writes=["den"])
                        P.emit("vector", lambda e: e.tensor_scalar(out=rden[:], in0=den[:], scalar1=1e-30, scalar2=None, op0=ALU.max), reads=["den"], writes=["rden"])
                        P.emit("vector", lambda e: e.reciprocal(out=rden[:], in_=rden[:]), reads=["rden"], writes=["rden"])
                        P.emit("vector", lambda e, gv=gv, br=br: e.tensor_tensor(out=fsc[:], in0=rden[:], in1=gv[:, :, br], op=ALU.mult), reads=["rden", f"gts{ib}"], writes=["fsc"])
                        for hf in range(2):
                            a = a0 + hf
                            ov = acc[a][:, 0:260].rearrange("p (g c) -> p g c", c=65)[:, :, 0:64]
                            P.emit("vector", lambda e, a=a, hf=hf, ov=ov: e.tensor_tensor(out=otmp[:, hf * 4:(hf + 1) * 4, :], in0=ov, in1=bc(fsc[:, hf * 4:(hf + 1) * 4], [128, 4, 64], 2), op=ALU.mult),
                                   reads=[f"acc{a}", "fsc"], writes=["otmp"])
                        P.emit("gpsimd", lambda e, ob_=ob_: e.tensor_tensor(out=oacc[ob_][:], in0=oacc[ob_][:], in1=otmp[:], op=ALU.add), reads=[f"oacc{ob_}", "otmp"], writes=[f"oacc{ob_}"])
                    P.dma("sync", lambda e, ob_=ob_, r0=r0, kvh=kvh: e.dma_start(out=o_out[r0:r0 + 128, kvh * 512:(kvh + 1) * 512], in_=oacc[ob_][:].rearrange("p g d -> p (g d)")),
                          reads=[f"oacc{ob_}"], chan=f"oacc{ob_}")
        P.replay(nc, es)
    return nc


def nsa_consts(core):
    tiles = core_tiles(core)
    tok = np.arange(128)
    j = np.arange(128)
    tA = np.zeros((NPT, 128, 128), np.float32)
    tB = np.zeros((NPT, 128, 128), np.float32)
    cbias = np.full((NPT, 4, 128, 128), NEGB, np.float32)
    for ti, (_, p0) in enumerate(tiles):
        qp = (p0 + tok)[:, None]
        cur = qp // 64
        forced = (j[None, :] == 0) | (j[None, :] == cur) | (j[None, :] == cur - 1)
        causal = j[None, :] * 64 <= qp
        tA[ti] = (causal & ~forced).astype(np.float32)
        tB[ti] = np.where(forced, 1e6, np.where(causal, 0.0, -1.0)).astype(np.float32)
        for ct in range(4):
            i = ct * 128 + np.arange(128)
            ok = (i[:, None] * 16 + 31 <= (p0 + tok)[None, :]) & (i[:, None] < 511)
            cbias[ti, ct] = np.where(ok, 0.0, NEGB)
    i = np.arange(512)
    ov = ((i[:, None] * 16 < (j[None, :] + 1) * 64) & (i[:, None] * 16 + 32 > j[None, :] * 64) & (i[:, None] < 511)).astype(np.float32)
    key = np.arange(SEQ)
    EK = ((key[None, :] // 64) % 64 == np.arange(64)[:, None]).astype(np.float32)
    inv = (THETA ** (-np.arange(32, dtype=np.float32) / 32)).astype(np.float32)
    ang = (np.arange(512) * 16).astype(np.float32)[None, :] * np.tile(inv, 2)[:, None]
    kk = np.arange(128)
    kb = np.zeros((NPT, 128, 15, 128), np.float32)
    wb = np.zeros((NPT, 128, 5, 128), np.float32)
    for ti, (_, p0) in enumerate(tiles):
        _, qmin, qmax = slot_range(ti)
        qp = (p0 + tok)[None, :]
        for jj in range(15):
            kp = ((qmin + jj) * 128 + kk)[:, None]
            kb[ti, :, jj, :] = np.where(kp <= qp, 0.0, NEGB)
        for r in range(5):
            kp = (p0 - 512 + r * 128 + kk)[:, None]
            wb[ti, :, r, :] = np.where((kp <= qp) & (kp > qp - WIN) & (kp >= 0), 0.0, NEGB)
    return {"topA": tA, "topB": tB, "cmpbias": cbias, "ovl": np.ascontiguousarray(ov.reshape(4, 128, 128)), "EK": EK,
            "ropeC": np.cos(ang).astype(np.float32), "ropeS": np.sin(ang).astype(np.float32), "kbias": kb, "wbias": wb}


def run_nsa_prompt(zp, w1, w2, pe):
    T = lambda a: np.ascontiguousarray(a)
    kvp = zp[:, :, 1024:1792].reshape(2, SEQ, 3, 2, 2, 64)
    selKT = T(kvp[:, :, 1, 0].transpose(0, 2, 3, 1))
    selV = T(kvp[:, :, 1, 1].transpose(0, 2, 1, 3))
    wpad = np.concatenate([np.zeros((2, WIN, 2, 2, 64), np.float32), kvp[:, :, 2]], axis=1)
    ck = kvp[:, :, 0].transpose(0, 3, 2, 4, 1)
    cmpA = np.zeros((2, 2, 2, 128, SEQ), np.float32)
    cmpA[:, :, :, 0:64, :] = ck
    cmpA[:, :, :, 64:128, :SEQ - 1] = ck[..., 1:]
    peT = T(pe.reshape(2, 16, 2, 64).transpose(0, 2, 3, 1).reshape(2, 128, 16))
    in_maps, ncs = [], []
    for c in range(NCORES):
        tiles = core_tiles(c)
        qT = np.stack([zp[b, p0:p0 + 128, 0:1024].reshape(128, 2, 8, 64).transpose(1, 3, 2, 0).reshape(2, 64, 1024) for b, p0 in tiles])
        gates = np.zeros((ROWS, 48), np.float32)
        for ti, (b, p0) in enumerate(tiles):
            gates[ti * 128:(ti + 1) * 128] = zp[b, p0:p0 + 128, 1792:1840]
        winKT = T(np.stack([wpad[b, p0:p0 + 640, 0].transpose(1, 2, 0) for b, p0 in tiles]))
        winV = T(np.stack([wpad[b, p0:p0 + 640, 1].transpose(1, 0, 2) for b, p0 in tiles]))
        m = {"qT": T(qT), "gates": gates, "selKT": selKT, "selV": selV, "winKT": winKT, "winV": winV, "cmpA": cmpA,
             "w1": T(w1), "w2": T(w2), "peT": peT}
        m.update(nsa_consts(c))
        in_maps.append(m)
    if "nsa" not in _NC_CACHE:
        _NC_CACHE["nsa"] = build_nsa()
    res = run_bass_kernel_spmd(_NC_CACHE["nsa"], in_maps, core_ids=list(range(NCORES))).results
    op = np.zeros((2, SEQ, D), np.float32)
    for c in range(NCORES):
        for ti, (b_, p0) in enumerate(core_tiles(c)):
            op[b_, p0:p0 + 128] = res[c]["o_out"][ti * 128:(ti + 1) * 128]
    return op


NPOOL = 2560
NPG = PAST // 128


def build_nsa_s():
    nc = bass.Bass("TRN2", target_bir_lowering=False)
    dt_in = lambda n, s, d=F32: nc.dram_tensor(n, s, d, kind="ExternalInput").ap()
    pool_cmp = dt_in("pool_cmp", [NPOOL, 128, 128])
    pool_sel = dt_in("pool_sel", [NPOOL, 128, 128])
    ptl = dt_in("ptl", [128, 8 * 8], I32)
    pmod = dt_in("pmod", [128, 1], I32)
    qT = dt_in("qT", [8, 64, 64])
    gates = dt_in("gates", [8, 8, 24])
    knewT = dt_in("knewT", [8, 64, 8])
    vnew = dt_in("vnew", [8, 8, 64])
    wKT = dt_in("wKT", [8, 64, 520])
    wV = dt_in("wV", [8, 520, 64])
    wrows = dt_in("wrows", [8, 520, 128])
    EK = dt_in("EK", [64, SEQ])
    topA = dt_in("topA", [8, 128])
    topB = dt_in("topB", [8, 128])
    ovl = dt_in("ovl", [4, 128, 128])
    ropeC = dt_in("ropeC", [64, 512])
    ropeS = dt_in("ropeS", [64, 512])
    causb = dt_in("causb", [8, 64])
    wbias = dt_in("wbias", [128, 5, 64])
    w1s = dt_in("w1s", [128, 32, 256])
    w2 = dt_in("w2", [2, 256, 64])
    pes = dt_in("pes", [128, 32])
    o_out = nc.dram_tensor("o_out", [64, 512], F32, kind="ExternalOutput").ap()
    wins_out = nc.dram_tensor("wins_out", [8, WIN, 128], F32, kind="ExternalOutput").ap()
    P = Prog()
    with ExitStack() as es:
        sb = lambda n, s, d: es.enter_context(nc.sbuf_tensor(n, s, d))
        ps = lambda n, s, d: es.enter_context(nc.psum_tensor(n, s, d))
        ident = make_ident(P, nc, es)
        ptb = sb("ptb", [128, 8 * 8], I32)
        pidx = sb("pidx", [128, 8 * 8], I32)
        piota = sb("piota", [128, 1], I32)
        w1b = sb("w1b", [128, 32, 256], BF16)
        w2b = sb("w2b", [128, 2, 2, 64], BF16)
        w2R = sb("w2R", [128, 2, 64], BF16)
        peb = sb("peb", [128, 32], BF16)
        rC = sb("rC", [64, 512], F32)
        rS = sb("rS", [64, 512], F32)
        tA = sb("tA", [8, 128], F32)
        tB = sb("tB", [8, 128], F32)
        cbs = sb("cbs", [8, 64], BF16)
        wbs = sb("wbs", [128, 5, 64], BF16)
        b1 = sb("b1", [128, 2, 2], F32)
        kcT = sb("kcT", [64, 512], BF16)
        rhsC = sb("rhsC", [128, 4, 193], BF16)
        cTok = [sb(f"cTok{i}", [128, NPG, 128], BF16) for i in range(2)]
        sTok = [sb(f"sTok{i}", [128, NPG, 128], BF16) for i in range(2)]
        AT = sb("AT", [128, SEQ], BF16)
        KTa = sb("KTa", [128, SEQ], BF16)
        Vs = sb("Vs", [128, NPG, 65], BF16)
        hid = sb("hid", [128, 2, 512], BF16)
        t64 = [sb(f"t64_{i}", [64, 512], F32) for i in range(2)]
        QTa = sb("QTa", [128, 2, 64], BF16)
        gts = sb("gts", [8, 24], F32)
        knT = sb("knT", [64, 8], BF16)
        vnA = sb("vnA", [8, 65], BF16)
        KwT = sb("KwT", [64, 640], BF16)
        Vw = sb("Vw", [128, 5, 65], BF16)
        pT = [sb(f"pT{i}", [128, 256], BF16) for i in range(3)]
        den = sb("den", [8, 8], F32)
        rden = sb("rden", [8, 8], F32)
        fsc = sb("fsc", [8, 8], F32)
        imp = sb("imp", [8, 128], F32)
        wk = sb("wk", [8, 128], F32)
        mx1 = sb("mx1", [8, 8], F32)
        mx2 = sb("mx2", [8, 8], F32)
        selb2 = sb("selb2", [8, 2, 128], BF16)
        otmp = sb("otmp", [8, 8, 64], F32)
        oacc = [sb(f"oacc{i}", [8, 8, 64], F32) for i in range(2)]
        tpp = [ps(f"tpp{i}", [128, 4, 128], BF16) for i in range(2)]
        sp = [ps(f"sp{i}", [128, 512], F32) for i in range(2)]
        acc = [ps(f"acc{i}", [128, 512], F32) for i in range(4)]

        P.dma("gpsimd", lambda e: e.dma_start(out=ptb[:], in_=ptl[:, :]), writes=["ptb"], chan="ptb")
        P.dma("gpsimd", lambda e: e.dma_start(out=piota[:], in_=pmod[:, :]), writes=["piota"], chan="piota")
        P.emit("gpsimd", lambda e: e.tensor_scalar(out=pidx[:], in0=ptb[:], scalar1=16, scalar2=None, op0=ALU.mult), reads=["ptb"], writes=["pt"])
        P.emit("gpsimd", lambda e: e.tensor_tensor(out=pidx[:], in0=pidx[:], in1=piota[:].to_broadcast([128, 8 * 8]), op=ALU.add), reads=["pt", "piota"], writes=["pt"])
        rows_cmp = pool_cmp.rearrange("n (g r) c -> (n g) (r c)", r=8)
        rows_sel = pool_sel.rearrange("n (g r) c -> (n g) (r c)", r=8)
        for r4 in range(4):
            P.dma("gpsimd", lambda e, r4=r4: e.dma_start(out=w1b[:, r4 * 8:(r4 + 1) * 8, :], in_=w1s[:, r4 * 8:(r4 + 1) * 8, :]), writes=["w1b"], chan="w1b")
        for kv in range(2):
            for hc in range(2):
                P.dma("gpsimd", lambda e, kv=kv, hc=hc: e.dma_start(out=w2b[:, kv, hc, :], in_=w2[kv, hc * 128:(hc + 1) * 128, :]), writes=["w2b"], chan="w2b")
        P.dma("gpsimd", lambda e: e.dma_start(out=peb[:], in_=pes[:, :]), writes=["peb"], chan="peb")
        P.dma("gpsimd", lambda e: e.dma_start(out=cbs[:], in_=causb[:, :]), writes=["cbs"], chan="cbs")
        P.dma("gpsimd", lambda e: e.dma_start(out=wbs[:], in_=wbias[:, :, :]), writes=["wbs"], chan="wbs")
        P.dma("sync", lambda e: e.dma_start(out=rC[:], in_=ropeC[:, :]), writes=["rC"], chan="rC")
        P.dma("sync", lambda e: e.dma_start(out=rS[:], in_=ropeS[:, :]), writes=["rS"], chan="rS")
        P.dma("sync", lambda e: e.dma_start(out=tA[:], in_=topA[:, :]), writes=["tA"], chan="tA")
        P.dma("sync", lambda e: e.dma_start(out=tB[:], in_=topB[:, :]), writes=["tB"], chan="tB")
        P.emit("vector", lambda e: e.tensor_scalar(out=w2R[:, :, 0:32], in0=w2b[:, 0, :, 32:64], scalar1=-1.0, scalar2=None, op0=ALU.mult), reads=["w2b"], writes=["w2R"])
        P.emit("vector", lambda e: e.tensor_copy(out=w2R[:, :, 32:64], in_=w2b[:, 0, :, 0:32]), reads=["w2b"], writes=["w2R"])
        P.emit("vector", lambda e: e.memset(kcT[:], 0.0), writes=["kcT"])
        P.emit("gpsimd", lambda e: e.memset(rhsC[:], 0.0), writes=["rhsC"])
        P.emit("vector", lambda e: e.memset(selb2[:], 0.0), writes=["selb2"])
        P.emit("gpsimd", lambda e: e.memset(Vs[:, :, 64:65], 1.0), writes=["Vs"])
        P.emit("gpsimd", lambda e: e.memset(Vw[:, :, 64:65], 1.0), writes=["Vw"])
        P.emit("gpsimd", lambda e: e.memset(vnA[:, 64:65], 1.0), writes=["vnA"])
        for ct in range(4):
            P.dma("gpsimd", lambda e, ct=ct: e.dma_start(out=rhsC[:, ct, 65:193], in_=ovl[ct, :, :]), writes=["rhsC"], chan="rhsC")
        for q4 in range(4):
            P.dma("gpsimd", lambda e, q4=q4: e.dma_start(out=KTa[64:128, q4 * 2048:(q4 + 1) * 2048], in_=EK[:, q4 * 2048:(q4 + 1) * 2048]), writes=["KTa"], chan="KTa")
        NCMP = 511
        cnt = {"sp": 0, "pt": 0, "tp": 0, "cp": 0}

        def nxt(k, n):
            cnt[k] += 1
            return cnt[k] % n

        def gather(j, bi):
            for d in range(8):
                for (prow, dst, nm) in ((rows_cmp, cTok, "cTok"), (rows_sel, sTok, "sTok")):
                    def g(e, j=j, d=d, prow=prow, dst=dst, bi=bi):
                        return e.indirect_dma_start(out=dst[bi][:, d * 8:(d + 1) * 8, :].rearrange("p r c -> p (r c)"), out_offset=None, in_=prow[:, :],
                                                    in_offset=bass.IndirectOffsetOnAxis(ap=pidx[:, j * 8 + d:j * 8 + d + 1], axis=0))
                    P.dma("gpsimd", g, reads=["pt"], writes=[f"{nm}{bi}"], chan=f"{nm}{bi}")

        gather(0, 0)
        for j in range(8):
            bi = j % 2
            if j + 1 < 8:
                gather(j + 1, 1 - bi)
            for (src_, dstT, nm, rows) in ((cTok, AT, "AT", slice(0, 128)), (sTok, KTa, "KTa", slice(0, 64))):
                for p4 in range(NPG // 4):
                    tb = nxt("tp", 2)
                    for q in range(4):
                        P.emit("tensor", lambda e, tb=tb, q=q, p4=p4, src_=src_, bi=bi: e.transpose(out=tpp[tb][:, q, :], in_=src_[bi][:, p4 * 4 + q, :], identity=ident[:]),
                               reads=[f"{'cTok' if nm == 'AT' else 'sTok'}{bi}", "ident"], writes=[f"tpp{tb}"])
                    eng = "scalar" if nxt("cp", 2) else "vector"
                    if nm == "AT":
                        d_, r0 = (p4 * 4) // 8, (p4 * 4) % 8
                        a_ = dstT[rows, d_ * 1024 + r0:d_ * 1024 + r0 + 1]
                        o_ap = bass.AP(a_.tensor, a_.offset, [list(a_.ap[0]), [1, 4], [128, 8], [8, 16]])
                        i_ap = tpp[tb][rows, :, :].rearrange("p q (s g) -> p q s g", s=8)
                    else:
                        o_ap = dstT[rows, p4 * 512:(p4 + 1) * 512]
                        i_ap = tpp[tb][rows, :, :].rearrange("p q t -> p (q t)")
                    if eng == "scalar":
                        P.emit("scalar", lambda e, o_ap=o_ap, i_ap=i_ap: e.copy(out=o_ap, in_=i_ap), reads=[f"tpp{tb}"], writes=[nm])
                    else:
                        P.emit("vector", lambda e, o_ap=o_ap, i_ap=i_ap: e.tensor_copy(out=o_ap, in_=i_ap), reads=[f"tpp{tb}"], writes=[nm])
            P.emit("gpsimd", lambda e, bi=bi: e.tensor_copy(out=Vs[:, :, 0:64], in_=sTok[bi][:, :, 64:128]), reads=[f"sTok{bi}"], writes=["Vs"])
            for kv in range(2):
                pr = slice(kv * 64, (kv + 1) * 64)
                for hc in range(2):
                    s = nxt("sp", 2)
                    for r in range(32):
                        P.emit("tensor", lambda e, s=s, r=r, hc=hc, pr=pr: e.matmul(sp[s][:, 0:1], lhsT=w1b[pr, r, hc * 128:(hc + 1) * 128], rhs=peb[pr, r:r + 1], start=(r == 0), stop=(r == 31)),
                               reads=["w1b", "peb"], writes=[f"sp{s}"])
                    P.emit("vector", lambda e, s=s, kv=kv, hc=hc: e.tensor_copy(out=b1[:, kv, hc:hc + 1], in_=sp[s][:, 0:1]), reads=[f"sp{s}"], writes=["b1"])
                    s = nxt("sp", 2)
                    for r in range(32):
                        P.emit("tensor", lambda e, s=s, r=r, hc=hc, pr=pr: e.matmul(sp[s][:, 0:NCMP], lhsT=w1b[pr, r, hc * 128:(hc + 1) * 128], rhs=fview(AT, pr, r, 16, NCMP), start=(r == 0), stop=(r == 31)),
                               reads=["w1b", "AT"], writes=[f"sp{s}"])
                    P.emit("scalar", lambda e, s=s, kv=kv, hc=hc: e.activation(out=hid[:, hc, 0:NCMP], in_=sp[s][:, 0:NCMP], func=AF.Silu, bias=b1[:, kv, hc:hc + 1]), reads=[f"sp{s}", "b1"], writes=["hid"])
                if kv == 0:
                    for hc in range(2):
                        P.emit("tensor", lambda e, hc=hc: e.matmul(sp[0][0:64, 0:NCMP], lhsT=w2b[:, 0, hc, :], rhs=hid[:, hc, 0:NCMP], start=(hc == 0), stop=(hc == 1)), reads=["w2b", "hid"], writes=["sp0"])
                    for hc in range(2):
                        P.emit("tensor", lambda e, hc=hc: e.matmul(sp[1][0:64, 0:NCMP], lhsT=w2R[:, hc, :], rhs=hid[:, hc, 0:NCMP], start=(hc == 0), stop=(hc == 1)), reads=["w2R", "hid"], writes=["sp1"])
                    P.emit("vector", lambda e: e.tensor_tensor(out=t64[0][:, 0:NCMP], in0=sp[0][0:64, 0:NCMP], in1=rC[:, 0:NCMP], op=ALU.mult), reads=["sp0", "rC"], writes=["t64_0"])
                    P.emit("vector", lambda e: e.tensor_tensor(out=t64[1][:, 0:NCMP], in0=sp[1][0:64, 0:NCMP], in1=rS[:, 0:NCMP], op=ALU.mult), reads=["sp1", "rS"], writes=["t64_1"])
                    P.emit("vector", lambda e: e.tensor_tensor(out=kcT[:, 0:NCMP], in0=t64[0][:, 0:NCMP], in1=t64[1][:, 0:NCMP], op=ALU.add), reads=["t64_0", "t64_1"], writes=["kcT"])
                else:
                    for ct in range(4):
                        nb = min(128, NCMP - ct * 128)
                        s = nxt("sp", 2)
                        for hc in range(2):
                            P.emit("tensor", lambda e, s=s, hc=hc, ct=ct, nb=nb: e.matmul(sp[s][0:nb, 0:64], lhsT=hid[:, hc, ct * 128:ct * 128 + nb], rhs=w2b[:, 1, hc, :], start=(hc == 0), stop=(hc == 1)),
                                   reads=["w2b", "hid"], writes=[f"sp{s}"])
                        P.emit("vector", lambda e, s=s, ct=ct, nb=nb: e.tensor_copy(out=rhsC[0:nb, ct, 0:64], in_=sp[s][0:nb, 0:64]), reads=[f"sp{s}"], writes=["rhsC"])
                        P.emit("gpsimd", lambda e, ct=ct, nb=nb: e.memset(rhsC[0:nb, ct, 64:65], 1.0), writes=["rhsC"])
            for h in range(2):
                P.dma("gpsimd", lambda e, j=j, h=h: e.dma_start(out=QTa[0:64, h, :], in_=qT[j, :, :]), writes=["QTa"], chan="QTa")
            P.dma("sync", lambda e, j=j: e.dma_start(out=gts[:], in_=gates[j, :, :]), writes=["gts"], chan="gts")
            P.dma("gpsimd", lambda e, j=j: e.dma_start(out=knT[:], in_=knewT[j, :, :]), writes=["knT"], chan="knT")
            P.dma("gpsimd", lambda e, j=j: e.dma_start(out=vnA[:, 0:64], in_=vnew[j, :, :]), writes=["vnA"], chan="vnA")
            P.emit("gpsimd", lambda e: e.memset(KwT[:], 0.0), writes=["KwT"])
            P.emit("gpsimd", lambda e: e.memset(Vw[:, 4, 0:64], 0.0), writes=["Vw"])
            P.dma("gpsimd", lambda e, j=j: e.dma_start(out=KwT[:, 0:520], in_=wKT[j, :, :]), writes=["KwT"], chan="KwT")
            P.dma("gpsimd", lambda e, j=j: e.dma_start(out=Vw[:, 0:4, 0:64], in_=wV[j, 0:512, :].rearrange("(k p) d -> p k d", p=128)), writes=["Vw"], chan="Vw")
            P.dma("gpsimd", lambda e, j=j: e.dma_start(out=Vw[0:8, 4, 0:64], in_=wV[j, 512:520, :]), writes=["Vw"], chan="Vw")
            P.dma("sync", lambda e, j=j: e.dma_start(out=wins_out[j, :, :], in_=wrows[j, DEC_T:DEC_T + WIN, :]), chan="wins")
            gv = gts[:, :].rearrange("p (g c) -> p g c", c=3)
            for ct in range(4):
                s = nxt("sp", 2)
                P.emit("tensor", lambda e, s=s, ct=ct: e.matmul(sp[s][:, 0:64], lhsT=kcT[:, ct * 128:(ct + 1) * 128], rhs=QTa[0:64, 0, :], start=True, stop=True), reads=["kcT", "QTa"], writes=[f"sp{s}"])
                p = nxt("pt", 3)
                P.emit("scalar", lambda e, s=s, p=p: e.activation(out=pT[p][:, 0:64], in_=sp[s][:, 0:64], func=AF.Exp, scale=0.125), reads=[f"sp{s}"], writes=[f"pT{p}"])
                for g in range(8):
                    P.emit("tensor", lambda e, p=p, g=g, ct=ct: e.matmul(acc[g // 2][0:8, (g % 2) * 193:(g % 2) * 193 + 193], lhsT=pT[p][:, g * 8:(g + 1) * 8], rhs=rhsC[:, ct, :],
                                                                   start=(ct == 0 and g % 2 == 0), stop=(ct == 3)), reads=[f"pT{p}", "rhsC"], writes=[f"acc{g // 2}"])
            r8 = slice(0, 8)
            for a in range(4):
                P.emit("vector", lambda e, a=a: e.tensor_copy(out=den[:, 2 * a:2 * a + 2], in_=fview(acc[a], r8, 64, 193, 2)), reads=[f"acc{a}"], writes=["den"])
            P.emit("vector", lambda e: e.tensor_scalar(out=rden[:], in0=den[:], scalar1=1e-30, scalar2=None, op0=ALU.max), reads=["den"], writes=["rden"])
            P.emit("vector", lambda e: e.reciprocal(out=rden[:], in_=rden[:]), reads=["rden"], writes=["rden"])
            for g in range(8):
                U = acc[g // 2][0:8, (g % 2) * 193 + 65:(g % 2) * 193 + 193]
                if g == 0:
                    P.emit("vector", lambda e, U=U: e.tensor_scalar(out=imp[:], in0=U, scalar1=rden[:, 0:1], scalar2=None, op0=ALU.mult), reads=["acc0", "rden"], writes=["imp"])
                else:
                    P.emit("vector", lambda e, U=U, g=g: e.scalar_tensor_tensor(out=imp[:], in0=U, scalar=rden[:, g:g + 1], in1=imp[:], op0=ALU.mult, op1=ALU.add), reads=[f"acc{g // 2}", "rden", "imp"], writes=["imp"])
            ob_ = j % 2
            P.emit("vector", lambda e, gv=gv: e.tensor_tensor(out=fsc[:], in0=rden[:], in1=gv[:, :, 0], op=ALU.mult), reads=["rden", "gts"], writes=["fsc"])
            for a in range(4):
                ov = acc[a][0:8, 0:386].rearrange("p (g c) -> p g c", c=193)[:, :, 0:64]
                P.emit("vector", lambda e, a=a, ov=ov, ob_=ob_: e.tensor_tensor(out=oacc[ob_][:, 2 * a:2 * a + 2, :], in0=ov, in1=bc(fsc[:, 2 * a:2 * a + 2], [8, 2, 64], 2), op=ALU.mult), reads=[f"acc{a}", "fsc"], writes=[f"oacc{ob_}"])
            P.emit("vector", lambda e: e.tensor_tensor(out=imp[:], in0=imp[:], in1=tA[:], op=ALU.mult), reads=["imp", "tA"], writes=["imp"])
            P.emit("vector", lambda e: e.tensor_tensor(out=imp[:], in0=imp[:], in1=tB[:], op=ALU.add), reads=["imp", "tB"], writes=["imp"])
            P.emit("vector", lambda e: e.max(out=mx1[:], in_=imp[:]), reads=["imp"], writes=["mx1"])
            P.emit("vector", lambda e: e.match_replace(out=wk[:], in_to_replace=mx1[:], in_values=imp[:], imm_value=-1e30), reads=["imp", "mx1"], writes=["wk"])
            P.emit("vector", lambda e: e.max(out=mx2[:], in_=wk[:]), reads=["wk"], writes=["mx2"])
            P.emit("vector", lambda e: e.tensor_scalar(out=selb2[:, :, 64:128], in0=imp[:].rearrange("p (h j) -> p h j", h=2), scalar1=mx2[:, 6:7], scalar2=NEGB, op0=ALU.is_lt, op1=ALU.mult),
                   reads=["imp", "mx2"], writes=["selb2"])
            for h in range(2):
                tb = nxt("tp", 2)
                P.emit("tensor", lambda e, h=h, tb=tb: e.transpose(out=tpp[tb][:, 0, 0:8], in_=selb2[:, h, :], identity=ident[0:8, 0:8]), reads=["selb2", "ident"], writes=[f"tpp{tb}"])
                P.emit("scalar", lambda e, h=h, tb=tb: e.copy(out=QTa[64:128, h, :].rearrange("p (g t) -> p g t", g=8), in_=bc(tpp[tb][64:128, 0, 0:8], [64, 8, 8], 1)), reads=[f"tpp{tb}"], writes=["QTa"])
            for k4 in range(NPG // 4):
                s = nxt("sp", 2)
                for q in range(4):
                    kt = k4 * 4 + q
                    P.emit("tensor", lambda e, s=s, q=q, kt=kt: e.matmul(sp[s][:, q * 64:(q + 1) * 64], lhsT=KTa[:, kt * 128:(kt + 1) * 128], rhs=QTa[:, kt // 32, :], start=True, stop=True),
                           reads=["KTa", "QTa"], writes=[f"sp{s}"])
                p = nxt("pt", 3)
                P.emit("scalar", lambda e, s=s, p=p: e.activation(out=pT[p][:, :], in_=sp[s][:, 0:256], func=AF.Exp, scale=0.125), reads=[f"sp{s}"], writes=[f"pT{p}"])
                for q in range(4):
                    kt = k4 * 4 + q
                    for g in range(8):
                        P.emit("tensor", lambda e, p=p, q=q, g=g, kt=kt: e.matmul(acc[g // 4][0:8, (g % 4) * 65:(g % 4) * 65 + 65], lhsT=pT[p][:, q * 64 + g * 8:q * 64 + (g + 1) * 8], rhs=Vs[:, kt, :],
                                                                            start=(kt == 0 and g % 4 == 0), stop=False), reads=[f"pT{p}", "Vs"], writes=[f"acc{g // 4}"])
            s = nxt("sp", 2)
            P.emit("tensor", lambda e, s=s: e.matmul(sp[s][0:8, 0:64], lhsT=knT[:, :], rhs=QTa[0:64, 0, :], start=True, stop=False), reads=["knT", "QTa"], writes=[f"sp{s}"])
            P.emit("tensor", lambda e, s=s: e.matmul(sp[s][0:8, 0:64], lhsT=ident[0:8, 0:8], rhs=cbs[:, :], start=False, stop=True), reads=["ident", "cbs"], writes=[f"sp{s}"])
            p = nxt("pt", 3)
            P.emit("scalar", lambda e, s=s, p=p: e.activation(out=pT[p][0:8, 0:64], in_=sp[s][0:8, 0:64], func=AF.Exp, scale=0.125), reads=[f"sp{s}"], writes=[f"pT{p}"])
            for g in range(8):
                P.emit("tensor", lambda e, p=p, g=g: e.matmul(acc[g // 4][0:8, (g % 4) * 65:(g % 4) * 65 + 65], lhsT=pT[p][0:8, g * 8:(g + 1) * 8], rhs=vnA[:, :], start=False, stop=True),
                       reads=[f"pT{p}", "vnA"], writes=[f"acc{g // 4}"])
            for kt in range(5):
                s = nxt("sp", 2)
                P.emit("tensor", lambda e, s=s, kt=kt: e.matmul(sp[s][:, 0:64], lhsT=KwT[:, kt * 128:(kt + 1) * 128], rhs=QTa[0:64, 0, :], start=True, stop=False), reads=["KwT", "QTa"], writes=[f"sp{s}"])
                P.emit("tensor", lambda e, s=s, kt=kt: e.matmul(sp[s][:, 0:64], lhsT=ident[:], rhs=wbs[:, kt, :], start=False, stop=True), reads=["ident", "wbs"], writes=[f"sp{s}"])
                p = nxt("pt", 3)
                P.emit("scalar", lambda e, s=s, p=p: e.activation(out=pT[p][:, 0:64], in_=sp[s][:, 0:64], func=AF.Exp, scale=0.125), reads=[f"sp{s}"], writes=[f"pT{p}"])
                for g in range(8):
                    P.emit("tensor", lambda e, p=p, g=g, kt=kt: e.matmul(acc[2 + g // 4][0:8, (g % 4) * 65:(g % 4) * 65 + 65], lhsT=pT[p][:, g * 8:(g + 1) * 8], rhs=Vw[:, kt, :],
                                                                   start=(kt == 0 and g % 4 == 0), stop=(kt == 4)), reads=[f"pT{p}", "Vw"], writes=[f"acc{2 + g // 4}"])
            for br, a0 in ((1, 0), (2, 2)):
                for hf in range(2):
                    a = a0 + hf
                    P.emit("vector", lambda e, a=a, hf=hf: e.tensor_copy(out=den[:, hf * 4:(hf + 1) * 4], in_=fview(acc[a], r8, 64, 65, 4)), reads=[f"acc{a}"], writes=["den"])
                P.emit("vector", lambda e: e.tensor_scalar(out=rden[:], in0=den[:], scalar1=1e-30, scalar2=None, op0=ALU.max), reads=["den"], writes=["rden"])
                P.emit("vector", lambda e: e.reciprocal(out=rden[:], in_=rden[:]), reads=["rden"], writes=["rden"])
                P.emit("vector", lambda e, gv=gv, br=br: e.tensor_tensor(out=fsc[:], in0=rden[:], in1=gv[:, :, br], op=ALU.mult), reads=["rden", "gts"], writes=["fsc"])
                for hf in range(2):
                    a = a0 + hf
                    ov = acc[a][0:8, 0:260].rearrange("p (g c) -> p g c", c=65)[:, :, 0:64]
                    P.emit("vector", lambda e, a=a, hf=hf, ov=ov: e.tensor_tensor(out=otmp[:, hf * 4:(hf + 1) * 4, :], in0=ov, in1=bc(fsc[:, hf * 4:(hf + 1) * 4], [8, 4, 64], 2), op=ALU.mult), reads=[f"acc{a}", "fsc"], writes=["otmp"])
                P.emit("gpsimd", lambda e, ob_=ob_: e.tensor_tensor(out=oacc[ob_][:], in0=oacc[ob_][:], in1=otmp[:], op=ALU.add), reads=[f"oacc{ob_}", "otmp"], writes=[f"oacc{ob_}"])
            P.dma("sync", lambda e, ob_=ob_, j=j: e.dma_start(out=o_out[j * 8:(j + 1) * 8, :], in_=oacc[ob_][:].rearrange("p g d -> p (g d)")), reads=[f"oacc{ob_}"], chan=f"oacc{ob_}")
        P.replay(nc, es)
    return nc


def run_nsa_sample(zs, cache_cmp, cache_sel, state_win, page_table, w1, w2, pe):
    T = lambda a: np.ascontiguousarray(a)
    base = nsa_consts(0)
    tok = np.arange(DEC_T)
    qp = (PAST + tok)[:, None]
    j = np.arange(128)[None, :]
    cur = qp // 64
    forced = (j == 0) | (j == cur) | (j == cur - 1)
    tA = (~forced).astype(np.float32)
    tB = np.where(forced, 1e6, 0.0).astype(np.float32)
    causb = np.tile(np.where(np.arange(8)[:, None] <= tok[None, :], 0.0, NEGB).astype(np.float32), (1, 8))
    wb = np.zeros((128, 5, 64), np.float32)
    for r in range(5):
        kp = (PAST - WIN + r * 128 + np.arange(128))[:, None]
        ok = (kp <= qp.T) & (kp > qp.T - WIN) & (kp < PAST + DEC_T)
        wb[:, r, :] = np.tile(np.where(ok, 0.0, NEGB), (1, 8))
    cc_ = np.arange(SEQ)
    kt_, p_ = cc_ // 128, cc_ % 128
    tokc = ((kt_ // 8) * 8 + p_ // 16) * 128 + (p_ % 16) * 8 + (kt_ % 8)
    EKp = (((tokc // 64) % 64)[None, :] == np.arange(64)[:, None]).astype(np.float32)
    assert ((tokc // 64) // 64 == kt_ // 32).all() and sorted(tokc) == list(range(SEQ))
    w1s = T(w1.reshape(2, 32, 64, 256).transpose(0, 2, 1, 3).reshape(128, 32, 256))
    pes = T(pe.transpose(0, 2, 1).reshape(128, 32))
    kvs = zs[:, :, 1024:1792].reshape(DEC_B, DEC_T, 3, 2, 2, 64)
    in_maps = []
    for c in range(NCORES):
        kvh, s0 = c % 2, (c // 2) * 8
        sq = slice(s0, s0 + 8)
        wall = np.concatenate([state_win[sq, :, :, kvh, :], kvs[sq, :, 2, :, kvh, :]], axis=1)
        in_maps.append({
            "pool_cmp": T(cache_cmp[:, :, :, kvh, :].reshape(NPOOL, 128, 128)),
            "pool_sel": T(cache_sel[:, :, :, kvh, :].reshape(NPOOL, 128, 128)),
            "ptl": T(np.stack([page_table[sq][:, d * 8 + np.arange(128) // 16] for d in range(8)], axis=1).transpose(2, 0, 1).reshape(128, 64).astype(np.int32)),
            "pmod": T((np.arange(128) % 16).astype(np.int32).reshape(128, 1)),
            "qT": T(zs[sq, :, 0:1024].reshape(8, DEC_T, 2, 8, 64)[:, :, kvh].transpose(0, 3, 2, 1).reshape(8, 64, 64)),
            "gates": T(zs[sq, :, 1792:1840].reshape(8, DEC_T, 2, 24)[:, :, kvh]),
            "knewT": T(kvs[sq, :, 1, 0, kvh].transpose(0, 2, 1)), "vnew": T(kvs[sq, :, 1, 1, kvh]),
            "wKT": T(wall[:, :, 0].transpose(0, 2, 1)), "wV": T(wall[:, :, 1]), "wrows": T(wall.reshape(8, 520, 128)),
            "EK": EKp, "topA": tA, "topB": tB, "ovl": base["ovl"], "ropeC": base["ropeC"], "ropeS": base["ropeS"],
            "causb": causb, "wbias": wb, "w1s": w1s, "w2": T(w2), "pes": pes})
    if "nsa_s" not in _NC_CACHE:
        _NC_CACHE["nsa_s"] = build_nsa_s()
    res = run_bass_kernel_spmd(_NC_CACHE["nsa_s"], in_maps, core_ids=list(range(NCORES))).results
    os_ = np.zeros((DEC_B, DEC_T, 2, 512), np.float32)
    win_s = np.zeros((DEC_B, WIN, 2, 2, 64), np.float32)
    for c in range(NCORES):
        kvh, s0 = c % 2, (c // 2) * 8
        os_[s0:s0 + 8, :, kvh] = res[c]["o_out"].reshape(8, DEC_T, 512)
        win_s[s0:s0 + 8, :, :, kvh, :] = res[c]["wins_out"].reshape(8, WIN, 2, 64)
    return os_.reshape(DEC_B, DEC_T, D), win_s


LAM_INIT = 0.8 - 0.6 * float(np.exp(-0.3 * 1))


def build_diff():
    nc = bass.Bass("TRN2", target_bir_lowering=False)
    dt_in = lambda n, s: nc.dram_tensor(n, s, F32, kind="ExternalInput").ap()
    qT = dt_in("qT", [NPT, 8, 128, 128])
    KTd = dt_in("KTd", [2, 8, 128, SEQ])
    Vd = dt_in("Vd", [2, 8, SEQ, 128])
    kbias = dt_in("kbias", [NPT, 128, 15, 128])
    lam_rep = dt_in("lam_rep", [128, 256])
    sub_rep = dt_in("sub_rep", [128, 128])
    o_out = nc.dram_tensor("o_out", [NPT * 128, D], F32, kind="ExternalOutput").ap()
    P = Prog()
    with ExitStack() as es:
        sb = lambda n, s, d: es.enter_context(nc.sbuf_tensor(n, s, d))
        ps = lambda n, s, d: es.enter_context(nc.psum_tensor(n, s, d))
        ident = make_ident(P, nc, es)
        KT = [sb(f"KT{i}", [128, SEQ], BF16) for i in range(2)]
        Va = [sb(f"Va{i}", [128, 64, 129], BF16) for i in range(2)]
        QB = [sb(f"QB{i}", [128, 2, 128], BF16) for i in range(2)]
        kb1 = sb("kb1", [128, 15, 128], BF16)
        kbr = [sb(f"kbr{i}", [128, 15, 2, 128], BF16) for i in range(2)]
        pT = [sb(f"pT{i}", [128, 256], BF16) for i in range(3)]
        lamt = sb("lamt", [128, 256], F32)
        lprod = sb("lprod", [128, 128], F32)
        lsum = sb("lsum", [128, 2], F32)
        nlam = sb("nlam", [128, 1], F32)
        gsub = sb("gsub", [128, 128], F32)
        den = sb("den", [128, 2], F32)
        rd = sb("rd", [128, 2], F32)
        t0 = sb("t0", [128, 128], F32)
        o1 = sb("o1", [128, 128], F32)
        junk = sb("junk", [128, 128], F32)
        ss = sb("ss", [128, 1], F32)
        rstd = sb("rstd", [128, 1], F32)
        ot = [sb(f"ot{i}", [128, 128], F32) for i in range(2)]
        sp = [ps(f"sp{i}", [128, 256], F32) for i in range(3)]
        acc = [ps(f"acc{i}", [128, 258], F32) for i in range(2)]
        P.dma("sync", lambda e: e.dma_start(out=lamt[:], in_=lam_rep[:, :]), writes=["lamt"], chan="lamt")
        P.dma("sync", lambda e: e.dma_start(out=gsub[:], in_=sub_rep[:, :]), writes=["gsub"], chan="gsub")
        P.emit("vector", lambda e: e.tensor_tensor(out=lprod[:].rearrange("p (a d) -> p a d", a=2), in0=lamt[:].rearrange("p (a two d) -> p a two d", a=2, two=2)[:, :, 0, :],
                                                   in1=lamt[:].rearrange("p (a two d) -> p a two d", a=2, two=2)[:, :, 1, :], op=ALU.mult), reads=["lamt"], writes=["lprod"])
        P.emit("vector", lambda e: e.reduce_sum(out=lsum[:], in_=lprod[:].rearrange("p (a d) -> p a d", a=2), axis=mybir.AxisListType.X), reads=["lprod"], writes=["lsum"])
        P.emit("scalar", lambda e: e.activation(out=lsum[:], in_=lsum[:], func=AF.Exp), reads=["lsum"], writes=["lsum"])
        P.emit("vector", lambda e: e.tensor_tensor(out=nlam[:], in0=lsum[:, 1:2], in1=lsum[:, 0:1], op=ALU.subtract), reads=["lsum"], writes=["nlam"])
        P.emit("vector", lambda e: e.tensor_scalar(out=nlam[:], in0=nlam[:], scalar1=-LAM_INIT, scalar2=None, op0=ALU.add), reads=["nlam"], writes=["nlam"])
        P.emit("vector", lambda e: e.tensor_scalar(out=gsub[:], in0=gsub[:], scalar1=1.0 - LAM_INIT, scalar2=None, op0=ALU.mult), reads=["gsub"], writes=["gsub"])
        for i in range(2):
            P.emit("gpsimd", lambda e, i=i: e.memset(Va[i][:, :, 128:129], 1.0), writes=[f"Va{i}"])
            P.emit("gpsimd", lambda e, i=i: e.memset(QB[i][:], 0.0), writes=[f"QB{i}"])
        spi, pti, it = [0], [0], 0
        for b in range(2):
            for h in range(8):
                cb = (b * 8 + h) % 2
                for q4 in range(4):
                    cs_ = slice(q4 * 2048, (q4 + 1) * 2048)
                    P.dma("gpsimd", lambda e, cb=cb, b=b, h=h, cs_=cs_: e.dma_start(out=KT[cb][:, cs_], in_=KTd[b, h, :, cs_]), writes=[f"KT{cb}"], chan=f"KT{cb}")
                    P.dma("gpsimd", lambda e, cb=cb, b=b, h=h, q4=q4: e.dma_start(out=Va[cb][:, q4 * 16:(q4 + 1) * 16, 0:128], in_=Vd[b, h, q4 * 2048:(q4 + 1) * 2048, :].rearrange("(k p) d -> p k d", p=128)),
                          writes=[f"Va{cb}"], chan=f"Va{cb}")
                for ti in range(NPT):
                    tb_, qmin, qmax = slot_range(ti)
                    if tb_ != b:
                        continue
                    ib = it % 2
                    it += 1
                    r0 = ti * 128
                    P.dma("gpsimd", lambda e, ib=ib, ti=ti, h=h: e.dma_start(out=QB[ib][0:64, 0, :], in_=qT[ti, h, 0:64, :]), writes=[f"QB{ib}"], chan=f"QB{ib}")
                    P.dma("gpsimd", lambda e, ib=ib, ti=ti, h=h: e.dma_start(out=QB[ib][64:128, 1, :], in_=qT[ti, h, 64:128, :]), writes=[f"QB{ib}"], chan=f"QB{ib}")
                    P.dma("gpsimd", lambda e, ti=ti: e.dma_start(out=kb1[:], in_=kbias[ti, :, :, :]), writes=["kb1"], chan="kb1")
                    P.emit("gpsimd", lambda e, ib=ib: e.tensor_copy(out=kbr[ib][:], in_=bc(kb1[:], [128, 15, 2, 128], 2)), reads=["kb1"], writes=[f"kbr{ib}"])
                    ab = it % 2
                    pend = None

                    def pv_diff(p, kt, qmax, cb, ab):
                        for c in range(2):
                            P.emit("tensor", lambda e, p=p, c=c, kt=kt: e.matmul(acc[ab][:, c * 129:(c + 1) * 129], lhsT=pT[p][:, c * 128:(c + 1) * 128], rhs=Va[cb][:, kt, :],
                                                                           start=(kt == 0 and c == 0), stop=(kt == qmax)), reads=[f"pT{p}", f"Va{cb}"], writes=[f"acc{ab}"])

                    for kt in range(qmax + 1):
                        spi[0] += 1
                        s = spi[0] % 3
                        hasb = kt >= qmin
                        P.emit("tensor", lambda e, s=s, cb=cb, ib=ib, kt=kt, hasb=hasb: e.matmul(sp[s][:, :], lhsT=KT[cb][:, kt * 128:(kt + 1) * 128], rhs=QB[ib][:].rearrange("p c t -> p (c t)"),
                                                                                           start=True, stop=(not hasb)), reads=[f"KT{cb}", f"QB{ib}"], writes=[f"sp{s}"])
                        if hasb:
                            P.emit("tensor", lambda e, s=s, ib=ib, kt=kt, qmin=qmin: e.matmul(sp[s][:, :], lhsT=ident[:], rhs=kbr[ib][:, kt - qmin, :, :].rearrange("p c t -> p (c t)"), start=False, stop=True),
                                   reads=["ident", f"kbr{ib}"], writes=[f"sp{s}"])
                        pti[0] += 1
                        p = pti[0] % 3
                        P.emit("scalar", lambda e, s=s, p=p: e.activation(out=pT[p][:], in_=sp[s][:], func=AF.Exp, scale=0.125), reads=[f"sp{s}"], writes=[f"pT{p}"])
                        if pend is not None:
                            pv_diff(*pend)
                        pend = (p, kt, qmax, cb, ab)
                    pv_diff(*pend)
                    A = f"acc{ab}"
                    P.emit("vector", lambda e, ab=ab: e.tensor_copy(out=den[:], in_=fview(acc[ab], slice(0, 128), 128, 129, 2)), reads=[A], writes=["den"])
                    P.emit("vector", lambda e: e.tensor_scalar(out=rd[:], in0=den[:], scalar1=1e-30, scalar2=None, op0=ALU.max), reads=["den"], writes=["rd"])
                    P.emit("vector", lambda e: e.reciprocal(out=rd[:], in_=rd[:]), reads=["rd"], writes=["rd"])
                    P.emit("vector", lambda e: e.tensor_tensor(out=rd[:, 1:2], in0=rd[:, 1:2], in1=nlam[:], op=ALU.mult), reads=["rd", "nlam"], writes=["rd"])
                    P.emit("vector", lambda e, ab=ab: e.tensor_scalar(out=t0[:], in0=acc[ab][:, 0:128], scalar1=rd[:, 0:1], scalar2=None, op0=ALU.mult), reads=[A, "rd"], writes=["t0"])
                    P.emit("vector", lambda e, ab=ab: e.scalar_tensor_tensor(out=o1[:], in0=acc[ab][:, 129:257], scalar=rd[:, 1:2], in1=t0[:], op0=ALU.mult, op1=ALU.add), reads=[A, "rd", "t0"], writes=["o1"])
                    P.emit("vector", lambda e: e.memset(ss[:], 0.0), writes=["ss"])
                    P.emit("scalar", lambda e: e.activation(out=junk[:], in_=o1[:], func=AF.Square, accum_out=ss[:, 0:1]), reads=["o1", "ss"], writes=["junk", "ss"])
                    P.emit("vector", lambda e: e.tensor_scalar(out=rstd[:], in0=ss[:], scalar1=1.0 / 128, scalar2=EPS, op0=ALU.mult, op1=ALU.add), reads=["ss"], writes=["rstd"])
                    P.emit("scalar", lambda e: e.activation(out=rstd[:], in_=rstd[:], func=AF.Sqrt), reads=["rstd"], writes=["rstd"])
                    P.emit("vector", lambda e: e.reciprocal(out=rstd[:], in_=rstd[:]), reads=["rstd"], writes=["rstd"])
                    ob_ = it % 2
                    P.emit("vector", lambda e, ob_=ob_: e.scalar_tensor_tensor(out=ot[ob_][:], in0=o1[:], scalar=rstd[:, 0:1], in1=gsub[:], op0=ALU.mult, op1=ALU.mult), reads=["o1", "rstd", "gsub"], writes=[f"ot{ob_}"])
                    P.dma("sync", lambda e, ob_=ob_, r0=r0, h=h: e.dma_start(out=o_out[r0:r0 + 128, h * 128:(h + 1) * 128], in_=ot[ob_][:]), reads=[f"ot{ob_}"], chan=f"ot{ob_}")
        P.replay(nc, es)
    return nc


def run_diff_prompt(z1p, lam_p, subnorm):
    T = lambda a: np.ascontiguousarray(a)
    KTd = T(z1p[:, :, 1024:2048].reshape(2, SEQ, 8, 128).transpose(0, 2, 3, 1))
    Vd = T(z1p[:, :, 2048:3072].reshape(2, SEQ, 8, 128).transpose(0, 2, 1, 3))
    lam_rep = T(np.broadcast_to(np.asarray(lam_p, np.float32).reshape(1, 256), (128, 256)))
    sub_rep = T(np.broadcast_to(np.asarray(subnorm, np.float32).reshape(1, 128), (128, 128)))
    in_maps = []
    for c in range(NCORES):
        tiles = core_tiles(c)
        qT = T(np.stack([z1p[b, p0:p0 + 128, 0:1024].reshape(128, 8, 128).transpose(1, 2, 0) for b, p0 in tiles]))
        in_maps.append({"qT": qT, "KTd": KTd, "Vd": Vd, "kbias": nsa_consts(c)["kbias"], "lam_rep": lam_rep, "sub_rep": sub_rep})
    if "diff" not in _NC_CACHE:
        _NC_CACHE["diff"] = build_diff()
    res = run_bass_kernel_spmd(_NC_CACHE["diff"], in_maps, core_ids=list(range(NCORES))).results
    op = np.zeros((2, SEQ, D), np.float32)
    for c in range(NCORES):
        for ti, (b_, p0) in enumerate(core_tiles(c)):
            op[b_, p0:p0 + 128] = res[c]["o_out"][ti * 128:(ti + 1) * 128]
    return op


def build_diff_s():
    nc = bass.Bass("TRN2", target_bir_lowering=False)
    dt_in = lambda n, s, d=F32: nc.dram_tensor(n, s, d, kind="ExternalInput").ap()
    pool = dt_in("pool", [NPOOL, 128, 256])
    ptl = dt_in("ptl", [128, DEC_B * 8], I32)
    pmod = dt_in("pmod", [128, 1], I32)
    qT = dt_in("qT", [DEC_B, 128, 8])
    knewT = dt_in("knewT", [DEC_B, 128, 8])
    vnew = dt_in("vnew", [DEC_B, 8, 128])
    causb = dt_in("causb", [8, 16])
    lam_rep = dt_in("lam_rep", [128, 256])
    sub_rep = dt_in("sub_rep", [128, 128])
    o_out = nc.dram_tensor("o_out", [DEC_B * DEC_T, 128], F32, kind="ExternalOutput").ap()
    P = Prog()
    with ExitStack() as es:
        sb = lambda n, s, d: es.enter_context(nc.sbuf_tensor(n, s, d))
        ps = lambda n, s, d: es.enter_context(nc.psum_tensor(n, s, d))
        ident = make_ident(P, nc, es)
        ptb = sb("ptb", [128, DEC_B * 8], I32)
        pidx = sb("pidx", [128, DEC_B * 8], I32)
        piota = sb("piota", [128, 1], I32)
        dTok = [sb(f"dTok{i}", [128, NPG, 256], BF16) for i in range(2)]
        KT = sb("KT", [128, SEQ], BF16)
        Va = sb("Va", [128, NPG, 129], BF16)
        QB = sb("QB", [128, 2, 8], BF16)
        knT = sb("knT", [128, 8], BF16)
        vnA = sb("vnA", [8, 129], BF16)
        cbs = sb("cbs", [8, 16], BF16)
        pT = [sb(f"pT{i}", [128, 128], BF16) for i in range(3)]
        lamt = sb("lamt", [128, 256], F32)
        lprod = sb("lprod", [128, 128], F32)
        lsum = sb("lsum", [128, 2], F32)
        nlam = sb("nlam", [128, 1], F32)
        gsub = sb("gsub", [128, 128], F32)
        den = sb("den", [8, 2], F32)
        rd = sb("rd", [8, 2], F32)
        t0 = sb("t0", [8, 128], F32)
        o1 = sb("o1", [8, 128], F32)
        junk = sb("junk", [8, 128], F32)
        ss = sb("ss", [8, 1], F32)
        rstd = sb("rstd", [8, 1], F32)
        ot = [sb(f"ot{i}", [8, 128], F32) for i in range(2)]
        tpp = [ps(f"tpp{i}", [128, 4, 128], BF16) for i in range(2)]
        sp = [ps(f"sp{i}", [128, 128], F32) for i in range(2)]
        acc = [ps(f"acc{i}", [128, 258], F32) for i in range(2)]
        P.dma("gpsimd", lambda e: e.dma_start(out=ptb[:], in_=ptl[:, :]), writes=["ptb"], chan="ptb")
        P.dma("gpsimd", lambda e: e.dma_start(out=piota[:], in_=pmod[:, :]), writes=["piota"], chan="piota")
        P.emit("gpsimd", lambda e: e.tensor_scalar(out=pidx[:], in0=ptb[:], scalar1=16, scalar2=None, op0=ALU.mult), reads=["ptb"], writes=["pt"])
        P.emit("gpsimd", lambda e: e.tensor_tensor(out=pidx[:], in0=pidx[:], in1=piota[:].to_broadcast([128, DEC_B * 8]), op=ALU.add), reads=["pt", "piota"], writes=["pt"])
        prow = pool.rearrange("n (g r) c -> (n g) (r c)", r=8)
        P.dma("sync", lambda e: e.dma_start(out=lamt[:], in_=lam_rep[:, :]), writes=["lamt"], chan="lamt")
        P.dma("sync", lambda e: e.dma_start(out=gsub[:], in_=sub_rep[:, :]), writes=["gsub"], chan="gsub")
        P.dma("gpsimd", lambda e: e.dma_start(out=cbs[:], in_=causb[:, :]), writes=["cbs"], chan="cbs")
        P.emit("vector", lambda e: e.tensor_tensor(out=lprod[:].rearrange("p (a d) -> p a d", a=2), in0=lamt[:].rearrange("p (a two d) -> p a two d", a=2, two=2)[:, :, 0, :],
                                                   in1=lamt[:].rearrange("p (a two d) -> p a two d", a=2, two=2)[:, :, 1, :], op=ALU.mult), reads=["lamt"], writes=["lprod"])
        P.emit("vector", lambda e: e.reduce_sum(out=lsum[:], in_=lprod[:].rearrange("p (a d) -> p a d", a=2), axis=mybir.AxisListType.X), reads=["lprod"], writes=["lsum"])
        P.emit("scalar", lambda e: e.activation(out=lsum[:], in_=lsum[:], func=AF.Exp), reads=["lsum"], writes=["lsum"])
        P.emit("vector", lambda e: e.tensor_tensor(out=nlam[:], in0=lsum[:, 1:2], in1=lsum[:, 0:1], op=ALU.subtract), reads=["lsum"], writes=["nlam"])
        P.emit("vector", lambda e: e.tensor_scalar(out=nlam[:], in0=nlam[:], scalar1=-LAM_INIT, scalar2=None, op0=ALU.add), reads=["nlam"], writes=["nlam"])
        P.emit("vector", lambda e: e.tensor_scalar(out=gsub[:], in0=gsub[:], scalar1=1.0 - LAM_INIT, scalar2=None, op0=ALU.mult), reads=["gsub"], writes=["gsub"])
        P.emit("gpsimd", lambda e: e.memset(Va[:, :, 128:129], 1.0), writes=["Va"])
        P.emit("gpsimd", lambda e: e.memset(vnA[:, 128:129], 1.0), writes=["vnA"])
        P.emit("gpsimd", lambda e: e.memset(QB[:], 0.0), writes=["QB"])
        cnt = {"sp": 0, "pt": 0, "tp": 0, "cp": 0}

        def nxt(k, n):
            cnt[k] += 1
            return cnt[k] % n

        def gather(j, bi):
            for d in range(8):
                def g(e, j=j, d=d, bi=bi):
                    return e.indirect_dma_start(out=dTok[bi][:, d * 8:(d + 1) * 8, :].rearrange("p r c -> p (r c)"), out_offset=None, in_=prow[:, :],
                                                in_offset=bass.IndirectOffsetOnAxis(ap=pidx[:, j * 8 + d:j * 8 + d + 1], axis=0))
                P.dma("gpsimd", g, reads=["pt"], writes=[f"dTok{bi}"], chan=f"dTok{bi}")

        gather(0, 0)
        r8 = slice(0, 8)
        for j in range(DEC_B):
            bi = j % 2
            if j + 1 < DEC_B:
                gather(j + 1, 1 - bi)
            for p4 in range(NPG // 4):
                tb = nxt("tp", 2)
                for q in range(4):
                    P.emit("tensor", lambda e, tb=tb, q=q, p4=p4, bi=bi: e.transpose(out=tpp[tb][:, q, :], in_=dTok[bi][:, p4 * 4 + q, 0:128], identity=ident[:]),
                           reads=[f"dTok{bi}", "ident"], writes=[f"tpp{tb}"])
                if nxt("cp", 2):
                    P.emit("scalar", lambda e, tb=tb, p4=p4: e.copy(out=KT[:, p4 * 512:(p4 + 1) * 512], in_=tpp[tb][:, :, :].rearrange("p q t -> p (q t)")), reads=[f"tpp{tb}"], writes=["KT"])
                else:
                    P.emit("vector", lambda e, tb=tb, p4=p4: e.tensor_copy(out=KT[:, p4 * 512:(p4 + 1) * 512], in_=tpp[tb][:, :, :].rearrange("p q t -> p (q t)")), reads=[f"tpp{tb}"], writes=["KT"])
            P.emit("gpsimd", lambda e, bi=bi: e.tensor_copy(out=Va[:, :, 0:128], in_=dTok[bi][:, :, 128:256]), reads=[f"dTok{bi}"], writes=["Va"])
            P.dma("gpsimd", lambda e, j=j: e.dma_start(out=QB[0:64, 0, :], in_=qT[j, 0:64, :]), writes=["QB"], chan="QB")
            P.dma("gpsimd", lambda e, j=j: e.dma_start(out=QB[64:128, 1, :], in_=qT[j, 64:128, :]), writes=["QB"], chan="QB")
            P.dma("gpsimd", lambda e, j=j: e.dma_start(out=knT[:], in_=knewT[j, :, :]), writes=["knT"], chan="knT")
            P.dma("gpsimd", lambda e, j=j: e.dma_start(out=vnA[:, 0:128], in_=vnew[j, :, :]), writes=["vnA"], chan="vnA")
            ab = j % 2
            for k8 in range(NPG // 8):
                s = nxt("sp", 2)
                for q in range(8):
                    kt = k8 * 8 + q
                    P.emit("tensor", lambda e, s=s, q=q, kt=kt: e.matmul(sp[s][:, q * 16:(q + 1) * 16], lhsT=KT[:, kt * 128:(kt + 1) * 128], rhs=QB[:].rearrange("p c t -> p (c t)"), start=True, stop=True),
                           reads=["KT", "QB"], writes=[f"sp{s}"])
                p = nxt("pt", 3)
                P.emit("scalar", lambda e, s=s, p=p: e.activation(out=pT[p][:, :], in_=sp[s][:, :], func=AF.Exp, scale=0.125), reads=[f"sp{s}"], writes=[f"pT{p}"])
                for q in range(8):
                    kt = k8 * 8 + q
                    for c in range(2):
                        P.emit("tensor", lambda e, p=p, q=q, c=c, kt=kt, ab=ab: e.matmul(acc[ab][0:8, c * 129:(c + 1) * 129], lhsT=pT[p][:, q * 16 + c * 8:q * 16 + (c + 1) * 8], rhs=Va[:, kt, :],
                                                                                   start=(kt == 0 and c == 0), stop=False), reads=[f"pT{p}", "Va"], writes=[f"acc{ab}"])
            s = nxt("sp", 2)
            P.emit("tensor", lambda e, s=s: e.matmul(sp[s][0:8, 0:16], lhsT=knT[:, :], rhs=QB[:].rearrange("p c t -> p (c t)"), start=True, stop=False), reads=["knT", "QB"], writes=[f"sp{s}"])
            P.emit("tensor", lambda e, s=s: e.matmul(sp[s][0:8, 0:16], lhsT=ident[0:8, 0:8], rhs=cbs[:, :], start=False, stop=True), reads=["ident", "cbs"], writes=[f"sp{s}"])
            p = nxt("pt", 3)
            P.emit("scalar", lambda e, s=s, p=p: e.activation(out=pT[p][0:8, 0:16], in_=sp[s][0:8, 0:16], func=AF.Exp, scale=0.125), reads=[f"sp{s}"], writes=[f"pT{p}"])
            for c in range(2):
                P.emit("tensor", lambda e, p=p, c=c, ab=ab: e.matmul(acc[ab][0:8, c * 129:(c + 1) * 129], lhsT=pT[p][0:8, c * 8:(c + 1) * 8], rhs=vnA[:, :], start=False, stop=True),
                       reads=[f"pT{p}", "vnA"], writes=[f"acc{ab}"])
            A = f"acc{ab}"
            P.emit("vector", lambda e, ab=ab: e.tensor_copy(out=den[:], in_=fview(acc[ab], r8, 128, 129, 2)), reads=[A], writes=["den"])
            P.emit("vector", lambda e: e.tensor_scalar(out=rd[:], in0=den[:], scalar1=1e-30, scalar2=None, op0=ALU.max), reads=["den"], writes=["rd"])
            P.emit("vector", lambda e: e.reciprocal(out=rd[:], in_=rd[:]), reads=["rd"], writes=["rd"])
            P.emit("vector", lambda e: e.tensor_tensor(out=rd[:, 1:2], in0=rd[:, 1:2], in1=nlam[r8, :], op=ALU.mult), reads=["rd", "nlam"], writes=["rd"])
            P.emit("vector", lambda e, ab=ab: e.tensor_scalar(out=t0[:], in0=acc[ab][0:8, 0:128], scalar1=rd[:, 0:1], scalar2=None, op0=ALU.mult), reads=[A, "rd"], writes=["t0"])
            P.emit("vector", lambda e, ab=ab: e.scalar_tensor_tensor(out=o1[:], in0=acc[ab][0:8, 129:257], scalar=rd[:, 1:2], in1=t0[:], op0=ALU.mult, op1=ALU.add), reads=[A, "rd", "t0"], writes=["o1"])
            P.emit("vector", lambda e: e.memset(ss[:], 0.0), writes=["ss"])
            P.emit("scalar", lambda e: e.activation(out=junk[:], in_=o1[:], func=AF.Square, accum_out=ss[:, 0:1]), reads=["o1", "ss"], writes=["junk", "ss"])
            P.emit("vector", lambda e: e.tensor_scalar(out=rstd[:], in0=ss[:], scalar1=1.0 / 128, scalar2=EPS, op0=ALU.mult, op1=ALU.add), reads=["ss"], writes=["rstd"])
            P.emit("scalar", lambda e: e.activation(out=rstd[:], in_=rstd[:], func=AF.Sqrt), reads=["rstd"], writes=["rstd"])
            P.emit("vector", lambda e: e.reciprocal(out=rstd[:], in_=rstd[:]), reads=["rstd"], writes=["rstd"])
            ob_ = j % 2
            P.emit("vector", lambda e, ob_=ob_: e.scalar_tensor_tensor(out=ot[ob_][:], in0=o1[:], scalar=rstd[:, 0:1], in1=gsub[r8, :], op0=ALU.mult, op1=ALU.mult), reads=["o1", "rstd", "gsub"], writes=[f"ot{ob_}"])
            P.dma("sync", lambda e, ob_=ob_, j=j: e.dma_start(out=o_out[j * 8:(j + 1) * 8, :], in_=ot[ob_][:]), reads=[f"ot{ob_}"], chan=f"ot{ob_}")
        P.replay(nc, es)
    return nc


def run_diff_sample(z1s, cache_kv, page_table, lam_p, subnorm):
    T = lambda a: np.ascontiguousarray(a)
    lam_rep = T(np.broadcast_to(np.asarray(lam_p, np.float32).reshape(1, 256), (128, 256)))
    sub_rep = T(np.broadcast_to(np.asarray(subnorm, np.float32).reshape(1, 128), (128, 128)))
    tok = np.arange(DEC_T)
    causb = np.tile(np.where(np.arange(8)[:, None] <= tok[None, :], 0.0, NEGB).astype(np.float32), (1, 2))
    pslot = np.arange(128) // 16
    ptl = T(np.stack([page_table[:, d * 8 + pslot] for d in range(8)], axis=1).transpose(2, 0, 1).reshape(128, DEC_B * 8).astype(np.int32))
    pmod = T((np.arange(128) % 16).astype(np.int32).reshape(128, 1))
    in_maps = []
    for h in range(NCORES):
        in_maps.append({
            "pool": T(cache_kv[:, :, :, h, :].reshape(NPOOL, 128, 256)), "ptl": ptl, "pmod": pmod,
            "qT": T(z1s[:, :, h * 128:(h + 1) * 128].transpose(0, 2, 1)),
            "knewT": T(z1s[:, :, 1024 + h * 128:1024 + (h + 1) * 128].transpose(0, 2, 1)),
            "vnew": T(z1s[:, :, 2048 + h * 128:2048 + (h + 1) * 128]),
            "causb": causb, "lam_rep": lam_rep, "sub_rep": sub_rep})
    if "diff_s" not in _NC_CACHE:
        _NC_CACHE["diff_s"] = build_diff_s()
    res = run_bass_kernel_spmd(_NC_CACHE["diff_s"], in_maps, core_ids=list(range(NCORES))).results
    os_ = np.zeros((DEC_B, DEC_T, 8, 128), np.float32)
    for h in range(NCORES):
        os_[:, :, h, :] = res[h]["o_out"].reshape(DEC_B, DEC_T, 128)
    return os_.reshape(DEC_B, DEC_T, D)


def rope_tables(pos):
    half = 32
    inv = (THETA ** (-np.arange(half, dtype=np.float32) / half)).astype(np.float32)
    ang = pos.astype(np.float32)[:, None] * inv[None, :]
    return np.cos(ang).astype(np.float32), np.sin(ang).astype(np.float32)


def core_positions(c):
    return np.concatenate([np.arange(p0, p0 + 128) for _, p0 in core_tiles(c)] + [np.tile(PAST + np.arange(DEC_T), 4)])


def shard_tokens(xp, xs, c):
    parts = [xp[b, p0:p0 + 128] for b, p0 in core_tiles(c)]
    parts.append(xs[4 * c:4 * c + 4].reshape(32, -1))
    return np.ascontiguousarray(np.concatenate(parts, axis=0))


def unshard_tokens(per_core, F):
    xp = np.zeros((2, SEQ, F), np.float32)
    xs = np.zeros((DEC_B, DEC_T, F), np.float32)
    for c in range(NCORES):
        r = per_core[c]
        for ti, (b, p0) in enumerate(core_tiles(c)):
            xp[b, p0:p0 + 128] = r[ti * 128:(ti + 1) * 128]
        xs[4 * c:4 * c + 4] = r[NPT * 128:].reshape(4, DEC_T, F)
    return xp, xs


def rep128(v):
    return np.ascontiguousarray(np.broadcast_to(np.asarray(v, np.float32)[None, :], (128, D)))


_NC_CACHE = {}


def run_proj(hp, hs, g, w, ncols, rope_sets, sig_cols):
    key = ("proj", ncols)
    if key not in _NC_CACHE:
        _NC_CACHE[key] = build_proj(ncols, rope_sets, sig_cols)
    in_maps = []
    for c in range(NCORES):
        cos, sin = rope_tables(core_positions(c))
        in_maps.append({"x": shard_tokens(hp, hs, c), "cs": cos, "sn": sin,
                        "w": np.ascontiguousarray(w, np.float32), "g_rep": rep128(g)})
    res = run_bass_kernel_spmd(_NC_CACHE[key], in_maps, core_ids=list(range(NCORES))).results
    return unshard_tokens([r["z_out"] for r in res], ncols)


def run_post(hp, hs, op, os_, w_out, g_mlp, w_up, w_down, g_fin, final):
    key = ("post", final)
    if key not in _NC_CACHE:
        _NC_CACHE[key] = build_post(final)
    in_maps = []
    for c in range(NCORES):
        in_maps.append({"x": shard_tokens(hp, hs, c), "o": shard_tokens(op, os_, c),
                        "w_out": np.ascontiguousarray(w_out, np.float32), "w_up": np.ascontiguousarray(w_up, np.float32),
                        "w_down": np.ascontiguousarray(w_down, np.float32), "g_rep": rep128(g_mlp), "gf_rep": rep128(g_fin)})
    res = run_bass_kernel_spmd(_NC_CACHE[key], in_maps, core_ids=list(range(NCORES))).results
    return unshard_tokens([r["h_out"] for r in res], D)


NSA_ROPE = [(0, 1, 0, 16), (1280, 2, 256, 2)]
DIFF_ROPE = [(0, 1, 0, 16), (1024, 1, 0, 16)]


def kernel(x_prompt, x_sample, cache_nsa_cmp, cache_nsa_sel, state_nsa_win, cache_diff_kv, page_table,
           norm_mix, norm_mlp, norm_final, nsa_w_in, nsa_cmp_pe, nsa_cmp_w1, nsa_cmp_w2, nsa_w_out,
           diff_w_in, diff_lambda, diff_subnorm, diff_w_out, mlp_w_up, mlp_w_down):
    f = lambda a: np.asarray(a, np.float32)
    x_prompt, x_sample = f(x_prompt), f(x_sample)
    zp, zs = run_proj(x_prompt, x_sample, f(norm_mix)[0], f(nsa_w_in)[0], NSA_IN, NSA_ROPE, (1792, 1840))
    op0 = run_nsa_prompt(zp, f(nsa_cmp_w1)[0], f(nsa_cmp_w2)[0], f(nsa_cmp_pe)[0])
    os0, win_s5 = run_nsa_sample(zs, f(cache_nsa_cmp)[0], f(cache_nsa_sel)[0], f(state_nsa_win)[0], np.asarray(page_table),
                                 f(nsa_cmp_w1)[0], f(nsa_cmp_w2)[0], f(nsa_cmp_pe)[0])
    h1p, h1s = run_post(x_prompt, x_sample, op0, os0, f(nsa_w_out)[0], f(norm_mlp)[0], f(mlp_w_up)[0], f(mlp_w_down)[0], f(norm_final), False)
    z1p, z1s = run_proj(h1p, h1s, f(norm_mix)[1], f(diff_w_in)[0], DIFF_IN, DIFF_ROPE, None)
    op1 = run_diff_prompt(z1p, f(diff_lambda)[0], f(diff_subnorm)[0])
    os1 = run_diff_sample(z1s, f(cache_diff_kv)[0], np.asarray(page_table), f(diff_lambda)[0], f(diff_subnorm)[0])
    yp, ys = run_post(h1p, h1s, op1, os1, f(diff_w_out)[0], f(norm_mlp)[1], f(mlp_w_up)[1], f(mlp_w_down)[1], f(norm_final), True)
    kv6 = lambda a: np.ascontiguousarray(a).reshape(a.shape[:2] + (2, 2, 64))[None]
    cmp_p, sel_p, win_p = kv6(zp[:, :, 1024:1280]), kv6(zp[:, :, 1280:1536]), kv6(zp[:, SEQ - WIN:, 1536:1792])
    cmp_s, sel_s = kv6(zs[:, :, 1024:1280]), kv6(zs[:, :, 1280:1536])
    win_s = win_s5[None]
    dkv = lambda z1: np.ascontiguousarray(z1[:, :, 1024:3072]).reshape(z1.shape[:2] + (2, 8, 128))[None]
    return (yp, ys, cmp_p, cmp_s, sel_p, sel_s, win_p, win_s, dkv(z1p), dkv(z1s))
```

```python
import numpy as np
from contextlib import ExitStack
import concourse.bass as bass
import concourse.mybir as mybir
from concourse.bass_utils import run_bass_kernel_spmd

F32 = mybir.dt.float32
BF16 = mybir.dt.bfloat16
I32 = mybir.dt.int32
AF = mybir.ActivationFunctionType
ALU = mybir.AluOpType

NCORES = 8
D = 1024
SEQ = 8192
PAST = 8192
DEC_B, DEC_T = 32, 8
NSA_IN = 1840
DIFF_IN = 3072
DFF = 4096
WIN = 512
EPS = 1e-6
NPT = 16
ROWS = NPT * 128 + 32
THETA = 10000.0
NEGB = -30000.0
PIPE = 2


def core_tiles(c):
    out = []
    for b in range(2):
        for u in sorted([c, 15 - c, 16 + c, 31 - c]):
            for s in range(2):
                out.append((b, u * 256 + s * 128))
    return out


def tile_rows():
    return [(i * 128, 128) for i in range(NPT)] + [(NPT * 128, 32)]


class Prog:
    ENG = ("sync", "scalar", "vector", "tensor", "gpsimd")

    def __init__(self):
        self.items = []
        self.cnt = {}
        self.lastw = {}
        self.readers = {}
        self.waited = {e: {} for e in self.ENG}

    def emit(self, eng, fn, reads=(), writes=(), inc=1, chan=None):
        key = (eng, chan)
        deps = {}

        def need(tok):
            if tok is None:
                return
            k, c = tok
            if k == ("tensor", None) and eng == "tensor":
                return
            if self.waited[eng].get(k, 0) >= c:
                return
            deps[k] = max(deps.get(k, 0), c)

        for b in reads:
            need(self.lastw.get(b))
        for b in writes:
            need(self.lastw.get(b))
            for k, c in self.readers.get(b, {}).items():
                need((k, c))
        for k, c in deps.items():
            self.waited[eng][k] = c
        c = self.cnt.get(key, 0) + inc
        self.cnt[key] = c
        for b in reads:
            self.readers.setdefault(b, {})[key] = c
        for b in writes:
            self.lastw[b] = (key, c)
            self.readers[b] = {}
        self.items.append((eng, deps, fn, key, inc))

    def dma(self, eng, fn, reads=(), writes=(), chan=None):
        assert chan is not None
        self.emit(eng, fn, reads=reads, writes=writes, inc=16, chan="d_" + chan)

    def replay(self, nc, es):
        sems = {k: es.enter_context(nc.semaphore("s%d" % i)) for i, k in enumerate(self.cnt)}
        block = es.enter_context(nc.Block())

        def make(name):
            def body(e):
                for eng, deps, fn, key, inc in self.items:
                    if eng != name:
                        continue
                    for k, c in deps.items():
                        e.wait_ge(sems[k], c)
                    fn(e).then_inc(sems[key], inc)
                if name == "sync":
                    for k, c in self.cnt.items():
                        e.wait_ge(sems[k], c)
            return body
        for name in self.ENG:
            getattr(block, name)(make(name))


def bc(ap, shape, axis):
    return ap.unsqueeze(axis).to_broadcast(list(shape))


def make_ident(P, nc, es):
    identf = es.enter_context(nc.sbuf_tensor("identf", [128, 128], F32))
    ident = es.enter_context(nc.sbuf_tensor("ident", [128, 128], BF16))
    P.emit("gpsimd", lambda e: e.memset(identf[:], 0.0), writes=["identf"])
    P.emit("gpsimd", lambda e: e.affine_select(out=identf[:], in_=identf[:], pattern=[[-1, 128]],
                                               compare_op=ALU.not_equal, fill=1.0, base=0, channel_multiplier=1),
           reads=["identf"], writes=["identf"])
    P.emit("vector", lambda e: e.tensor_copy(out=ident[:], in_=identf[:]), reads=["identf"], writes=["ident"])
    return ident


def emit_rmsnorm(P, xt, rows, gt, hn, ss, rstd, junk, tag):
    P.emit("vector", lambda e: e.memset(ss[rows, :], 0.0), writes=["ss" + tag])
    P.emit("scalar", lambda e: e.activation(out=junk[rows, :], in_=xt[rows, :], func=AF.Square, accum_out=ss[rows, 0:1]),
           reads=["xt" + tag, "ss" + tag], writes=["junk" + tag, "ss" + tag])
    P.emit("vector", lambda e: e.tensor_scalar(out=rstd[rows, :], in0=ss[rows, :], scalar1=1.0 / D, scalar2=EPS,
                                               op0=ALU.mult, op1=ALU.add), reads=["ss" + tag], writes=["rstd" + tag])
    P.emit("scalar", lambda e: e.activation(out=rstd[rows, :], in_=rstd[rows, :], func=AF.Sqrt),
           reads=["rstd" + tag], writes=["rstd" + tag])
    P.emit("vector", lambda e: e.reciprocal(out=rstd[rows, :], in_=rstd[rows, :]), reads=["rstd" + tag], writes=["rstd" + tag])
    P.emit("vector", lambda e: e.scalar_tensor_tensor(out=hn[rows, :], in0=xt[rows, :], scalar=rstd[rows, 0:1],
                                                      in1=gt[rows, :], op0=ALU.mult, op1=ALU.mult),
           reads=["xt" + tag, "rstd" + tag, "gt"], writes=["hn" + tag])


def build_proj(ncols, rope_sets, sig_cols):
    nc = bass.Bass("TRN2", target_bir_lowering=False)
    x = nc.dram_tensor("x", [ROWS, D], F32, kind="ExternalInput").ap()
    cs = nc.dram_tensor("cs", [ROWS, 32], F32, kind="ExternalInput").ap()
    sn = nc.dram_tensor("sn", [ROWS, 32], F32, kind="ExternalInput").ap()
    w = nc.dram_tensor("w", [D, ncols], F32, kind="ExternalInput").ap()
    g_rep = nc.dram_tensor("g_rep", [128, D], F32, kind="ExternalInput").ap()
    z_out = nc.dram_tensor("z_out", [ROWS, ncols], F32, kind="ExternalOutput").ap()
    P = Prog()
    with ExitStack() as es:
        sb = lambda n, s, d: es.enter_context(nc.sbuf_tensor(n, s, d))
        wb = sb("wb", [128, 8, ncols], BF16)
        gt = sb("gt", [128, D], F32)
        NB = 2
        xt = [sb(f"xt{i}", [128, D], F32) for i in range(NB)]
        junk = [sb(f"junk{i}", [128, D], F32) for i in range(NB)]
        hn = [sb(f"hn{i}", [128, D], BF16) for i in range(NB)]
        hnT = [sb(f"hnT{i}", [128, 8, 128], BF16) for i in range(NB)]
        cst = [sb(f"cst{i}", [128, 32], F32) for i in range(NB)]
        snt = [sb(f"snt{i}", [128, 32], F32) for i in range(NB)]
        ss = [sb(f"ss{i}", [128, 1], F32) for i in range(NB)]
        rstd = [sb(f"rstd{i}", [128, 1], F32) for i in range(NB)]
        zsb = [sb(f"zsb{i}", [128, ncols], F32) for i in range(NB)]
        tmp = [sb(f"tmp{i}", [128, 1024], F32) for i in range(4)]
        tp = [es.enter_context(nc.psum_tensor(f"tp{i}", [128, 8, 128], BF16)) for i in range(NB)]
        zp = [es.enter_context(nc.psum_tensor(f"zp{i}", [128, 512], F32)) for i in range(4)]
        ident = make_ident(P, nc, es)
        for k in range(8):
            P.dma("gpsimd", lambda e, k=k: e.dma_start(out=wb[:, k, :], in_=w[k * 128:(k + 1) * 128, :]), writes=["wb"], chan="wb")
        P.dma("sync", lambda e: e.dma_start(out=gt[:], in_=g_rep[:, :]), writes=["gt"], chan="gt")
        chunks = [(c0, min(512, ncols - c0)) for c0 in range(0, ncols, 512)]
        zi = 0
        for ti, (r0, R) in enumerate(tile_rows()):
            bi = ti % NB
            t = str(bi)
            rows = slice(0, R)
            P.dma("sync", lambda e, bi=bi, r0=r0, R=R: e.dma_start(out=xt[bi][0:R, :], in_=x[r0:r0 + R, :]), writes=["xt" + t], chan="xt" + t)
            P.dma("sync", lambda e, bi=bi, r0=r0, R=R: e.dma_start(out=cst[bi][0:R, :], in_=cs[r0:r0 + R, :]), writes=["cst" + t], chan="cst" + t)
            P.dma("sync", lambda e, bi=bi, r0=r0, R=R: e.dma_start(out=snt[bi][0:R, :], in_=sn[r0:r0 + R, :]), writes=["snt" + t], chan="snt" + t)
            emit_rmsnorm(P, xt[bi], rows, gt, hn[bi], ss[bi], rstd[bi], junk[bi], t)
            for k in range(8):
                P.emit("tensor", lambda e, k=k, R=R, bi=bi: e.transpose(out=tp[bi][:, k, 0:R], in_=hn[bi][0:R, k * 128:(k + 1) * 128],
                                                                        identity=ident[0:R, 0:R]),
                       reads=["hn" + t, "ident"], writes=["tp" + t])
            P.emit("scalar", lambda e, R=R, bi=bi: e.copy(out=hnT[bi][:, :, 0:R], in_=tp[bi][:, :, 0:R]), reads=["tp" + t], writes=["hnT" + t])
            for (c0, wd) in chunks:
                zb = zi % 4
                zi += 1
                for k in range(8):
                    P.emit("tensor", lambda e, zb=zb, k=k, c0=c0, wd=wd, R=R, bi=bi: e.matmul(
                        zp[zb][0:R, 0:wd], lhsT=hnT[bi][:, k, 0:R], rhs=wb[:, k, c0:c0 + wd], start=(k == 0), stop=(k == 7)),
                        reads=["hnT" + t, "wb"], writes=[f"zp{zb}"])
                P.emit("scalar", lambda e, zb=zb, c0=c0, wd=wd, rows=rows, bi=bi: e.copy(out=zsb[bi][rows, c0:c0 + wd], in_=zp[zb][rows, 0:wd]),
                       reads=[f"zp{zb}"], writes=["zsb" + t])
            for (col0, no, ostride, ni) in rope_sets:
                def views(bi=bi, col0=col0, no=no, ostride=ostride, ni=ni, R=R):
                    a1 = zsb[bi][0:R, col0:col0 + 1]
                    a2 = zsb[bi][0:R, col0 + 32:col0 + 33]
                    pat = [[ostride if no > 1 else 64 * ni, no], [64, ni], [1, 32]]
                    x1 = bass.AP(a1.tensor, a1.offset, [list(a1.ap[0])] + pat)
                    x2 = bass.AP(a2.tensor, a2.offset, [list(a2.ap[0])] + pat)
                    return x1, x2
                x1, x2 = views()
                ng = no * ni
                shp = [R, no, ni, 32]
                c4 = cst[bi][0:R, :].unsqueeze(1).unsqueeze(1).to_broadcast(shp)
                s4 = snt[bi][0:R, :].unsqueeze(1).unsqueeze(1).to_broadcast(shp)
                a = [tmp[i][0:R, 0:ng * 32].rearrange("p (o i f) -> p o i f", o=no, i=ni, f=32) for i in range(4)]
                P.emit("vector", lambda e, a=a, x1=x1, c4=c4: e.tensor_tensor(out=a[0], in0=x1, in1=c4, op=ALU.mult), reads=["zsb" + t, "cst" + t], writes=["tmp0"])
                P.emit("gpsimd", lambda e, a=a, x2=x2, s4=s4: e.tensor_tensor(out=a[1], in0=x2, in1=s4, op=ALU.mult), reads=["zsb" + t, "snt" + t], writes=["tmp1"])
                P.emit("vector", lambda e, a=a, x2=x2, c4=c4: e.tensor_tensor(out=a[2], in0=x2, in1=c4, op=ALU.mult), reads=["zsb" + t, "cst" + t], writes=["tmp2"])
                P.emit("gpsimd", lambda e, a=a, x1=x1, s4=s4: e.tensor_tensor(out=a[3], in0=x1, in1=s4, op=ALU.mult), reads=["zsb" + t, "snt" + t], writes=["tmp3"])
                P.emit("vector", lambda e, a=a, x1=x1: e.tensor_tensor(out=x1, in0=a[0], in1=a[1], op=ALU.subtract), reads=["tmp0", "tmp1"], writes=["zsb" + t])
                P.emit("gpsimd", lambda e, a=a, x2=x2: e.tensor_tensor(out=x2, in0=a[2], in1=a[3], op=ALU.add), reads=["tmp2", "tmp3"], writes=["zsb" + t])
            if sig_cols is not None:
                s0, s1 = sig_cols
                P.emit("scalar", lambda e, bi=bi, rows=rows, s0=s0, s1=s1: e.activation(out=zsb[bi][rows, s0:s1], in_=zsb[bi][rows, s0:s1], func=AF.Sigmoid),
                       reads=["zsb" + t], writes=["zsb" + t])
            P.dma("sync", lambda e, bi=bi, r0=r0, R=R: e.dma_start(out=z_out[r0:r0 + R, :], in_=zsb[bi][0:R, :]), reads=["zsb" + t], chan="zsb" + t)
        P.replay(nc, es)
    return nc


def build_post(final):
    nc = bass.Bass("TRN2", target_bir_lowering=False)
    x = nc.dram_tensor("x", [ROWS, D], F32, kind="ExternalInput").ap()
    o = nc.dram_tensor("o", [ROWS, D], F32, kind="ExternalInput").ap()
    w_out = nc.dram_tensor("w_out", [D, D], F32, kind="ExternalInput").ap()
    w_up = nc.dram_tensor("w_up", [D, DFF], F32, kind="ExternalInput").ap()
    w_down = nc.dram_tensor("w_down", [DFF, D], F32, kind="ExternalInput").ap()
    g_rep = nc.dram_tensor("g_rep", [128, D], F32, kind="ExternalInput").ap()
    gf_rep = nc.dram_tensor("gf_rep", [128, D], F32, kind="ExternalInput").ap()
    h_out = nc.dram_tensor("h_out", [ROWS, D], F32, kind="ExternalOutput").ap()
    P = Prog()
    with ExitStack() as es:
        sb = lambda n, s, d: es.enter_context(nc.sbuf_tensor(n, s, d))
        wo = sb("wo", [128, 8, D], BF16)
        wu = sb("wu", [128, 8, DFF], BF16)
        wd_ = sb("wd", [128, 32, D], BF16)
        gt = sb("gt", [128, D], F32)
        gf = sb("gf", [128, D], F32)
        xh = sb("xh", [128, 2, D], F32)
        ost = sb("ost", [128, D], F32)
        ob = sb("ob", [128, 2, D], BF16)
        aT = sb("aT", [128, 8, 256], BF16)
        uT = sb("uT", [128, 32, 256], BF16)
        rt = [sb(f"rt{i}", [128, 512], F32) for i in range(2)]
        ss = sb("ss", [128, 1], F32)
        rstd = sb("rstd", [128, 1], F32)
        tp = [es.enter_context(nc.psum_tensor(f"tp{i}", [128, 8, 128], BF16)) for i in range(2)]
        mp = [es.enter_context(nc.psum_tensor(f"mp{i}", [128, 512], F32)) for i in range(4)]
        ident = make_ident(P, nc, es)
        for k in range(8):
            P.dma("gpsimd", lambda e, k=k: e.dma_start(out=wo[:, k, :], in_=w_out[k * 128:(k + 1) * 128, :]), writes=["wo"], chan="wo")
        for k in range(8):
            for hh in range(2):
                P.dma("gpsimd", lambda e, k=k, hh=hh: e.dma_start(out=wu[:, k, hh * 2048:(hh + 1) * 2048], in_=w_up[k * 128:(k + 1) * 128, hh * 2048:(hh + 1) * 2048]),
                      writes=["wu"], chan="wu")
        for f in range(32):
            P.dma("gpsimd", lambda e, f=f: e.dma_start(out=wd_[:, f, :], in_=w_down[f * 128:(f + 1) * 128, :]), writes=["wd"], chan="wd")
        P.dma("sync", lambda e: e.dma_start(out=gt[:], in_=g_rep[:, :]), writes=["gt"], chan="gt")
        P.dma("sync", lambda e: e.dma_start(out=gf[:], in_=gf_rep[:, :]), writes=["gf"], chan="gf")
        groups = [[(2 * gi) * 128, 128, (2 * gi + 1) * 128, 128] for gi in range(NPT // 2)] + [[NPT * 128, 32]]
        mi = 0
        for grp in groups:
            tl = [(grp[i], grp[i + 1]) for i in range(0, len(grp), 2)]
            NT = sum(R for _, R in tl)
            for j, (r0, R) in enumerate(tl):
                P.dma("sync", lambda e, j=j, r0=r0, R=R: e.dma_start(out=xh[0:R, j, :], in_=x[r0:r0 + R, :]), writes=["xh"], chan="xh")
                P.dma("sync", lambda e, r0=r0, R=R: e.dma_start(out=ost[0:R, :], in_=o[r0:r0 + R, :]), writes=["ost"], chan="ost")
                P.emit("scalar", lambda e, j=j, R=R: e.copy(out=ob[0:R, j, :], in_=ost[0:R, :]), reads=["ost"], writes=["ob"])
                tb = j % 2
                for k in range(8):
                    P.emit("tensor", lambda e, k=k, R=R, j=j, tb=tb: e.transpose(out=tp[tb][:, k, 0:R], in_=ob[0:R, j, k * 128:(k + 1) * 128], identity=ident[0:R, 0:R]),
                           reads=["ob", "ident"], writes=[f"tp{tb}"])
                P.emit("scalar", lambda e, j=j, R=R, tb=tb: e.copy(out=aT[:, :, j * 128:j * 128 + R], in_=tp[tb][:, :, 0:R]), reads=[f"tp{tb}"], writes=["aT"])
            for j, (r0, R) in enumerate(tl):
                for hf in range(2):
                    mb = mi % 4
                    mi += 1
                    for k in range(8):
                        P.emit("tensor", lambda e, mb=mb, k=k, j=j, R=R, hf=hf: e.matmul(mp[mb][0:R, :], lhsT=aT[:, k, j * 128:j * 128 + R], rhs=wo[:, k, hf * 512:(hf + 1) * 512],
                                                                                     start=(k == 0), stop=(k == 7)), reads=["aT", "wo"], writes=[f"mp{mb}"])
                    P.emit("vector", lambda e, mb=mb, j=j, R=R, hf=hf: e.tensor_tensor(out=xh[0:R, j, hf * 512:(hf + 1) * 512], in0=mp[mb][0:R, :], in1=xh[0:R, j, hf * 512:(hf + 1) * 512], op=ALU.add),
                           reads=[f"mp{mb}", "xh"], writes=["xh"])
            for j, (r0, R) in enumerate(tl):
                rows = slice(0, R)
                P.emit("vector", lambda e, rows=rows: e.memset(ss[rows, :], 0.0), writes=["ss"])
                P.emit("scalar", lambda e, rows=rows, j=j: e.activation(out=ost[rows, :], in_=xh[rows, j, :], func=AF.Square, accum_out=ss[rows, 0:1]),
                       reads=["xh", "ss"], writes=["ost", "ss"])
                P.emit("vector", lambda e, rows=rows: e.tensor_scalar(out=rstd[rows, :], in0=ss[rows, :], scalar1=1.0 / D, scalar2=EPS, op0=ALU.mult, op1=ALU.add),
                       reads=["ss"], writes=["rstd"])
                P.emit("scalar", lambda e, rows=rows: e.activation(out=rstd[rows, :], in_=rstd[rows, :], func=AF.Sqrt), reads=["rstd"], writes=["rstd"])
                P.emit("vector", lambda e, rows=rows: e.reciprocal(out=rstd[rows, :], in_=rstd[rows, :]), reads=["rstd"], writes=["rstd"])
                P.emit("vector", lambda e, rows=rows, j=j: e.scalar_tensor_tensor(out=ob[rows, j, :], in0=xh[rows, j, :], scalar=rstd[rows, 0:1], in1=gt[rows, :], op0=ALU.mult, op1=ALU.mult),
                       reads=["xh", "rstd", "gt"], writes=["ob"])
                tb = j % 2
                for k in range(8):
                    P.emit("tensor", lambda e, k=k, R=R, j=j, tb=tb: e.transpose(out=tp[tb][:, k, 0:R], in_=ob[0:R, j, k * 128:(k + 1) * 128], identity=ident[0:R, 0:R]),
                           reads=["ob", "ident"], writes=[f"tp{tb}"])
                P.emit("scalar", lambda e, j=j, R=R, tb=tb: e.copy(out=aT[:, :, j * 128:j * 128 + R], in_=tp[tb][:, :, 0:R]), reads=[f"tp{tb}"], writes=["aT"])
            for fc in range(32):
                mb = mi % 4
                mi += 1
                for k in range(8):
                    P.emit("tensor", lambda e, mb=mb, k=k, fc=fc, NT=NT: e.matmul(mp[mb][:, 0:NT], lhsT=wu[:, k, fc * 128:(fc + 1) * 128], rhs=aT[:, k, 0:NT], start=(k == 0), stop=(k == 7)),
                           reads=["aT", "wu"], writes=[f"mp{mb}"])
                rb = fc % 2
                P.emit("scalar", lambda e, mb=mb, rb=rb, NT=NT: e.activation(out=rt[rb][:, 0:NT], in_=mp[mb][:, 0:NT], func=AF.Relu), reads=[f"mp{mb}"], writes=[f"rt{rb}"])
                P.emit("vector", lambda e, rb=rb, fc=fc, NT=NT: e.tensor_tensor(out=uT[:, fc, 0:NT], in0=rt[rb][:, 0:NT], in1=rt[rb][:, 0:NT], op=ALU.mult), reads=[f"rt{rb}"], writes=["uT"])
            for j, (r0, R) in enumerate(tl):
                for hf in range(2):
                    mb = mi % 4
                    mi += 1
                    for fc in range(32):
                        P.emit("tensor", lambda e, mb=mb, fc=fc, j=j, R=R, hf=hf: e.matmul(mp[mb][0:R, :], lhsT=uT[:, fc, j * 128:j * 128 + R], rhs=wd_[:, fc, hf * 512:(hf + 1) * 512],
                                                                                       start=(fc == 0), stop=(fc == 31)), reads=["uT", "wd"], writes=[f"mp{mb}"])
                    P.emit("vector", lambda e, mb=mb, j=j, R=R, hf=hf: e.tensor_tensor(out=xh[0:R, j, hf * 512:(hf + 1) * 512], in0=mp[mb][0:R, :], in1=xh[0:R, j, hf * 512:(hf + 1) * 512], op=ALU.add),
                           reads=[f"mp{mb}", "xh"], writes=["xh"])
                if final:
                    rows = slice(0, R)
                    P.emit("vector", lambda e, rows=rows: e.memset(ss[rows, :], 0.0), writes=["ss"])
                    P.emit("scalar", lambda e, rows=rows, j=j: e.activation(out=ost[rows, :], in_=xh[rows, j, :], func=AF.Square, accum_out=ss[rows, 0:1]),
                           reads=["xh", "ss"], writes=["ost", "ss"])
                    P.emit("vector", lambda e, rows=rows: e.tensor_scalar(out=rstd[rows, :], in0=ss[rows, :], scalar1=1.0 / D, scalar2=EPS, op0=ALU.mult, op1=ALU.add),
                           reads=["ss"], writes=["rstd"])
                    P.emit("scalar", lambda e, rows=rows: e.activation(out=rstd[rows, :], in_=rstd[rows, :], func=AF.Sqrt), reads=["rstd"], writes=["rstd"])
                    P.emit("vector", lambda e, rows=rows: e.reciprocal(out=rstd[rows, :], in_=rstd[rows, :]), reads=["rstd"], writes=["rstd"])
                    P.emit("vector", lambda e, rows=rows, j=j: e.scalar_tensor_tensor(out=xh[rows, j, :], in0=xh[rows, j, :], scalar=rstd[rows, 0:1], in1=gf[rows, :], op0=ALU.mult, op1=ALU.mult),
                           reads=["xh", "rstd", "gf"], writes=["xh"])
                P.dma("sync", lambda e, j=j, r0=r0, R=R: e.dma_start(out=h_out[r0:r0 + R, :], in_=xh[0:R, j, :]), reads=["xh"], chan="xh")
        P.replay(nc, es)
    return nc


def fview(t, rows, start, step, count):
    a = t[rows, start:start + 1]
    return bass.AP(a.tensor, a.offset, [list(a.ap[0]), [step, count]])


def slot_range(slot):
    k, s = (slot % 8) // 2, slot % 2
    return slot // 8, 16 * k + s, 16 * k + 14 + s


def build_nsa():
    nc = bass.Bass("TRN2", target_bir_lowering=False)
    dt_in = lambda n, s: nc.dram_tensor(n, s, F32, kind="ExternalInput").ap()
    qT = dt_in("qT", [NPT, 2, 64, 1024])
    gates = dt_in("gates", [ROWS, 48])
    selKT = dt_in("selKT", [2, 2, 64, SEQ])
    selV = dt_in("selV", [2, 2, SEQ, 64])
    winKT = dt_in("winKT", [NPT, 2, 64, 640])
    winV = dt_in("winV", [NPT, 2, 640, 64])
    kbias = dt_in("kbias", [NPT, 128, 15, 128])
    wbias = dt_in("wbias", [NPT, 128, 5, 128])
    cmpA = dt_in("cmpA", [2, 2, 2, 128, SEQ])
    EK = dt_in("EK", [64, SEQ])
    topA = dt_in("topA", [NPT, 128, 128])
    topB = dt_in("topB", [NPT, 128, 128])
    cmpbias = dt_in("cmpbias", [NPT, 4, 128, 128])
    ovl = dt_in("ovl", [4, 128, 128])
    ropeC = dt_in("ropeC", [64, 512])
    ropeS = dt_in("ropeS", [64, 512])
    w1 = dt_in("w1", [2, 2048, 256])
    w2 = dt_in("w2", [2, 256, 64])
    peT = dt_in("peT", [2, 128, 16])
    o_out = nc.dram_tensor("o_out", [NPT * 128, D], F32, kind="ExternalOutput").ap()
    P = Prog()
    with ExitStack() as es:
        sb = lambda n, s, d: es.enter_context(nc.sbuf_tensor(n, s, d))
        ps = lambda n, s, d: es.enter_context(nc.psum_tensor(n, s, d))
        ident = make_ident(P, nc, es)
        w1b = sb("w1b", [128, 2, 16, 256], BF16)
        w2b = sb("w2b", [128, 2, 2, 64], BF16)
        w2R = sb("w2R", [128, 2, 64], BF16)
        peb = sb("peb", [128, 2, 16], BF16)
        rC = sb("rC", [64, 512], F32)
        rS = sb("rS", [64, 512], F32)
        kb1 = sb("kb1", [128, 15, 128], BF16)
        wb1 = sb("wb1", [128, 5, 128], BF16)
        kbr = [sb(f"kbr{i}", [128, 15, 4, 128], BF16) for i in range(2)]
        wbr = [sb(f"wbr{i}", [128, 5, 4, 128], BF16) for i in range(2)]
        b1 = sb("b1", [128, 2, 2], F32)
        kcT = sb("kcT", [64, 4, 512], BF16)
        rhsC = sb("rhsC", [128, 4, 4, 193], BF16)
        Acmp = sb("Acmp", [128, SEQ], BF16)
        hid = sb("hid", [128, 2, 512], BF16)
        t64 = [sb(f"t64_{i}", [64, 512], F32) for i in range(2)]
        KTa = [sb(f"KTa{i}", [128, SEQ], BF16) for i in range(2)]
        KwT = [sb(f"KwT{i}", [64, 640], BF16) for i in range(2)]
        Vs = [sb(f"Vs{i}", [128, 64, 65], BF16) for i in range(2)]
        Vw = [sb(f"Vw{i}", [128, 5, 65], BF16) for i in range(2)]
        QTa = [sb(f"QTa{i}", [128, 2, 1024], BF16) for i in range(2)]
        gts = [sb(f"gts{i}", [128, 48], F32) for i in range(2)]
        tA = [sb(f"tA{i}", [128, 128], F32) for i in range(2)]
        tB = [sb(f"tB{i}", [128, 128], F32) for i in range(2)]
        cb1 = [sb(f"cb1_{i}", [128, 128], BF16) for i in range(2)]
        cbr = [sb(f"cbr{i}", [128, 8, 128], BF16) for i in range(2)]
        pT = [sb(f"pT{i}", [128, 512], BF16) for i in range(3)]
        den = sb("den", [128, 8], F32)
        rden = sb("rden", [128, 8], F32)
        fsc = sb("fsc", [128, 8], F32)
        imp = sb("imp", [128, 128], F32)
        wk = sb("wk", [128, 128], F32)
        mx1 = sb("mx1", [128, 8], F32)
        mx2 = sb("mx2", [128, 8], F32)
        selb2 = sb("selb2", [128, 2, 128], BF16)
        otmp = sb("otmp", [128, 8, 64], F32)
        oacc = [sb(f"oacc{i}", [128, 8, 64], F32) for i in range(2)]
        sp = [ps(f"sp{i}", [128, 512], F32) for i in range(3)]
        acc = [ps(f"acc{i}", [128, 512], F32) for i in range(4)]
        tps = ps("tps", [128, 128], BF16)
        cp = ps

        for kv in range(2):
            for j in range(16):
                P.dma("gpsimd", lambda e, kv=kv, j=j: e.dma_start(out=w1b[:, kv, j, :], in_=w1[kv, j * 128:(j + 1) * 128, :]), writes=["w1b"], chan="w1b")
            for hc in range(2):
                P.dma("gpsimd", lambda e, kv=kv, hc=hc: e.dma_start(out=w2b[:, kv, hc, :], in_=w2[kv, hc * 128:(hc + 1) * 128, :]), writes=["w2b"], chan="w2b")
            P.dma("gpsimd", lambda e, kv=kv: e.dma_start(out=peb[:, kv, :], in_=peT[kv, :, :]), writes=["peb"], chan="peb")
        P.dma("sync", lambda e: e.dma_start(out=rC[:], in_=ropeC[:, :]), writes=["rC"], chan="rC")
        P.dma("sync", lambda e: e.dma_start(out=rS[:], in_=ropeS[:, :]), writes=["rS"], chan="rS")
        P.emit("vector", lambda e: e.tensor_scalar(out=w2R[:, :, 0:32], in0=w2b[:, 0, :, 32:64], scalar1=-1.0, scalar2=None, op0=ALU.mult), reads=["w2b"], writes=["w2R"])
        P.emit("vector", lambda e: e.tensor_copy(out=w2R[:, :, 32:64], in_=w2b[:, 0, :, 0:32]), reads=["w2b"], writes=["w2R"])
        P.emit("vector", lambda e: e.memset(kcT[:], 0.0), writes=["kcT"])
        P.emit("gpsimd", lambda e: e.memset(rhsC[:], 0.0), writes=["rhsC"])
        P.emit("vector", lambda e: e.memset(selb2[:], 0.0), writes=["selb2"])
        for i in range(2):
            P.emit("gpsimd", lambda e, i=i: e.memset(Vs[i][:, :, 64:65], 1.0), writes=[f"Vs{i}"])
            P.emit("gpsimd", lambda e, i=i: e.memset(Vw[i][:, :, 64:65], 1.0), writes=[f"Vw{i}"])
            for q4 in range(4):
                P.dma("gpsimd", lambda e, i=i, q4=q4: e.dma_start(out=KTa[i][64:128, q4 * 2048:(q4 + 1) * 2048], in_=EK[:, q4 * 2048:(q4 + 1) * 2048]), writes=[f"KTa{i}"], chan=f"KTa{i}")

        NCMP = 511
        spi = [0]

        def nsp():
            spi[0] += 1
            return spi[0] % 3

        for b in range(2):
            for kvh in range(2):
                combo = b * 2 + kvh
                for kv in range(2):
                    for q4 in range(4):
                        P.dma("gpsimd", lambda e, b=b, kvh=kvh, kv=kv, q4=q4: e.dma_start(out=Acmp[:, q4 * 2048:(q4 + 1) * 2048], in_=cmpA[b, kvh, kv, :, q4 * 2048:(q4 + 1) * 2048]),
                              writes=["Acmp"], chan="Acmp")
                    for hc in range(2):
                        s = nsp()
                        for j in range(16):
                            P.emit("tensor", lambda e, s=s, kv=kv, j=j, hc=hc: e.matmul(sp[s][:, 0:1], lhsT=w1b[:, kv, j, hc * 128:(hc + 1) * 128], rhs=peb[:, kv, j:j + 1], start=(j == 0), stop=(j == 15)),
                                   reads=["w1b", "peb"], writes=[f"sp{s}"])
                        P.emit("vector", lambda e, s=s, kv=kv, hc=hc: e.tensor_copy(out=b1[:, kv, hc:hc + 1], in_=sp[s][:, 0:1]), reads=[f"sp{s}"], writes=["b1"])
                        s = nsp()
                        for j in range(16):
                            P.emit("tensor", lambda e, s=s, kv=kv, j=j, hc=hc: e.matmul(sp[s][:, 0:NCMP], lhsT=w1b[:, kv, j, hc * 128:(hc + 1) * 128], rhs=fview(Acmp, slice(0, 128), 2 * j, 16, NCMP),
                                                                                     start=(j == 0), stop=(j == 15)), reads=["w1b", "Acmp"], writes=[f"sp{s}"])
                        P.emit("scalar", lambda e, s=s, kv=kv, hc=hc: e.activation(out=hid[:, hc, 0:NCMP], in_=sp[s][:, 0:NCMP], func=AF.Silu, bias=b1[:, kv, hc:hc + 1]),
                               reads=[f"sp{s}", "b1"], writes=["hid"])
                    if kv == 0:
                        s0, s1 = nsp(), nsp()
                        for hc in range(2):
                            P.emit("tensor", lambda e, s0=s0, hc=hc: e.matmul(sp[s0][0:64, 0:NCMP], lhsT=w2b[:, 0, hc, :], rhs=hid[:, hc, 0:NCMP], start=(hc == 0), stop=(hc == 1)),
                                   reads=["w2b", "hid"], writes=[f"sp{s0}"])
                        for hc in range(2):
                            P.emit("tensor", lambda e, s1=s1, hc=hc: e.matmul(sp[s1][0:64, 0:NCMP], lhsT=w2R[:, hc, :], rhs=hid[:, hc, 0:NCMP], start=(hc == 0), stop=(hc == 1)),
                                   reads=["w2R", "hid"], writes=[f"sp{s1}"])
                        P.emit("vector", lambda e, s0=s0: e.tensor_tensor(out=t64[0][:, 0:NCMP], in0=sp[s0][0:64, 0:NCMP], in1=rC[:, 0:NCMP], op=ALU.mult), reads=[f"sp{s0}", "rC"], writes=["t64_0"])
                        P.emit("vector", lambda e, s1=s1: e.tensor_tensor(out=t64[1][:, 0:NCMP], in0=sp[s1][0:64, 0:NCMP], in1=rS[:, 0:NCMP], op=ALU.mult), reads=[f"sp{s1}", "rS"], writes=["t64_1"])
                        P.emit("vector", lambda e, combo=combo: e.tensor_tensor(out=kcT[:, combo, 0:NCMP], in0=t64[0][:, 0:NCMP], in1=t64[1][:, 0:NCMP], op=ALU.add), reads=["t64_0", "t64_1"], writes=["kcT"])
                    else:
                        for ct in range(4):
                            nb = min(128, NCMP - ct * 128)
                            s = nsp()
                            for hc in range(2):
                                P.emit("tensor", lambda e, s=s, hc=hc, ct=ct, nb=nb: e.matmul(sp[s][0:nb, 0:64], lhsT=hid[:, hc, ct * 128:ct * 128 + nb], rhs=w2b[:, 1, hc, :], start=(hc == 0), stop=(hc == 1)),
                                       reads=["w2b", "hid"], writes=[f"sp{s}"])
                            P.emit("vector", lambda e, s=s, ct=ct, nb=nb, combo=combo: e.tensor_copy(out=rhsC[0:nb, combo, ct, 0:64], in_=sp[s][0:nb, 0:64]), reads=[f"sp{s}"], writes=["rhsC"])
                            P.emit("gpsimd", lambda e, ct=ct, nb=nb, combo=combo: e.memset(rhsC[0:nb, combo, ct, 64:65], 1.0), reads=[], writes=["rhsC"])
                            P.dma("gpsimd", lambda e, ct=ct, combo=combo: e.dma_start(out=rhsC[:, combo, ct, 65:193], in_=ovl[ct, :, :]), writes=["rhsC"], chan="rhsC")

        pti = [0]

        def npt():
            pti[0] += 1
            return pti[0] % 3

        def score_unit(lhsT, rhs, kparts, bias, kbuf, qbuf, biasname):
            s = nsp()
            P.emit("tensor", lambda e: e.matmul(sp[s][:, :], lhsT=lhsT, rhs=rhs, start=True, stop=(bias is None)), reads=[kbuf, qbuf], writes=[f"sp{s}"])
            if bias is not None:
                P.emit("tensor", lambda e: e.matmul(sp[s][:, :], lhsT=ident[:], rhs=bias, start=False, stop=True), reads=["ident", biasname], writes=[f"sp{s}"])
            p = npt()
            P.emit("scalar", lambda e: e.activation(out=pT[p][:], in_=sp[s][:], func=AF.Exp, scale=0.125), reads=[f"sp{s}"], writes=[f"pT{p}"])
            return p

        it = 0
        for b in range(2):
            for kvh in range(2):
                combo = b * 2 + kvh
                cb = combo % 2
                for q4 in range(4):
                    cs_ = slice(q4 * 2048, (q4 + 1) * 2048)
                    P.dma("gpsimd", lambda e, cb=cb, b=b, kvh=kvh, cs_=cs_: e.dma_start(out=KTa[cb][0:64, cs_], in_=selKT[b, kvh, :, cs_]), writes=[f"KTa{cb}"], chan=f"KTa{cb}")
                    ks = slice(q4 * 16, (q4 + 1) * 16)
                    P.dma("gpsimd", lambda e, cb=cb, b=b, kvh=kvh, ks=ks, q4=q4: e.dma_start(out=Vs[cb][:, ks, 0:64], in_=selV[b, kvh, q4 * 2048:(q4 + 1) * 2048, :].rearrange("(k p) d -> p k d", p=128)),
                          writes=[f"Vs{cb}"], chan=f"Vs{cb}")
                for ti in range(NPT):
                    tb_, qmin, qmax = slot_range(ti)
                    if tb_ != b:
                        continue
                    ib = it % 2
                    it += 1
                    r0 = ti * 128
                    for h in range(2):
                        P.dma("gpsimd", lambda e, ib=ib, ti=ti, kvh=kvh, h=h: e.dma_start(out=QTa[ib][0:64, h, :], in_=qT[ti, kvh, :, :]), writes=[f"QTa{ib}"], chan=f"QTa{ib}")
                    P.dma("sync", lambda e, ib=ib, r0=r0: e.dma_start(out=gts[ib][:], in_=gates[r0:r0 + 128, :]), writes=[f"gts{ib}"], chan=f"gts{ib}")
                    P.dma("sync", lambda e, ib=ib, ti=ti: e.dma_start(out=tA[ib][:], in_=topA[ti, :, :]), writes=[f"tA{ib}"], chan=f"tA{ib}")
                    P.dma("sync", lambda e, ib=ib, ti=ti: e.dma_start(out=tB[ib][:], in_=topB[ti, :, :]), writes=[f"tB{ib}"], chan=f"tB{ib}")
                    P.dma("gpsimd", lambda e, ti=ti: e.dma_start(out=kb1[:], in_=kbias[ti, :, :, :]), writes=["kb1"], chan="kb1")
                    P.emit("gpsimd", lambda e, ib=ib: e.tensor_copy(out=kbr[ib][:], in_=bc(kb1[:], [128, 15, 4, 128], 2)), reads=["kb1"], writes=[f"kbr{ib}"])
                    P.dma("gpsimd", lambda e, ti=ti: e.dma_start(out=wb1[:], in_=wbias[ti, :, :, :]), writes=["wb1"], chan="wb1")
                    P.emit("gpsimd", lambda e, ib=ib: e.tensor_copy(out=wbr[ib][:], in_=bc(wb1[:], [128, 5, 4, 128], 2)), reads=["wb1"], writes=[f"wbr{ib}"])
                    P.dma("gpsimd", lambda e, ib=ib, ti=ti, kvh=kvh: e.dma_start(out=KwT[ib][:, :], in_=winKT[ti, kvh, :, :]), writes=[f"KwT{ib}"], chan=f"KwT{ib}")
                    P.dma("gpsimd", lambda e, ib=ib, ti=ti, kvh=kvh: e.dma_start(out=Vw[ib][:, :, 0:64], in_=winV[ti, kvh, :, :].rearrange("(k p) d -> p k d", p=128)),
                          writes=[f"Vw{ib}"], chan=f"Vw{ib}")
                    gv = gts[ib][:, kvh * 24:(kvh + 1) * 24].rearrange("p (g c) -> p g c", c=3)
                    nct = 4
                    pend = []

                    def pv_cmp(p, ct, hf, combo=combo, nct=nct):
                        for g4 in range(4):
                            g = hf * 4 + g4
                            P.emit("tensor", lambda e, p=p, g=g, g4=g4, ct=ct: e.matmul(
                                acc[g // 2][:, (g % 2) * 193:(g % 2) * 193 + 193], lhsT=pT[p][:, g4 * 128:(g4 + 1) * 128], rhs=rhsC[:, combo, ct, :],
                                start=(ct == 0 and g % 2 == 0), stop=(ct == nct - 1)), reads=[f"pT{p}", "rhsC"], writes=[f"acc{g // 2}"])

                    for ct in range(nct):
                        cbi = ct % 2
                        P.dma("gpsimd", lambda e, cbi=cbi, ti=ti, ct=ct: e.dma_start(out=cb1[cbi][:], in_=cmpbias[ti, ct, :, :]), writes=[f"cb1_{cbi}"], chan=f"cb1_{cbi}")
                        P.emit("gpsimd", lambda e, cbi=cbi: e.tensor_copy(out=cbr[cbi][:], in_=bc(cb1[cbi][:], [128, 8, 128], 1)), reads=[f"cb1_{cbi}"], writes=[f"cbr{cbi}"])
                        for hf in range(2):
                            p = score_unit(kcT[:, combo, ct * 128:(ct + 1) * 128], QTa[ib][0:64, 0, hf * 512:(hf + 1) * 512], 64,
                                           cbr[cbi][:, hf * 4:(hf + 1) * 4, :].rearrange("p g t -> p (g t)"), "kcT", f"QTa{ib}", f"cbr{cbi}")
                            pend.append((p, ct, hf))
                            if len(pend) > PIPE:
                                pv_cmp(*pend.pop(0))
                    while pend:
                        pv_cmp(*pend.pop(0))
                    for a in range(4):
                        P.emit("vector", lambda e, a=a: e.tensor_copy(out=den[:, 2 * a:2 * a + 2], in_=fview(acc[a], slice(0, 128), 64, 193, 2)), reads=[f"acc{a}"], writes=["den"])
                    P.emit("vector", lambda e: e.tensor_scalar(out=rden[:], in0=den[:], scalar1=1e-30, scalar2=None, op0=ALU.max), reads=["den"], writes=["rden"])
                    P.emit("vector", lambda e: e.reciprocal(out=rden[:], in_=rden[:]), reads=["rden"], writes=["rden"])
                    for g in range(8):
                        U = acc[g // 2][:, (g % 2) * 193 + 65:(g % 2) * 193 + 193]
                        if g == 0:
                            P.emit("vector", lambda e, U=U: e.tensor_scalar(out=imp[:], in0=U, scalar1=rden[:, 0:1], scalar2=None, op0=ALU.mult), reads=["acc0", "rden"], writes=["imp"])
                        else:
                            P.emit("vector", lambda e, U=U, g=g: e.scalar_tensor_tensor(out=imp[:], in0=U, scalar=rden[:, g:g + 1], in1=imp[:], op0=ALU.mult, op1=ALU.add),
                                   reads=[f"acc{g // 2}", "rden", "imp"], writes=["imp"])
                    ob_ = it % 2
                    P.emit("vector", lambda e, gv=gv: e.tensor_tensor(out=fsc[:], in0=rden[:], in1=gv[:, :, 0], op=ALU.mult), reads=["rden", f"gts{ib}"], writes=["fsc"])
                    for a in range(4):
                        ov = acc[a][:, 0:386].rearrange("p (g c) -> p g c", c=193)[:, :, 0:64]
                        P.emit("vector", lambda e, a=a, ov=ov, ob_=ob_: e.tensor_tensor(out=oacc[ob_][:, 2 * a:2 * a + 2, :], in0=ov, in1=bc(fsc[:, 2 * a:2 * a + 2], [128, 2, 64], 2), op=ALU.mult),
                               reads=[f"acc{a}", "fsc"], writes=[f"oacc{ob_}"])
                    pend = []

                    def pv_win(p, kt, hf, ib=ib):
                        for g4 in range(4):
                            P.emit("tensor", lambda e, p=p, g4=g4, hf=hf, kt=kt: e.matmul(acc[2 + hf][:, g4 * 65:(g4 + 1) * 65], lhsT=pT[p][:, g4 * 128:(g4 + 1) * 128], rhs=Vw[ib][:, kt, :],
                                                                                    start=(kt == 0 and g4 == 0), stop=(kt == 4)), reads=[f"pT{p}", f"Vw{ib}"], writes=[f"acc{2 + hf}"])

                    for kt in range(5):
                        for hf in range(2):
                            bias = wbr[ib][:, kt, :, :].rearrange("p g t -> p (g t)")
                            p = score_unit(KwT[ib][:, kt * 128:(kt + 1) * 128], QTa[ib][0:64, 0, hf * 512:(hf + 1) * 512], 64, bias, f"KwT{ib}", f"QTa{ib}", f"wbr{ib}")
                            pend.append((p, kt, hf))
                            if len(pend) > PIPE:
                                pv_win(*pend.pop(0))
                    while pend:
                        pv_win(*pend.pop(0))
                    P.emit("vector", lambda e, ib=ib: e.tensor_tensor(out=imp[:], in0=imp[:], in1=tA[ib][:], op=ALU.mult), reads=["imp", f"tA{ib}"], writes=["imp"])
                    P.emit("vector", lambda e, ib=ib: e.tensor_tensor(out=imp[:], in0=imp[:], in1=tB[ib][:], op=ALU.add), reads=["imp", f"tB{ib}"], writes=["imp"])
                    P.emit("vector", lambda e: e.max(out=mx1[:], in_=imp[:]), reads=["imp"], writes=["mx1"])
                    P.emit("vector", lambda e: e.match_replace(out=wk[:], in_to_replace=mx1[:], in_values=imp[:], imm_value=-1e30), reads=["imp", "mx1"], writes=["wk"])
                    P.emit("vector", lambda e: e.max(out=mx2[:], in_=wk[:]), reads=["wk"], writes=["mx2"])
                    P.emit("vector", lambda e: e.tensor_scalar(out=selb2[:, :, 64:128], in0=imp[:].rearrange("p (h j) -> p h j", h=2), scalar1=mx2[:, 7:8], scalar2=NEGB,
                                                               op0=ALU.is_lt, op1=ALU.mult), reads=["imp", "mx2"], writes=["selb2"])
                    for h in range(2):
                        P.emit("tensor", lambda e, h=h: e.transpose(out=tps[:], in_=selb2[:, h, :], identity=ident[:]), reads=["selb2", "ident"], writes=["tps"])
                        P.emit("scalar", lambda e, h=h, ib=ib: e.copy(out=QTa[ib][64:128, h, :].rearrange("p (g t) -> p g t", g=8), in_=bc(tps[64:128, :], [64, 8, 128], 1)),
                               reads=["tps"], writes=[f"QTa{ib}"])
                    pend = []

                    def pv_sel(p, kt, hf, qmax=qmax, cb=cb):
                        for g4 in range(4):
                            P.emit("tensor", lambda e, p=p, g4=g4, hf=hf, kt=kt: e.matmul(acc[hf][:, g4 * 65:(g4 + 1) * 65], lhsT=pT[p][:, g4 * 128:(g4 + 1) * 128], rhs=Vs[cb][:, kt, :],
                                                                                    start=(kt == 0 and g4 == 0), stop=(kt == qmax)), reads=[f"pT{p}", f"Vs{cb}"], writes=[f"acc{hf}"])

                    for kt in range(qmax + 1):
                        half = kt // 32
                        for hf in range(2):
                            bias = kbr[ib][:, kt - qmin, :, :].rearrange("p g t -> p (g t)") if kt >= qmin else None
                            p = score_unit(KTa[cb][:, kt * 128:(kt + 1) * 128], QTa[ib][:, half, hf * 512:(hf + 1) * 512], 128, bias, f"KTa{cb}", f"QTa{ib}", f"kbr{ib}")
                            pend.append((p, kt, hf))
                            if len(pend) > PIPE:
                                pv_sel(*pend.pop(0))
                    while pend:
                        pv_sel(*pend.pop(0))
                    for br, a0 in ((1, 0), (2, 2)):
                        for hf in range(2):
                            a = a0 + hf
                            P.emit("vector", lambda e, a=a, hf=hf: e.tensor_copy(out=den[:, hf * 4:(hf + 1) * 4], in_=fview(acc[a], slice(0, 128), 64, 65, 4)), reads=[f"acc{a}"], writes=["den"])
                        P.emit("vector", lambda e: e.tensor_scalar(out=rden[:], in0=den[:], scalar1=1e-30, scalar2=None, op0=ALU.max), reads=["den"], writes=["rden"])
                        P.emit("vector", lambda e: e.reciprocal(out=rden[:], in_=rden[:]), reads=["rden"], writes=["rden"])
                        P.emit("vector", lambda e, gv=gv, br=br: e.tensor_tensor(out=fsc[:], in0=rden[:], in1=gv[:, :, br], op=ALU.mult), reads=["rden", f"gts{ib}"], writes=["fsc"])
                        for hf in range(2):
                            a = a0 + hf
                            ov = acc[a][:, 0:260].rearrange("p (g c) -> p g c", c=65)[:, :, 0:64]
                            P.emit("vector", lambda e, a=a, hf=hf, ov=ov: e.tensor_tensor(out=otmp[:, hf * 4:(hf + 1) * 4, :], in0=ov, in1=bc(fsc[:, hf * 4:(hf + 1) * 4], [128, 4, 64], 2), op=ALU.mult),
                                   reads=[f"acc{a}", "fsc"], writes=["otmp"])
                        P.emit("gpsimd", lambda e, ob_=ob_: e.tensor_tensor(out=oacc[ob_][:], in0=oacc[ob_][:], in1=otmp[:], op=ALU.add), reads=[f"oacc{ob_}", "otmp"], writes=[f"oacc{ob_}"])
                    P.dma("sync", lambda e, ob_=ob_, r0=r0, kvh=kvh: e.dma_start(out=o_out[r0:r0 + 128, kvh * 512:(kvh + 1) * 512], in_=oacc[ob_][:].rearrange("p g d -> p (g d)")),
                          reads=[f"oacc{ob_}"], chan=f"oacc{ob_}")
        P.replay(nc, es)
    return nc


def nsa_consts(core):
    tiles = core_tiles(core)
    tok = np.arange(128)
    j = np.arange(128)
    tA = np.zeros((NPT, 128, 128), np.float32)
    tB = np.zeros((NPT, 128, 128), np.float32)
    cbias = np.full((NPT, 4, 128, 128), NEGB, np.float32)
    for ti, (_, p0) in enumerate(tiles):
        qp = (p0 + tok)[:, None]
        cur = qp // 64
        forced = (j[None, :] == 0) | (j[None, :] == cur) | (j[None, :] == cur - 1)
        causal = j[None, :] * 64 <= qp
        tA[ti] = (causal & ~forced).astype(np.float32)
        tB[ti] = np.where(forced, 1e6, np.where(causal, 0.0, -1.0)).astype(np.float32)
        for ct in range(4):
            i = ct * 128 + np.arange(128)
            ok = (i[:, None] * 16 + 31 <= (p0 + tok)[None, :]) & (i[:, None] < 511)
            cbias[ti, ct] = np.where(ok, 0.0, NEGB)
    i = np.arange(512)
    ov = ((i[:, None] * 16 < (j[None, :] + 1) * 64) & (i[:, None] * 16 + 32 > j[None, :] * 64) & (i[:, None] < 511)).astype(np.float32)
    key = np.arange(SEQ)
    EK = ((key[None, :] // 64) % 64 == np.arange(64)[:, None]).astype(np.float32)
    inv = (THETA ** (-np.arange(32, dtype=np.float32) / 32)).astype(np.float32)
    ang = (np.arange(512) * 16).astype(np.float32)[None, :] * np.tile(inv, 2)[:, None]
    kk = np.arange(128)
    kb = np.zeros((NPT, 128, 15, 128), np.float32)
    wb = np.zeros((NPT, 128, 5, 128), np.float32)
    for ti, (_, p0) in enumerate(tiles):
        _, qmin, qmax = slot_range(ti)
        qp = (p0 + tok)[None, :]
        for jj in range(15):
            kp = ((qmin + jj) * 128 + kk)[:, None]
            kb[ti, :, jj, :] = np.where(kp <= qp, 0.0, NEGB)
        for r in range(5):
            kp = (p0 - 512 + r * 128 + kk)[:, None]
            wb[ti, :, r, :] = np.where((kp <= qp) & (kp > qp - WIN) & (kp >= 0), 0.0, NEGB)
    return {"topA": tA, "topB": tB, "cmpbias": cbias, "ovl": np.ascontiguousarray(ov.reshape(4, 128, 128)), "EK": EK,
            "ropeC": np.cos(ang).astype(np.float32), "ropeS": np.sin(ang).astype(np.float32), "kbias": kb, "wbias": wb}


def run_nsa_prompt(zp, w1, w2, pe):
    T = lambda a: np.ascontiguousarray(a)
    kvp = zp[:, :, 1024:1792].reshape(2, SEQ, 3, 2, 2, 64)
    selKT = T(kvp[:, :, 1, 0].transpose(0, 2, 3, 1))
    selV = T(kvp[:, :, 1, 1].transpose(0, 2, 1, 3))
    wpad = np.concatenate([np.zeros((2, WIN, 2, 2, 64), np.float32), kvp[:, :, 2]], axis=1)
    ck = kvp[:, :, 0].transpose(0, 3, 2, 4, 1)
    cmpA = np.zeros((2, 2, 2, 128, SEQ), np.float32)
    cmpA[:, :, :, 0:64, :] = ck
    cmpA[:, :, :, 64:128, :SEQ - 1] = ck[..., 1:]
    peT = T(pe.reshape(2, 16, 2, 64).transpose(0, 2, 3, 1).reshape(2, 128, 16))
    in_maps, ncs = [], []
    for c in range(NCORES):
        tiles = core_tiles(c)
        qT = np.stack([zp[b, p0:p0 + 128, 0:1024].reshape(128, 2, 8, 64).transpose(1, 3, 2, 0).reshape(2, 64, 1024) for b, p0 in tiles])
        gates = np.zeros((ROWS, 48), np.float32)
        for ti, (b, p0) in enumerate(tiles):
            gates[ti * 128:(ti + 1) * 128] = zp[b, p0:p0 + 128, 1792:1840]
        winKT = T(np.stack([wpad[b, p0:p0 + 640, 0].transpose(1, 2, 0) for b, p0 in tiles]))
        winV = T(np.stack([wpad[b, p0:p0 + 640, 1].transpose(1, 0, 2) for b, p0 in tiles]))
        m = {"qT": T(qT), "gates": gates, "selKT": selKT, "selV": selV, "winKT": winKT, "winV": winV, "cmpA": cmpA,
             "w1": T(w1), "w2": T(w2), "peT": peT}
        m.update(nsa_consts(c))
        in_maps.append(m)
    if "nsa" not in _NC_CACHE:
        _NC_CACHE["nsa"] = build_nsa()
    res = run_bass_kernel_spmd(_NC_CACHE["nsa"], in_maps, core_ids=list(range(NCORES))).results
    op = np.zeros((2, SEQ, D), np.float32)
    for c in range(NCORES):
        for ti, (b_, p0) in enumerate(core_tiles(c)):
            op[b_, p0:p0 + 128] = res[c]["o_out"][ti * 128:(ti + 1) * 128]
    return op


NPOOL = 2560
NPG = PAST // 128


def build_nsa_s():
    nc = bass.Bass("TRN2", target_bir_lowering=False)
    dt_in = lambda n, s, d=F32: nc.dram_tensor(n, s, d, kind="ExternalInput").ap()
    pool_cmp = dt_in("pool_cmp", [NPOOL, 128, 128])
    pool_sel = dt_in("pool_sel", [NPOOL, 128, 128])
    ptl = dt_in("ptl", [128, 8 * 8], I32)
    pmod = dt_in("pmod", [128, 1], I32)
    qT = dt_in("qT", [8, 64, 64])
    gates = dt_in("gates", [8, 8, 24])
    knewT = dt_in("knewT", [8, 64, 8])
    vnew = dt_in("vnew", [8, 8, 64])
    wKT = dt_in("wKT", [8, 64, 520])
    wV = dt_in("wV", [8, 520, 64])
    wrows = dt_in("wrows", [8, 520, 128])
    EK = dt_in("EK", [64, SEQ])
    topA = dt_in("topA", [8, 128])
    topB = dt_in("topB", [8, 128])
    ovl = dt_in("ovl", [4, 128, 128])
    ropeC = dt_in("ropeC", [64, 512])
    ropeS = dt_in("ropeS", [64, 512])
    causb = dt_in("causb", [8, 64])
    wbias = dt_in("wbias", [128, 5, 64])
    w1s = dt_in("w1s", [128, 32, 256])
    w2 = dt_in("w2", [2, 256, 64])
    pes = dt_in("pes", [128, 32])
    o_out = nc.dram_tensor("o_out", [64, 512], F32, kind="ExternalOutput").ap()
    wins_out = nc.dram_tensor("wins_out", [8, WIN, 128], F32, kind="ExternalOutput").ap()
    P = Prog()
    with ExitStack() as es:
        sb = lambda n, s, d: es.enter_context(nc.sbuf_tensor(n, s, d))
        ps = lambda n, s, d: es.enter_context(nc.psum_tensor(n, s, d))
        ident = make_ident(P, nc, es)
        ptb = sb("ptb", [128, 8 * 8], I32)
        pidx = sb("pidx", [128, 8 * 8], I32)
        piota = sb("piota", [128, 1], I32)
        w1b = sb("w1b", [128, 32, 256], BF16)
        w2b = sb("w2b", [128, 2, 2, 64], BF16)
        w2R = sb("w2R", [128, 2, 64], BF16)
        peb = sb("peb", [128, 32], BF16)
        rC = sb("rC", [64, 512], F32)
        rS = sb("rS", [64, 512], F32)
        tA = sb("tA", [8, 128], F32)
        tB = sb("tB", [8, 128], F32)
        cbs = sb("cbs", [8, 64], BF16)
        wbs = sb("wbs", [128, 5, 64], BF16)
        b1 = sb("b1", [128, 2, 2], F32)
        kcT = sb("kcT", [64, 512], BF16)
        rhsC = sb("rhsC", [128, 4, 193], BF16)
        cTok = [sb(f"cTok{i}", [128, NPG, 128], BF16) for i in range(2)]
        sTok = [sb(f"sTok{i}", [128, NPG, 128], BF16) for i in range(2)]
        AT = sb("AT", [128, SEQ], BF16)
        KTa = sb("KTa", [128, SEQ], BF16)
        Vs = sb("Vs", [128, NPG, 65], BF16)
        hid = sb("hid", [128, 2, 512], BF16)
        t64 = [sb(f"t64_{i}", [64, 512], F32) for i in range(2)]
        QTa = sb("QTa", [128, 2, 64], BF16)
        gts = sb("gts", [8, 24], F32)
        knT = sb("knT", [64, 8], BF16)
        vnA = sb("vnA", [8, 65], BF16)
        KwT = sb("KwT", [64, 640], BF16)
        Vw = sb("Vw", [128, 5, 65], BF16)
        pT = [sb(f"pT{i}", [128, 256], BF16) for i in range(3)]
        den = sb("den", [8, 8], F32)
        rden = sb("rden", [8, 8], F32)
        fsc = sb("fsc", [8, 8], F32)
        imp = sb("imp", [8, 128], F32)
        wk = sb("wk", [8, 128], F32)
        mx1 = sb("mx1", [8, 8], F32)
        mx2 = sb("mx2", [8, 8], F32)
        selb2 = sb("selb2", [8, 2, 128], BF16)
        otmp = sb("otmp", [8, 8, 64], F32)
        oacc = [sb(f"oacc{i}", [8, 8, 64], F32) for i in range(2)]
        tpp = [ps(f"tpp{i}", [128, 4, 128], BF16) for i in range(2)]
        sp = [ps(f"sp{i}", [128, 512], F32) for i in range(2)]
        acc = [ps(f"acc{i}", [128, 512], F32) for i in range(4)]

        P.dma("gpsimd", lambda e: e.dma_start(out=ptb[:], in_=ptl[:, :]), writes=["ptb"], chan="ptb")
        P.dma("gpsimd", lambda e: e.dma_start(out=piota[:], in_=pmod[:, :]), writes=["piota"], chan="piota")
        P.emit("gpsimd", lambda e: e.tensor_scalar(out=pidx[:], in0=ptb[:], scalar1=16, scalar2=None, op0=ALU.mult), reads=["ptb"], writes=["pt"])
        P.emit("gpsimd", lambda e: e.tensor_tensor(out=pidx[:], in0=pidx[:], in1=piota[:].to_broadcast([128, 8 * 8]), op=ALU.add), reads=["pt", "piota"], writes=["pt"])
        rows_cmp = pool_cmp.rearrange("n (g r) c -> (n g) (r c)", r=8)
        rows_sel = pool_sel.rearrange("n (g r) c -> (n g) (r c)", r=8)
        for r4 in range(4):
            P.dma("gpsimd", lambda e, r4=r4: e.dma_start(out=w1b[:, r4 * 8:(r4 + 1) * 8, :], in_=w1s[:, r4 * 8:(r4 + 1) * 8, :]), writes=["w1b"], chan="w1b")
        for kv in range(2):
            for hc in range(2):
                P.dma("gpsimd", lambda e, kv=kv, hc=hc: e.dma_start(out=w2b[:, kv, hc, :], in_=w2[kv, hc * 128:(hc + 1) * 128, :]), writes=["w2b"], chan="w2b")
        P.dma("gpsimd", lambda e: e.dma_start(out=peb[:], in_=pes[:, :]), writes=["peb"], chan="peb")
        P.dma("gpsimd", lambda e: e.dma_start(out=cbs[:], in_=causb[:, :]), writes=["cbs"], chan="cbs")
        P.dma("gpsimd", lambda e: e.dma_start(out=wbs[:], in_=wbias[:, :, :]), writes=["wbs"], chan="wbs")
        P.dma("sync", lambda e: e.dma_start(out=rC[:], in_=ropeC[:, :]), writes=["rC"], chan="rC")
        P.dma("sync", lambda e: e.dma_start(out=rS[:], in_=ropeS[:, :]), writes=["rS"], chan="rS")
        P.dma("sync", lambda e: e.dma_start(out=tA[:], in_=topA[:, :]), writes=["tA"], chan="tA")
        P.dma("sync", lambda e: e.dma_start(out=tB[:], in_=topB[:, :]), writes=["tB"], chan="tB")
        P.emit("vector", lambda e: e.tensor_scalar(out=w2R[:, :, 0:32], in0=w2b[:, 0, :, 32:64], scalar1=-1.0, scalar2=None, op0=ALU.mult), reads=["w2b"], writes=["w2R"])
        P.emit("vector", lambda e: e.tensor_copy(out=w2R[:, :, 32:64], in_=w2b[:, 0, :, 0:32]), reads=["w2b"], writes=["w2R"])
        P.emit("vector", lambda e: e.memset(kcT[:], 0.0), writes=["kcT"])
        P.emit("gpsimd", lambda e: e.memset(rhsC[:], 0.0), writes=["rhsC"])
        P.emit("vector", lambda e: e.memset(selb2[:], 0.0), writes=["selb2"])
        P.emit("gpsimd", lambda e: e.memset(Vs[:, :, 64:65], 1.0), writes=["Vs"])
        P.emit("gpsimd", lambda e: e.memset(Vw[:, :, 64:65], 1.0), writes=["Vw"])
        P.emit("gpsimd", lambda e: e.memset(vnA[:, 64:65], 1.0), writes=["vnA"])
        for ct in range(4):
            P.dma("gpsimd", lambda e, ct=ct: e.dma_start(out=rhsC[:, ct, 65:193], in_=ovl[ct, :, :]), writes=["rhsC"], chan="rhsC")
        for q4 in range(4):
            P.dma("gpsimd", lambda e, q4=q4: e.dma_start(out=KTa[64:128, q4 * 2048:(q4 + 1) * 2048], in_=EK[:, q4 * 2048:(q4 + 1) * 2048]), writes=["KTa"], chan="KTa")
        NCMP = 511
        cnt = {"sp": 0, "pt": 0, "tp": 0, "cp": 0}

        def nxt(k, n):
            cnt[k] += 1
            return cnt[k] % n

        def gather(j, bi):
            for d in range(8):
                for (prow, dst, nm) in ((rows_cmp, cTok, "cTok"), (rows_sel, sTok, "sTok")):
                    def g(e, j=j, d=d, prow=prow, dst=dst, bi=bi):
                        return e.indirect_dma_start(out=dst[bi][:, d * 8:(d + 1) * 8, :].rearrange("p r c -> p (r c)"), out_offset=None, in_=prow[:, :],
                                                    in_offset=bass.IndirectOffsetOnAxis(ap=pidx[:, j * 8 + d:j * 8 + d + 1], axis=0))
                    P.dma("gpsimd", g, reads=["pt"], writes=[f"{nm}{bi}"], chan=f"{nm}{bi}")

        gather(0, 0)
        for j in range(8):
            bi = j % 2
            if j + 1 < 8:
                gather(j + 1, 1 - bi)
            for (src_, dstT, nm, rows) in ((cTok, AT, "AT", slice(0, 128)), (sTok, KTa, "KTa", slice(0, 64))):
                for p4 in range(NPG // 4):
                    tb = nxt("tp", 2)
                    for q in range(4):
                        P.emit("tensor", lambda e, tb=tb, q=q, p4=p4, src_=src_, bi=bi: e.transpose(out=tpp[tb][:, q, :], in_=src_[bi][:, p4 * 4 + q, :], identity=ident[:]),
                               reads=[f"{'cTok' if nm == 'AT' else 'sTok'}{bi}", "ident"], writes=[f"tpp{tb}"])
                    eng = "scalar" if nxt("cp", 2) else "vector"
                    if nm == "AT":
                        d_, r0 = (p4 * 4) // 8, (p4 * 4) % 8
                        a_ = dstT[rows, d_ * 1024 + r0:d_ * 1024 + r0 + 1]
                        o_ap = bass.AP(a_.tensor, a_.offset, [list(a_.ap[0]), [1, 4], [128, 8], [8, 16]])
                        i_ap = tpp[tb][rows, :, :].rearrange("p q (s g) -> p q s g", s=8)
                    else:
                        o_ap = dstT[rows, p4 * 512:(p4 + 1) * 512]
                        i_ap = tpp[tb][rows, :, :].rearrange("p q t -> p (q t)")
                    if eng == "scalar":
                        P.emit("scalar", lambda e, o_ap=o_ap, i_ap=i_ap: e.copy(out=o_ap, in_=i_ap), reads=[f"tpp{tb}"], writes=[nm])
                    else:
                        P.emit("vector", lambda e, o_ap=o_ap, i_ap=i_ap: e.tensor_copy(out=o_ap, in_=i_ap), reads=[f"tpp{tb}"], writes=[nm])
            P.emit("gpsimd", lambda e, bi=bi: e.tensor_copy(out=Vs[:, :, 0:64], in_=sTok[bi][:, :, 64:128]), reads=[f"sTok{bi}"], writes=["Vs"])
            for kv in range(2):
                pr = slice(kv * 64, (kv + 1) * 64)
                for hc in range(2):
                    s = nxt("sp", 2)
                    for r in range(32):
                        P.emit("tensor", lambda e, s=s, r=r, hc=hc, pr=pr: e.matmul(sp[s][:, 0:1], lhsT=w1b[pr, r, hc * 128:(hc + 1) * 128], rhs=peb[pr, r:r + 1], start=(r == 0), stop=(r == 31)),
                               reads=["w1b", "peb"], writes=[f"sp{s}"])
                    P.emit("vector", lambda e, s=s, kv=kv, hc=hc: e.tensor_copy(out=b1[:, kv, hc:hc + 1], in_=sp[s][:, 0:1]), reads=[f"sp{s}"], writes=["b1"])
                    s = nxt("sp", 2)
                    for r in range(32):
                        P.emit("tensor", lambda e, s=s, r=r, hc=hc, pr=pr: e.matmul(sp[s][:, 0:NCMP], lhsT=w1b[pr, r, hc * 128:(hc + 1) * 128], rhs=fview(AT, pr, r, 16, NCMP), start=(r == 0), stop=(r == 31)),
                               reads=["w1b", "AT"], writes=[f"sp{s}"])
                    P.emit("scalar", lambda e, s=s, kv=kv, hc=hc: e.activation(out=hid[:, hc, 0:NCMP], in_=sp[s][:, 0:NCMP], func=AF.Silu, bias=b1[:, kv, hc:hc + 1]), reads=[f"sp{s}", "b1"], writes=["hid"])
                if kv == 0:
                    for hc in range(2):
                        P.emit("tensor", lambda e, hc=hc: e.matmul(sp[0][0:64, 0:NCMP], lhsT=w2b[:, 0, hc, :], rhs=hid[:, hc, 0:NCMP], start=(hc == 0), stop=(hc == 1)), reads=["w2b", "hid"], writes=["sp0"])
                    for hc in range(2):
                        P.emit("tensor", lambda e, hc=hc: e.matmul(sp[1][0:64, 0:NCMP], lhsT=w2R[:, hc, :], rhs=hid[:, hc, 0:NCMP], start=(hc == 0), stop=(hc == 1)), reads=["w2R", "hid"], writes=["sp1"])
                    P.emit("vector", lambda e: e.tensor_tensor(out=t64[0][:, 0:NCMP], in0=sp[0][0:64, 0:NCMP], in1=rC[:, 0:NCMP], op=ALU.mult), reads=["sp0", "rC"], writes=["t64_0"])
                    P.emit("vector", lambda e: e.tensor_tensor(out=t64[1][:, 0:NCMP], in0=sp[1][0:64, 0:NCMP], in1=rS[:, 0:NCMP], op=ALU.mult), reads=["sp1", "rS"], writes=["t64_1"])
                    P.emit("vector", lambda e: e.tensor_tensor(out=kcT[:, 0:NCMP], in0=t64[0][:, 0:NCMP], in1=t64[1][:, 0:NCMP], op=ALU.add), reads=["t64_0", "t64_1"], writes=["kcT"])
                else:
                    for ct in range(4):
                        nb = min(128, NCMP - ct * 128)
                        s = nxt("sp", 2)
                        for hc in range(2):
                            P.emit("tensor", lambda e, s=s, hc=hc, ct=ct, nb=nb: e.matmul(sp[s][0:nb, 0:64], lhsT=hid[:, hc, ct * 128:ct * 128 + nb], rhs=w2b[:, 1, hc, :], start=(hc == 0), stop=(hc == 1)),
                                   reads=["w2b", "hid"], writes=[f"sp{s}"])
                        P.emit("vector", lambda e, s=s, ct=ct, nb=nb: e.tensor_copy(out=rhsC[0:nb, ct, 0:64], in_=sp[s][0:nb, 0:64]), reads=[f"sp{s}"], writes=["rhsC"])
                        P.emit("gpsimd", lambda e, ct=ct, nb=nb: e.memset(rhsC[0:nb, ct, 64:65], 1.0), writes=["rhsC"])
            for h in range(2):
                P.dma("gpsimd", lambda e, j=j, h=h: e.dma_start(out=QTa[0:64, h, :], in_=qT[j, :, :]), writes=["QTa"], chan="QTa")
            P.dma("sync", lambda e, j=j: e.dma_start(out=gts[:], in_=gates[j, :, :]), writes=["gts"], chan="gts")
            P.dma("gpsimd", lambda e, j=j: e.dma_start(out=knT[:], in_=knewT[j, :, :]), writes=["knT"], chan="knT")
            P.dma("gpsimd", lambda e, j=j: e.dma_start(out=vnA[:, 0:64], in_=vnew[j, :, :]), writes=["vnA"], chan="vnA")
            P.emit("gpsimd", lambda e: e.memset(KwT[:], 0.0), writes=["KwT"])
            P.emit("gpsimd", lambda e: e.memset(Vw[:, 4, 0:64], 0.0), writes=["Vw"])
            P.dma("gpsimd", lambda e, j=j: e.dma_start(out=KwT[:, 0:520], in_=wKT[j, :, :]), writes=["KwT"], chan="KwT")
            P.dma("gpsimd", lambda e, j=j: e.dma_start(out=Vw[:, 0:4, 0:64], in_=wV[j, 0:512, :].rearrange("(k p) d -> p k d", p=128)), writes=["Vw"], chan="Vw")
            P.dma("gpsimd", lambda e, j=j: e.dma_start(out=Vw[0:8, 4, 0:64], in_=wV[j, 512:520, :]), writes=["Vw"], chan="Vw")
            P.dma("sync", lambda e, j=j: e.dma_start(out=wins_out[j, :, :], in_=wrows[j, DEC_T:DEC_T + WIN, :]), chan="wins")
            gv = gts[:, :].rearrange("p (g c) -> p g c", c=3)
            for ct in range(4):
                s = nxt("sp", 2)
                P.emit("tensor", lambda e, s=s, ct=ct: e.matmul(sp[s][:, 0:64], lhsT=kcT[:, ct * 128:(ct + 1) * 128], rhs=QTa[0:64, 0, :], start=True, stop=True), reads=["kcT", "QTa"], writes=[f"sp{s}"])
                p = nxt("pt", 3)
                P.emit("scalar", lambda e, s=s, p=p: e.activation(out=pT[p][:, 0:64], in_=sp[s][:, 0:64], func=AF.Exp, scale=0.125), reads=[f"sp{s}"], writes=[f"pT{p}"])
                for g in range(8):
                    P.emit("tensor", lambda e, p=p, g=g, ct=ct: e.matmul(acc[g // 2][0:8, (g % 2) * 193:(g % 2) * 193 + 193], lhsT=pT[p][:, g * 8:(g + 1) * 8], rhs=rhsC[:, ct, :],
                                                                   start=(ct == 0 and g % 2 == 0), stop=(ct == 3)), reads=[f"pT{p}", "rhsC"], writes=[f"acc{g // 2}"])
            r8 = slice(0, 8)
            for a in range(4):
                P.emit("vector", lambda e, a=a: e.tensor_copy(out=den[:, 2 * a:2 * a + 2], in_=fview(acc[a], r8, 64, 193, 2)), reads=[f"acc{a}"], writes=["den"])
            P.emit("vector", lambda e: e.tensor_scalar(out=rden[:], in0=den[:], scalar1=1e-30, scalar2=None, op0=ALU.max), reads=["den"], writes=["rden"])
            P.emit("vector", lambda e: e.reciprocal(out=rden[:], in_=rden[:]), reads=["rden"], writes=["rden"])
            for g in range(8):
                U = acc[g // 2][0:8, (g % 2) * 193 + 65:(g % 2) * 193 + 193]
                if g == 0:
                    P.emit("vector", lambda e, U=U: e.tensor_scalar(out=imp[:], in0=U, scalar1=rden[:, 0:1], scalar2=None, op0=ALU.mult), reads=["acc0", "rden"], writes=["imp"])
                else:
                    P.emit("vector", lambda e, U=U, g=g: e.scalar_tensor_tensor(out=imp[:], in0=U, scalar=rden[:, g:g + 1], in1=imp[:], op0=ALU.mult, op1=ALU.add), reads=[f"acc{g // 2}", "rden", "imp"], writes=["imp"])
            ob_ = j % 2
            P.emit("vector", lambda e, gv=gv: e.tensor_tensor(out=fsc[:], in0=rden[:], in1=gv[:, :, 0], op=ALU.mult), reads=["rden", "gts"], writes=["fsc"])
            for a in range(4):
                ov = acc[a][0:8, 0:386].rearrange("p (g c) -> p g c", c=193)[:, :, 0:64]
                P.emit("vector", lambda e, a=a, ov=ov, ob_=ob_: e.tensor_tensor(out=oacc[ob_][:, 2 * a:2 * a + 2, :], in0=ov, in1=bc(fsc[:, 2 * a:2 * a + 2], [8, 2, 64], 2), op=ALU.mult), reads=[f"acc{a}", "fsc"], writes=[f"oacc{ob_}"])
            P.emit("vector", lambda e: e.tensor_tensor(out=imp[:], in0=imp[:], in1=tA[:], op=ALU.mult), reads=["imp", "tA"], writes=["imp"])
            P.emit("vector", lambda e: e.tensor_tensor(out=imp[:], in0=imp[:], in1=tB[:], op=ALU.add), reads=["imp", "tB"], writes=["imp"])
            P.emit("vector", lambda e: e.max(out=mx1[:], in_=imp[:]), reads=["imp"], writes=["mx1"])
            P.emit("vector", lambda e: e.match_replace(out=wk[:], in_to_replace=mx1[:], in_values=imp[:], imm_value=-1e30), reads=["imp", "mx1"], writes=["wk"])
            P.emit("vector", lambda e: e.max(out=mx2[:], in_=wk[:]), reads=["wk"], writes=["mx2"])
            P.emit("vector", lambda e: e.tensor_scalar(out=selb2[:, :, 64:128], in0=imp[:].rearrange("p (h j) -> p h j", h=2), scalar1=mx2[:, 6:7], scalar2=NEGB, op0=ALU.is_lt, op1=ALU.mult),
                   reads=["imp", "mx2"], writes=["selb2"])
            for h in range(2):
                tb = nxt("tp", 2)
                P.emit("tensor", lambda e, h=h, tb=tb: e.transpose(out=tpp[tb][:, 0, 0:8], in_=selb2[:, h, :], identity=ident[0:8, 0:8]), reads=["selb2", "ident"], writes=[f"tpp{tb}"])
                P.emit("scalar", lambda e, h=h, tb=tb: e.copy(out=QTa[64:128, h, :].rearrange("p (g t) -> p g t", g=8), in_=bc(tpp[tb][64:128, 0, 0:8], [64, 8, 8], 1)), reads=[f"tpp{tb}"], writes=["QTa"])
            for k4 in range(NPG // 4):
                s = nxt("sp", 2)
                for q in range(4):
                    kt = k4 * 4 + q
                    P.emit("tensor", lambda e, s=s, q=q, kt=kt: e.matmul(sp[s][:, q * 64:(q + 1) * 64], lhsT=KTa[:, kt * 128:(kt + 1) * 128], rhs=QTa[:, kt // 32, :], start=True, stop=True),
                           reads=["KTa", "QTa"], writes=[f"sp{s}"])
                p = nxt("pt", 3)
                P.emit("scalar", lambda e, s=s, p=p: e.activation(out=pT[p][:, :], in_=sp[s][:, 0:256], func=AF.Exp, scale=0.125), reads=[f"sp{s}"], writes=[f"pT{p}"])
                for q in range(4):
                    kt = k4 * 4 + q
                    for g in range(8):
                        P.emit("tensor", lambda e, p=p, q=q, g=g, kt=kt: e.matmul(acc[g // 4][0:8, (g % 4) * 65:(g % 4) * 65 + 65], lhsT=pT[p][:, q * 64 + g * 8:q * 64 + (g + 1) * 8], rhs=Vs[:, kt, :],
                                                                            start=(kt == 0 and g % 4 == 0), stop=False), reads=[f"pT{p}", "Vs"], writes=[f"acc{g // 4}"])
            s = nxt("sp", 2)
            P.emit("tensor", lambda e, s=s: e.matmul(sp[s][0:8, 0:64], lhsT=knT[:, :], rhs=QTa[0:64, 0, :], start=True, stop=False), reads=["knT", "QTa"], writes=[f"sp{s}"])
            P.emit("tensor", lambda e, s=s: e.matmul(sp[s][0:8, 0:64], lhsT=ident[0:8, 0:8], rhs=cbs[:, :], start=False, stop=True), reads=["ident", "cbs"], writes=[f"sp{s}"])
            p = nxt("pt", 3)
            P.emit("scalar", lambda e, s=s, p=p: e.activation(out=pT[p][0:8, 0:64], in_=sp[s][0:8, 0:64], func=AF.Exp, scale=0.125), reads=[f"sp{s}"], writes=[f"pT{p}"])
            for g in range(8):
                P.emit("tensor", lambda e, p=p, g=g: e.matmul(acc[g // 4][0:8, (g % 4) * 65:(g % 4) * 65 + 65], lhsT=pT[p][0:8, g * 8:(g + 1) * 8], rhs=vnA[:, :], start=False, stop=True),
                       reads=[f"pT{p}", "vnA"], writes=[f"acc{g // 4}"])
            for kt in range(5):
                s = nxt("sp", 2)
                P.emit("tensor", lambda e, s=s, kt=kt: e.matmul(sp[s][:, 0:64], lhsT=KwT[:, kt * 128:(kt + 1) * 128], rhs=QTa[0:64, 0, :], start=True, stop=False), reads=["KwT", "QTa"], writes=[f"sp{s}"])
                P.emit("tensor", lambda e, s=s, kt=kt: e.matmul(sp[s][:, 0:64], lhsT=ident[:], rhs=wbs[:, kt, :], start=False, stop=True), reads=["ident", "wbs"], writes=[f"sp{s}"])
                p = nxt("pt", 3)
                P.emit("scalar", lambda e, s=s, p=p: e.activation(out=pT[p][:, 0:64], in_=sp[s][:, 0:64], func=AF.Exp, scale=0.125), reads=[f"sp{s}"], writes=[f"pT{p}"])
                for g in range(8):
                    P.emit("tensor", lambda e, p=p, g=g, kt=kt: e.matmul(acc[2 + g // 4][0:8, (g % 4) * 65:(g % 4) * 65 + 65], lhsT=pT[p][:, g * 8:(g + 1) * 8], rhs=Vw[:, kt, :],
                                                                   start=(kt == 0 and g % 4 == 0), stop=(kt == 4)), reads=[f"pT{p}", "Vw"], writes=[f"acc{2 + g // 4}"])
            for br, a0 in ((1, 0), (2, 2)):
                for hf in range(2):
                    a = a0 + hf
                    P.emit("vector", lambda e, a=a, hf=hf: e.tensor_copy(out=den[:, hf * 4:(hf + 1) * 4], in_=fview(acc[a], r8, 64, 65, 4)), reads=[f"acc{a}"], writes=["den"])
                P.emit("vector", lambda e: e.tensor_scalar(out=rden[:], in0=den[:], scalar1=1e-30, scalar2=None, op0=ALU.max), reads=["den"], writes=["rden"])
                P.emit("vector", lambda e: e.reciprocal(out=rden[:], in_=rden[:]), reads=["rden"], writes=["rden"])
                P.emit("vector", lambda e, gv=gv, br=br: e.tensor_tensor(out=fsc[:], in0=rden[:], in1=gv[:, :, br], op=ALU.mult), reads=["rden", "gts"], writes=["fsc"])
                for hf in range(2):
                    a = a0 + hf
                    ov = acc[a][0:8, 0:260].rearrange("p (g c) -> p g c", c=65)[:, :, 0:64]
                    P.emit("vector", lambda e, a=a, hf=hf, ov=ov: e.tensor_tensor(out=otmp[:, hf * 4:(hf + 1) * 4, :], in0=ov, in1=bc(fsc[:, hf * 4:(hf + 1) * 4], [8, 4, 64], 2), op=ALU.mult), reads=[f"acc{a}", "fsc"], writes=["otmp"])
                P.emit("gpsimd", lambda e, ob_=ob_: e.tensor_tensor(out=oacc[ob_][:], in0=oacc[ob_][:], in1=otmp[:], op=ALU.add), reads=[f"oacc{ob_}", "otmp"], writes=[f"oacc{ob_}"])
            P.dma("sync", lambda e, ob_=ob_, j=j: e.dma_start(out=o_out[j * 8:(j + 1) * 8, :], in_=oacc[ob_][:].rearrange("p g d -> p (g d)")), reads=[f"oacc{ob_}"], chan=f"oacc{ob_}")
        P.replay(nc, es)
    return nc


def run_nsa_sample(zs, cache_cmp, cache_sel, state_win, page_table, w1, w2, pe):
    T = lambda a: np.ascontiguousarray(a)
    base = nsa_consts(0)
    tok = np.arange(DEC_T)
    qp = (PAST + tok)[:, None]
    j = np.arange(128)[None, :]
    cur = qp // 64
    forced = (j == 0) | (j == cur) | (j == cur - 1)
    tA = (~forced).astype(np.float32)
    tB = np.where(forced, 1e6, 0.0).astype(np.float32)
    causb = np.tile(np.where(np.arange(8)[:, None] <= tok[None, :], 0.0, NEGB).astype(np.float32), (1, 8))
    wb = np.zeros((128, 5, 64), np.float32)
    for r in range(5):
        kp = (PAST - WIN + r * 128 + np.arange(128))[:, None]
        ok = (kp <= qp.T) & (kp > qp.T - WIN) & (kp < PAST + DEC_T)
        wb[:, r, :] = np.tile(np.where(ok, 0.0, NEGB), (1, 8))
    cc_ = np.arange(SEQ)
    kt_, p_ = cc_ // 128, cc_ % 128
    tokc = ((kt_ // 8) * 8 + p_ // 16) * 128 + (p_ % 16) * 8 + (kt_ % 8)
    EKp = (((tokc // 64) % 64)[None, :] == np.arange(64)[:, None]).astype(np.float32)
    assert ((tokc // 64) // 64 == kt_ // 32).all() and sorted(tokc) == list(range(SEQ))
    w1s = T(w1.reshape(2, 32, 64, 256).transpose(0, 2, 1, 3).reshape(128, 32, 256))
    pes = T(pe.transpose(0, 2, 1).reshape(128, 32))
    kvs = zs[:, :, 1024:1792].reshape(DEC_B, DEC_T, 3, 2, 2, 64)
    in_maps = []
    for c in range(NCORES):
        kvh, s0 = c % 2, (c // 2) * 8
        sq = slice(s0, s0 + 8)
        wall = np.concatenate([state_win[sq, :, :, kvh, :], kvs[sq, :, 2, :, kvh, :]], axis=1)
        in_maps.append({
            "pool_cmp": T(cache_cmp[:, :, :, kvh, :].reshape(NPOOL, 128, 128)),
            "pool_sel": T(cache_sel[:, :, :, kvh, :].reshape(NPOOL, 128, 128)),
            "ptl": T(np.stack([page_table[sq][:, d * 8 + np.arange(128) // 16] for d in range(8)], axis=1).transpose(2, 0, 1).reshape(128, 64).astype(np.int32)),
            "pmod": T((np.arange(128) % 16).astype(np.int32).reshape(128, 1)),
            "qT": T(zs[sq, :, 0:1024].reshape(8, DEC_T, 2, 8, 64)[:, :, kvh].transpose(0, 3, 2, 1).reshape(8, 64, 64)),
            "gates": T(zs[sq, :, 1792:1840].reshape(8, DEC_T, 2, 24)[:, :, kvh]),
            "knewT": T(kvs[sq, :, 1, 0, kvh].transpose(0, 2, 1)), "vnew": T(kvs[sq, :, 1, 1, kvh]),
            "wKT": T(wall[:, :, 0].transpose(0, 2, 1)), "wV": T(wall[:, :, 1]), "wrows": T(wall.reshape(8, 520, 128)),
            "EK": EKp, "topA": tA, "topB": tB, "ovl": base["ovl"], "ropeC": base["ropeC"], "ropeS": base["ropeS"],
            "causb": causb, "wbias": wb, "w1s": w1s, "w2": T(w2), "pes": pes})
    if "nsa_s" not in _NC_CACHE:
        _NC_CACHE["nsa_s"] = build_nsa_s()
    res = run_bass_kernel_spmd(_NC_CACHE["nsa_s"], in_maps, core_ids=list(range(NCORES))).results
    os_ = np.zeros((DEC_B, DEC_T, 2, 512), np.float32)
    win_s = np.zeros((DEC_B, WIN, 2, 2, 64), np.float32)
    for c in range(NCORES):
        kvh, s0 = c % 2, (c // 2) * 8
        os_[s0:s0 + 8, :, kvh] = res[c]["o_out"].reshape(8, DEC_T, 512)
        win_s[s0:s0 + 8, :, :, kvh, :] = res[c]["wins_out"].reshape(8, WIN, 2, 64)
    return os_.reshape(DEC_B, DEC_T, D), win_s


LAM_INIT = 0.8 - 0.6 * float(np.exp(-0.3 * 1))


def build_diff():
    nc = bass.Bass("TRN2", target_bir_lowering=False)
    dt_in = lambda n, s: nc.dram_tensor(n, s, F32, kind="ExternalInput").ap()
    qT = dt_in("qT", [NPT, 8, 128, 128])
    KTd = dt_in("KTd", [2, 8, 128, SEQ])
    Vd = dt_in("Vd", [2, 8, SEQ, 128])
    kbias = dt_in("kbias", [NPT, 128, 15, 128])
    lam_rep = dt_in("lam_rep", [128, 256])
    sub_rep = dt_in("sub_rep", [128, 128])
    o_out = nc.dram_tensor("o_out", [NPT * 128, D], F32, kind="ExternalOutput").ap()
    P = Prog()
    with ExitStack() as es:
        sb = lambda n, s, d: es.enter_context(nc.sbuf_tensor(n, s, d))
        ps = lambda n, s, d: es.enter_context(nc.psum_tensor(n, s, d))
        ident = make_ident(P, nc, es)
        KT = [sb(f"KT{i}", [128, SEQ], BF16) for i in range(2)]
        Va = [sb(f"Va{i}", [128, 64, 129], BF16) for i in range(2)]
        QB = [sb(f"QB{i}", [128, 2, 128], BF16) for i in range(2)]
        kb1 = sb("kb1", [128, 15, 128], BF16)
        kbr = [sb(f"kbr{i}", [128, 15, 2, 128], BF16) for i in range(2)]
        pT = [sb(f"pT{i}", [128, 256], BF16) for i in range(3)]
        lamt = sb("lamt", [128, 256], F32)
        lprod = sb("lprod", [128, 128], F32)
        lsum = sb("lsum", [128, 2], F32)
        nlam = sb("nlam", [128, 1], F32)
        gsub = sb("gsub", [128, 128], F32)
        den = sb("den", [128, 2], F32)
        rd = sb("rd", [128, 2], F32)
        t0 = sb("t0", [128, 128], F32)
        o1 = sb("o1", [128, 128], F32)
        junk = sb("junk", [128, 128], F32)
        ss = sb("ss", [128, 1], F32)
        rstd = sb("rstd", [128, 1], F32)
        ot = [sb(f"ot{i}", [128, 128], F32) for i in range(2)]
        sp = [ps(f"sp{i}", [128, 256], F32) for i in range(3)]
        acc = [ps(f"acc{i}", [128, 258], F32) for i in range(2)]
        P.dma("sync", lambda e: e.dma_start(out=lamt[:], in_=lam_rep[:, :]), writes=["lamt"], chan="lamt")
        P.dma("sync", lambda e: e.dma_start(out=gsub[:], in_=sub_rep[:, :]), writes=["gsub"], chan="gsub")
        P.emit("vector", lambda e: e.tensor_tensor(out=lprod[:].rearrange("p (a d) -> p a d", a=2), in0=lamt[:].rearrange("p (a two d) -> p a two d", a=2, two=2)[:, :, 0, :],
                                                   in1=lamt[:].rearrange("p (a two d) -> p a two d", a=2, two=2)[:, :, 1, :], op=ALU.mult), reads=["lamt"], writes=["lprod"])
        P.emit("vector", lambda e: e.reduce_sum(out=lsum[:], in_=lprod[:].rearrange("p (a d) -> p a d", a=2), axis=mybir.AxisListType.X), reads=["lprod"], writes=["lsum"])
        P.emit("scalar", lambda e: e.activation(out=lsum[:], in_=lsum[:], func=AF.Exp), reads=["lsum"], writes=["lsum"])
        P.emit("vector", lambda e: e.tensor_tensor(out=nlam[:], in0=lsum[:, 1:2], in1=lsum[:, 0:1], op=ALU.subtract), reads=["lsum"], writes=["nlam"])
        P.emit("vector", lambda e: e.tensor_scalar(out=nlam[:], in0=nlam[:], scalar1=-LAM_INIT, scalar2=None, op0=ALU.add), reads=["nlam"], writes=["nlam"])
        P.emit("vector", lambda e: e.tensor_scalar(out=gsub[:], in0=gsub[:], scalar1=1.0 - LAM_INIT, scalar2=None, op0=ALU.mult), reads=["gsub"], writes=["gsub"])
        for i in range(2):
            P.emit("gpsimd", lambda e, i=i: e.memset(Va[i][:, :, 128:129], 1.0), writes=[f"Va{i}"])
            P.emit("gpsimd", lambda e, i=i: e.memset(QB[i][:], 0.0), writes=[f"QB{i}"])
        spi, pti, it = [0], [0], 0
        for b in range(2):
            for h in range(8):
                cb = (b * 8 + h) % 2
                for q4 in range(4):
                    cs_ = slice(q4 * 2048, (q4 + 1) * 2048)
                    P.dma("gpsimd", lambda e, cb=cb, b=b, h=h, cs_=cs_: e.dma_start(out=KT[cb][:, cs_], in_=KTd[b, h, :, cs_]), writes=[f"KT{cb}"], chan=f"KT{cb}")
                    P.dma("gpsimd", lambda e, cb=cb, b=b, h=h, q4=q4: e.dma_start(out=Va[cb][:, q4 * 16:(q4 + 1) * 16, 0:128], in_=Vd[b, h, q4 * 2048:(q4 + 1) * 2048, :].rearrange("(k p) d -> p k d", p=128)),
                          writes=[f"Va{cb}"], chan=f"Va{cb}")
                for ti in range(NPT):
                    tb_, qmin, qmax = slot_range(ti)
                    if tb_ != b:
                        continue
                    ib = it % 2
                    it += 1
                    r0 = ti * 128
                    P.dma("gpsimd", lambda e, ib=ib, ti=ti, h=h: e.dma_start(out=QB[ib][0:64, 0, :], in_=qT[ti, h, 0:64, :]), writes=[f"QB{ib}"], chan=f"QB{ib}")
                    P.dma("gpsimd", lambda e, ib=ib, ti=ti, h=h: e.dma_start(out=QB[ib][64:128, 1, :], in_=qT[ti, h, 64:128, :]), writes=[f"QB{ib}"], chan=f"QB{ib}")
                    P.dma("gpsimd", lambda e, ti=ti: e.dma_start(out=kb1[:], in_=kbias[ti, :, :, :]), writes=["kb1"], chan="kb1")
                    P.emit("gpsimd", lambda e, ib=ib: e.tensor_copy(out=kbr[ib][:], in_=bc(kb1[:], [128, 15, 2, 128], 2)), reads=["kb1"], writes=[f"kbr{ib}"])
                    ab = it % 2
                    pend = []

                    def pv_diff(p, kt, qmax, cb, ab):
                        for c in range(2):
                            P.emit("tensor", lambda e, p=p, c=c, kt=kt: e.matmul(acc[ab][:, c * 129:(c + 1) * 129], lhsT=pT[p][:, c * 128:(c + 1) * 128], rhs=Va[cb][:, kt, :],
                                                                           start=(kt == 0 and c == 0), stop=(kt == qmax)), reads=[f"pT{p}", f"Va{cb}"], writes=[f"acc{ab}"])

                    for kt in range(qmax + 1):
                        spi[0] += 1
                        s = spi[0] % 3
                        hasb = kt >= qmin
                        P.emit("tensor", lambda e, s=s, cb=cb, ib=ib, kt=kt, hasb=hasb: e.matmul(sp[s][:, :], lhsT=KT[cb][:, kt * 128:(kt + 1) * 128], rhs=QB[ib][:].rearrange("p c t -> p (c t)"),
                                                                                           start=True, stop=(not hasb)), reads=[f"KT{cb}", f"QB{ib}"], writes=[f"sp{s}"])
                        if hasb:
                            P.emit("tensor", lambda e, s=s, ib=ib, kt=kt, qmin=qmin: e.matmul(sp[s][:, :], lhsT=ident[:], rhs=kbr[ib][:, kt - qmin, :, :].rearrange("p c t -> p (c t)"), start=False, stop=True),
                                   reads=["ident", f"kbr{ib}"], writes=[f"sp{s}"])
                        pti[0] += 1
                        p = pti[0] % 3
                        P.emit("scalar", lambda e, s=s, p=p: e.activation(out=pT[p][:], in_=sp[s][:], func=AF.Exp, scale=0.125), reads=[f"sp{s}"], writes=[f"pT{p}"])
                        pend.append((p, kt, qmax, cb, ab))
                        if len(pend) > PIPE:
                            pv_diff(*pend.pop(0))
                    while pend:
                        pv_diff(*pend.pop(0))
                    A = f"acc{ab}"
                    P.emit("vector", lambda e, ab=ab: e.tensor_copy(out=den[:], in_=fview(acc[ab], slice(0, 128), 128, 129, 2)), reads=[A], writes=["den"])
                    P.emit("vector", lambda e: e.tensor_scalar(out=rd[:], in0=den[:], scalar1=1e-30, scalar2=None, op0=ALU.max), reads=["den"], writes=["rd"])
                    P.emit("vector", lambda e: e.reciprocal(out=rd[:], in_=rd[:]), reads=["rd"], writes=["rd"])
                    P.emit("vector", lambda e: e.tensor_tensor(out=rd[:, 1:2], in0=rd[:, 1:2], in1=nlam[:], op=ALU.mult), reads=["rd", "nlam"], writes=["rd"])
                    P.emit("vector", lambda e, ab=ab: e.tensor_scalar(out=t0[:], in0=acc[ab][:, 0:128], scalar1=rd[:, 0:1], scalar2=None, op0=ALU.mult), reads=[A, "rd"], writes=["t0"])
                    P.emit("vector", lambda e, ab=ab: e.scalar_tensor_tensor(out=o1[:], in0=acc[ab][:, 129:257], scalar=rd[:, 1:2], in1=t0[:], op0=ALU.mult, op1=ALU.add), reads=[A, "rd", "t0"], writes=["o1"])
                    P.emit("vector", lambda e: e.memset(ss[:], 0.0), writes=["ss"])
                    P.emit("scalar", lambda e: e.activation(out=junk[:], in_=o1[:], func=AF.Square, accum_out=ss[:, 0:1]), reads=["o1", "ss"], writes=["junk", "ss"])
                    P.emit("vector", lambda e: e.tensor_scalar(out=rstd[:], in0=ss[:], scalar1=1.0 / 128, scalar2=EPS, op0=ALU.mult, op1=ALU.add), reads=["ss"], writes=["rstd"])
                    P.emit("scalar", lambda e: e.activation(out=rstd[:], in_=rstd[:], func=AF.Sqrt), reads=["rstd"], writes=["rstd"])
                    P.emit("vector", lambda e: e.reciprocal(out=rstd[:], in_=rstd[:]), reads=["rstd"], writes=["rstd"])
                    ob_ = it % 2
                    P.emit("vector", lambda e, ob_=ob_: e.scalar_tensor_tensor(out=ot[ob_][:], in0=o1[:], scalar=rstd[:, 0:1], in1=gsub[:], op0=ALU.mult, op1=ALU.mult), reads=["o1", "rstd", "gsub"], writes=[f"ot{ob_}"])
                    P.dma("sync", lambda e, ob_=ob_, r0=r0, h=h: e.dma_start(out=o_out[r0:r0 + 128, h * 128:(h + 1) * 128], in_=ot[ob_][:]), reads=[f"ot{ob_}"], chan=f"ot{ob_}")
        P.replay(nc, es)
    return nc


def run_diff_prompt(z1p, lam_p, subnorm):
    T = lambda a: np.ascontiguousarray(a)
    KTd = T(z1p[:, :, 1024:2048].reshape(2, SEQ, 8, 128).transpose(0, 2, 3, 1))
    Vd = T(z1p[:, :, 2048:3072].reshape(2, SEQ, 8, 128).transpose(0, 2, 1, 3))
    lam_rep = T(np.broadcast_to(np.asarray(lam_p, np.float32).reshape(1, 256), (128, 256)))
    sub_rep = T(np.broadcast_to(np.asarray(subnorm, np.float32).reshape(1, 128), (128, 128)))
    in_maps = []
    for c in range(NCORES):
        tiles = core_tiles(c)
        qT = T(np.stack([z1p[b, p0:p0 + 128, 0:1024].reshape(128, 8, 128).transpose(1, 2, 0) for b, p0 in tiles]))
        in_maps.append({"qT": qT, "KTd": KTd, "Vd": Vd, "kbias": nsa_consts(c)["kbias"], "lam_rep": lam_rep, "sub_rep": sub_rep})
    if "diff" not in _NC_CACHE:
        _NC_CACHE["diff"] = build_diff()
    res = run_bass_kernel_spmd(_NC_CACHE["diff"], in_maps, core_ids=list(range(NCORES))).results
    op = np.zeros((2, SEQ, D), np.float32)
    for c in range(NCORES):
        for ti, (b_, p0) in enumerate(core_tiles(c)):
            op[b_, p0:p0 + 128] = res[c]["o_out"][ti * 128:(ti + 1) * 128]
    return op


def build_diff_s():
    nc = bass.Bass("TRN2", target_bir_lowering=False)
    dt_in = lambda n, s, d=F32: nc.dram_tensor(n, s, d, kind="ExternalInput").ap()
    pool = dt_in("pool", [NPOOL, 128, 256])
    ptl = dt_in("ptl", [128, DEC_B * 8], I32)
    pmod = dt_in("pmod", [128, 1], I32)
    qT = dt_in("qT", [DEC_B, 128, 8])
    knewT = dt_in("knewT", [DEC_B, 128, 8])
    vnew = dt_in("vnew", [DEC_B, 8, 128])
    causb = dt_in("causb", [8, 16])
    lam_rep = dt_in("lam_rep", [128, 256])
    sub_rep = dt_in("sub_rep", [128, 128])
    o_out = nc.dram_tensor("o_out", [DEC_B * DEC_T, 128], F32, kind="ExternalOutput").ap()
    P = Prog()
    with ExitStack() as es:
        sb = lambda n, s, d: es.enter_context(nc.sbuf_tensor(n, s, d))
        ps = lambda n, s, d: es.enter_context(nc.psum_tensor(n, s, d))
        ident = make_ident(P, nc, es)
        ptb = sb("ptb", [128, DEC_B * 8], I32)
        pidx = sb("pidx", [128, DEC_B * 8], I32)
        piota = sb("piota", [128, 1], I32)
        dTok = [sb(f"dTok{i}", [128, NPG, 256], BF16) for i in range(2)]
        KT = sb("KT", [128, SEQ], BF16)
        Va = sb("Va", [128, NPG, 129], BF16)
        QB = sb("QB", [128, 2, 8], BF16)
        knT = sb("knT", [128, 8], BF16)
        vnA = sb("vnA", [8, 129], BF16)
        cbs = sb("cbs", [8, 16], BF16)
        pT = [sb(f"pT{i}", [128, 128], BF16) for i in range(3)]
        lamt = sb("lamt", [128, 256], F32)
        lprod = sb("lprod", [128, 128], F32)
        lsum = sb("lsum", [128, 2], F32)
        nlam = sb("nlam", [128, 1], F32)
        gsub = sb("gsub", [128, 128], F32)
        den = sb("den", [8, 2], F32)
        rd = sb("rd", [8, 2], F32)
        t0 = sb("t0", [8, 128], F32)
        o1 = sb("o1", [8, 128], F32)
        junk = sb("junk", [8, 128], F32)
        ss = sb("ss", [8, 1], F32)
        rstd = sb("rstd", [8, 1], F32)
        ot = [sb(f"ot{i}", [8, 128], F32) for i in range(2)]
        tpp = [ps(f"tpp{i}", [128, 4, 128], BF16) for i in range(2)]
        sp = [ps(f"sp{i}", [128, 128], F32) for i in range(2)]
        acc = [ps(f"acc{i}", [128, 258], F32) for i in range(2)]
        P.dma("gpsimd", lambda e: e.dma_start(out=ptb[:], in_=ptl[:, :]), writes=["ptb"], chan="ptb")
        P.dma("gpsimd", lambda e: e.dma_start(out=piota[:], in_=pmod[:, :]), writes=["piota"], chan="piota")
        P.emit("gpsimd", lambda e: e.tensor_scalar(out=pidx[:], in0=ptb[:], scalar1=16, scalar2=None, op0=ALU.mult), reads=["ptb"], writes=["pt"])
        P.emit("gpsimd", lambda e: e.tensor_tensor(out=pidx[:], in0=pidx[:], in1=piota[:].to_broadcast([128, DEC_B * 8]), op=ALU.add), reads=["pt", "piota"], writes=["pt"])
        prow = pool.rearrange("n (g r) c -> (n g) (r c)", r=8)
        P.dma("sync", lambda e: e.dma_start(out=lamt[:], in_=lam_rep[:, :]), writes=["lamt"], chan="lamt")
        P.dma("sync", lambda e: e.dma_start(out=gsub[:], in_=sub_rep[:, :]), writes=["gsub"], chan="gsub")
        P.dma("gpsimd", lambda e: e.dma_start(out=cbs[:], in_=causb[:, :]), writes=["cbs"], chan="cbs")
        P.emit("vector", lambda e: e.tensor_tensor(out=lprod[:].rearrange("p (a d) -> p a d", a=2), in0=lamt[:].rearrange("p (a two d) -> p a two d", a=2, two=2)[:, :, 0, :],
                                                   in1=lamt[:].rearrange("p (a two d) -> p a two d", a=2, two=2)[:, :, 1, :], op=ALU.mult), reads=["lamt"], writes=["lprod"])
        P.emit("vector", lambda e: e.reduce_sum(out=lsum[:], in_=lprod[:].rearrange("p (a d) -> p a d", a=2), axis=mybir.AxisListType.X), reads=["lprod"], writes=["lsum"])
        P.emit("scalar", lambda e: e.activation(out=lsum[:], in_=lsum[:], func=AF.Exp), reads=["lsum"], writes=["lsum"])
        P.emit("vector", lambda e: e.tensor_tensor(out=nlam[:], in0=lsum[:, 1:2], in1=lsum[:, 0:1], op=ALU.subtract), reads=["lsum"], writes=["nlam"])
        P.emit("vector", lambda e: e.tensor_scalar(out=nlam[:], in0=nlam[:], scalar1=-LAM_INIT, scalar2=None, op0=ALU.add), reads=["nlam"], writes=["nlam"])
        P.emit("vector", lambda e: e.tensor_scalar(out=gsub[:], in0=gsub[:], scalar1=1.0 - LAM_INIT, scalar2=None, op0=ALU.mult), reads=["gsub"], writes=["gsub"])
        P.emit("gpsimd", lambda e: e.memset(Va[:, :, 128:129], 1.0), writes=["Va"])
        P.emit("gpsimd", lambda e: e.memset(vnA[:, 128:129], 1.0), writes=["vnA"])
        P.emit("gpsimd", lambda e: e.memset(QB[:], 0.0), writes=["QB"])
        cnt = {"sp": 0, "pt": 0, "tp": 0, "cp": 0}

        def nxt(k, n):
            cnt[k] += 1
            return cnt[k] % n

        def gather(j, bi):
            for d in range(8):
                def g(e, j=j, d=d, bi=bi):
                    return e.indirect_dma_start(out=dTok[bi][:, d * 8:(d + 1) * 8, :].rearrange("p r c -> p (r c)"), out_offset=None, in_=prow[:, :],
                                                in_offset=bass.IndirectOffsetOnAxis(ap=pidx[:, j * 8 + d:j * 8 + d + 1], axis=0))
                P.dma("gpsimd", g, reads=["pt"], writes=[f"dTok{bi}"], chan=f"dTok{bi}")

        gather(0, 0)
        r8 = slice(0, 8)
        for j in range(DEC_B):
            bi = j % 2
            if j + 1 < DEC_B:
                gather(j + 1, 1 - bi)
            for p4 in range(NPG // 4):
                tb = nxt("tp", 2)
                for q in range(4):
                    P.emit("tensor", lambda e, tb=tb, q=q, p4=p4, bi=bi: e.transpose(out=tpp[tb][:, q, :], in_=dTok[bi][:, p4 * 4 + q, 0:128], identity=ident[:]),
                           reads=[f"dTok{bi}", "ident"], writes=[f"tpp{tb}"])
                if nxt("cp", 2):
                    P.emit("scalar", lambda e, tb=tb, p4=p4: e.copy(out=KT[:, p4 * 512:(p4 + 1) * 512], in_=tpp[tb][:, :, :].rearrange("p q t -> p (q t)")), reads=[f"tpp{tb}"], writes=["KT"])
                else:
                    P.emit("vector", lambda e, tb=tb, p4=p4: e.tensor_copy(out=KT[:, p4 * 512:(p4 + 1) * 512], in_=tpp[tb][:, :, :].rearrange("p q t -> p (q t)")), reads=[f"tpp{tb}"], writes=["KT"])
            P.emit("gpsimd", lambda e, bi=bi: e.tensor_copy(out=Va[:, :, 0:128], in_=dTok[bi][:, :, 128:256]), reads=[f"dTok{bi}"], writes=["Va"])
            P.dma("gpsimd", lambda e, j=j: e.dma_start(out=QB[0:64, 0, :], in_=qT[j, 0:64, :]), writes=["QB"], chan="QB")
            P.dma("gpsimd", lambda e, j=j: e.dma_start(out=QB[64:128, 1, :], in_=qT[j, 64:128, :]), writes=["QB"], chan="QB")
            P.dma("gpsimd", lambda e, j=j: e.dma_start(out=knT[:], in_=knewT[j, :, :]), writes=["knT"], chan="knT")
            P.dma("gpsimd", lambda e, j=j: e.dma_start(out=vnA[:, 0:128], in_=vnew[j, :, :]), writes=["vnA"], chan="vnA")
            ab = j % 2
            for k8 in range(NPG // 8):
                s = nxt("sp", 2)
                for q in range(8):
                    kt = k8 * 8 + q
                    P.emit("tensor", lambda e, s=s, q=q, kt=kt: e.matmul(sp[s][:, q * 16:(q + 1) * 16], lhsT=KT[:, kt * 128:(kt + 1) * 128], rhs=QB[:].rearrange("p c t -> p (c t)"), start=True, stop=True),
                           reads=["KT", "QB"], writes=[f"sp{s}"])
                p = nxt("pt", 3)
                P.emit("scalar", lambda e, s=s, p=p: e.activation(out=pT[p][:, :], in_=sp[s][:, :], func=AF.Exp, scale=0.125), reads=[f"sp{s}"], writes=[f"pT{p}"])
                for q in range(8):
                    kt = k8 * 8 + q
                    for c in range(2):
                        P.emit("tensor", lambda e, p=p, q=q, c=c, kt=kt, ab=ab: e.matmul(acc[ab][0:8, c * 129:(c + 1) * 129], lhsT=pT[p][:, q * 16 + c * 8:q * 16 + (c + 1) * 8], rhs=Va[:, kt, :],
                                                                                   start=(kt == 0 and c == 0), stop=False), reads=[f"pT{p}", "Va"], writes=[f"acc{ab}"])
            s = nxt("sp", 2)
            P.emit("tensor", lambda e, s=s: e.matmul(sp[s][0:8, 0:16], lhsT=knT[:, :], rhs=QB[:].rearrange("p c t -> p (c t)"), start=True, stop=False), reads=["knT", "QB"], writes=[f"sp{s}"])
            P.emit("tensor", lambda e, s=s: e.matmul(sp[s][0:8, 0:16], lhsT=ident[0:8, 0:8], rhs=cbs[:, :], start=False, stop=True), reads=["ident", "cbs"], writes=[f"sp{s}"])
            p = nxt("pt", 3)
            P.emit("scalar", lambda e, s=s, p=p: e.activation(out=pT[p][0:8, 0:16], in_=sp[s][0:8, 0:16], func=AF.Exp, scale=0.125), reads=[f"sp{s}"], writes=[f"pT{p}"])
            for c in range(2):
                P.emit("tensor", lambda e, p=p, c=c, ab=ab: e.matmul(acc[ab][0:8, c * 129:(c + 1) * 129], lhsT=pT[p][0:8, c * 8:(c + 1) * 8], rhs=vnA[:, :], start=False, stop=True),
                       reads=[f"pT{p}", "vnA"], writes=[f"acc{ab}"])
            A = f"acc{ab}"
            P.emit("vector", lambda e, ab=ab: e.tensor_copy(out=den[:], in_=fview(acc[ab], r8, 128, 129, 2)), reads=[A], writes=["den"])
            P.emit("vector", lambda e: e.tensor_scalar(out=rd[:], in0=den[:], scalar1=1e-30, scalar2=None, op0=ALU.max), reads=["den"], writes=["rd"])
            P.emit("vector", lambda e: e.reciprocal(out=rd[:], in_=rd[:]), reads=["rd"], writes=["rd"])
            P.emit("vector", lambda e: e.tensor_tensor(out=rd[:, 1:2], in0=rd[:, 1:2], in1=nlam[r8, :], op=ALU.mult), reads=["rd", "nlam"], writes=["rd"])
            P.emit("vector", lambda e, ab=ab: e.tensor_scalar(out=t0[:], in0=acc[ab][0:8, 0:128], scalar1=rd[:, 0:1], scalar2=None, op0=ALU.mult), reads=[A, "rd"], writes=["t0"])
            P.emit("vector", lambda e, ab=ab: e.scalar_tensor_tensor(out=o1[:], in0=acc[ab][0:8, 129:257], scalar=rd[:, 1:2], in1=t0[:], op0=ALU.mult, op1=ALU.add), reads=[A, "rd", "t0"], writes=["o1"])
            P.emit("vector", lambda e: e.memset(ss[:], 0.0), writes=["ss"])
            P.emit("scalar", lambda e: e.activation(out=junk[:], in_=o1[:], func=AF.Square, accum_out=ss[:, 0:1]), reads=["o1", "ss"], writes=["junk", "ss"])
            P.emit("vector", lambda e: e.tensor_scalar(out=rstd[:], in0=ss[:], scalar1=1.0 / 128, scalar2=EPS, op0=ALU.mult, op1=ALU.add), reads=["ss"], writes=["rstd"])
            P.emit("scalar", lambda e: e.activation(out=rstd[:], in_=rstd[:], func=AF.Sqrt), reads=["rstd"], writes=["rstd"])
            P.emit("vector", lambda e: e.reciprocal(out=rstd[:], in_=rstd[:]), reads=["rstd"], writes=["rstd"])
            ob_ = j % 2
            P.emit("vector", lambda e, ob_=ob_: e.scalar_tensor_tensor(out=ot[ob_][:], in0=o1[:], scalar=rstd[:, 0:1], in1=gsub[r8, :], op0=ALU.mult, op1=ALU.mult), reads=["o1", "rstd", "gsub"], writes=[f"ot{ob_}"])
            P.dma("sync", lambda e, ob_=ob_, j=j: e.dma_start(out=o_out[j * 8:(j + 1) * 8, :], in_=ot[ob_][:]), reads=[f"ot{ob_}"], chan=f"ot{ob_}")
        P.replay(nc, es)
    return nc


def run_diff_sample(z1s, cache_kv, page_table, lam_p, subnorm):
    T = lambda a: np.ascontiguousarray(a)
    lam_rep = T(np.broadcast_to(np.asarray(lam_p, np.float32).reshape(1, 256), (128, 256)))
    sub_rep = T(np.broadcast_to(np.asarray(subnorm, np.float32).reshape(1, 128), (128, 128)))
    tok = np.arange(DEC_T)
    causb = np.tile(np.where(np.arange(8)[:, None] <= tok[None, :], 0.0, NEGB).astype(np.float32), (1, 2))
    pslot = np.arange(128) // 16
    ptl = T(np.stack([page_table[:, d * 8 + pslot] for d in range(8)], axis=1).transpose(2, 0, 1).reshape(128, DEC_B * 8).astype(np.int32))
    pmod = T((np.arange(128) % 16).astype(np.int32).reshape(128, 1))
    in_maps = []
    for h in range(NCORES):
        in_maps.append({
            "pool": T(cache_kv[:, :, :, h, :].reshape(NPOOL, 128, 256)), "ptl": ptl, "pmod": pmod,
            "qT": T(z1s[:, :, h * 128:(h + 1) * 128].transpose(0, 2, 1)),
            "knewT": T(z1s[:, :, 1024 + h * 128:1024 + (h + 1) * 128].transpose(0, 2, 1)),
            "vnew": T(z1s[:, :, 2048 + h * 128:2048 + (h + 1) * 128]),
            "causb": causb, "lam_rep": lam_rep, "sub_rep": sub_rep})
    if "diff_s" not in _NC_CACHE:
        _NC_CACHE["diff_s"] = build_diff_s()
    res = run_bass_kernel_spmd(_NC_CACHE["diff_s"], in_maps, core_ids=list(range(NCORES))).results
    os_ = np.zeros((DEC_B, DEC_T, 8, 128), np.float32)
    for h in range(NCORES):
        os_[:, :, h, :] = res[h]["o_out"].reshape(DEC_B, DEC_T, 128)
    return os_.reshape(DEC_B, DEC_T, D)


def rope_tables(pos):
    half = 32
    inv = (THETA ** (-np.arange(half, dtype=np.float32) / half)).astype(np.float32)
    ang = pos.astype(np.float32)[:, None] * inv[None, :]
    return np.cos(ang).astype(np.float32), np.sin(ang).astype(np.float32)


def core_positions(c):
    return np.concatenate([np.arange(p0, p0 + 128) for _, p0 in core_tiles(c)] + [np.tile(PAST + np.arange(DEC_T), 4)])


def shard_tokens(xp, xs, c):
    parts = [xp[b, p0:p0 + 128] for b, p0 in core_tiles(c)]
    parts.append(xs[4 * c:4 * c + 4].reshape(32, -1))
    return np.ascontiguousarray(np.concatenate(parts, axis=0))


def unshard_tokens(per_core, F):
    xp = np.zeros((2, SEQ, F), np.float32)
    xs = np.zeros((DEC_B, DEC_T, F), np.float32)
    for c in range(NCORES):
        r = per_core[c]
        for ti, (b, p0) in enumerate(core_tiles(c)):
            xp[b, p0:p0 + 128] = r[ti * 128:(ti + 1) * 128]
        xs[4 * c:4 * c + 4] = r[NPT * 128:].reshape(4, DEC_T, F)
    return xp, xs


def rep128(v):
    return np.ascontiguousarray(np.broadcast_to(np.asarray(v, np.float32)[None, :], (128, D)))


_NC_CACHE = {}


def run_proj(hp, hs, g, w, ncols, rope_sets, sig_cols):
    key = ("proj", ncols)
    if key not in _NC_CACHE:
        _NC_CACHE[key] = build_proj(ncols, rope_sets, sig_cols)
    in_maps = []
    for c in range(NCORES):
        cos, sin = rope_tables(core_positions(c))
        in_maps.append({"x": shard_tokens(hp, hs, c), "cs": cos, "sn": sin,
                        "w": np.ascontiguousarray(w, np.float32), "g_rep": rep128(g)})
    res = run_bass_kernel_spmd(_NC_CACHE[key], in_maps, core_ids=list(range(NCORES))).results
    return unshard_tokens([r["z_out"] for r in res], ncols)


def run_post(hp, hs, op, os_, w_out, g_mlp, w_up, w_down, g_fin, final):
    key = ("post", final)
    if key not in _NC_CACHE:
        _NC_CACHE[key] = build_post(final)
    in_maps = []
    for c in range(NCORES):
        in_maps.append({"x": shard_tokens(hp, hs, c), "o": shard_tokens(op, os_, c),
                        "w_out": np.ascontiguousarray(w_out, np.float32), "w_up": np.ascontiguousarray(w_up, np.float32),
                        "w_down": np.ascontiguousarray(w_down, np.float32), "g_rep": rep128(g_mlp), "gf_rep": rep128(g_fin)})
    res = run_bass_kernel_spmd(_NC_CACHE[key], in_maps, core_ids=list(range(NCORES))).results
    return unshard_tokens([r["h_out"] for r in res], D)


NSA_ROPE = [(0, 1, 0, 16), (1280, 2, 256, 2)]
DIFF_ROPE = [(0, 1, 0, 16), (1024, 1, 0, 16)]


def kernel(x_prompt, x_sample, cache_nsa_cmp, cache_nsa_sel, state_nsa_win, cache_diff_kv, page_table,
           norm_mix, norm_mlp, norm_final, nsa_w_in, nsa_cmp_pe, nsa_cmp_w1, nsa_cmp_w2, nsa_w_out,
           diff_w_in, diff_lambda, diff_subnorm, diff_w_out, mlp_w_up, mlp_w_down):
    f = lambda a: np.asarray(a, np.float32)
    x_prompt, x_sample = f(x_prompt), f(x_sample)
    zp, zs = run_proj(x_prompt, x_sample, f(norm_mix)[0], f(nsa_w_in)[0], NSA_IN, NSA_ROPE, (1792, 1840))
    op0 = run_nsa_prompt(zp, f(nsa_cmp_w1)[0], f(nsa_cmp_w2)[0], f(nsa_cmp_pe)[0])
    os0, win_s5 = run_nsa_sample(zs, f(cache_nsa_cmp)[0], f(cache_nsa_sel)[0], f(state_nsa_win)[0], np.asarray(page_table),
                                 f(nsa_cmp_w1)[0], f(nsa_cmp_w2)[0], f(nsa_cmp_pe)[0])
    h1p, h1s = run_post(x_prompt, x_sample, op0, os0, f(nsa_w_out)[0], f(norm_mlp)[0], f(mlp_w_up)[0], f(mlp_w_down)[0], f(norm_final), False)
    z1p, z1s = run_proj(h1p, h1s, f(norm_mix)[1], f(diff_w_in)[0], DIFF_IN, DIFF_ROPE, None)
    op1 = run_diff_prompt(z1p, f(diff_lambda)[0], f(diff_subnorm)[0])
    os1 = run_diff_sample(z1s, f(cache_diff_kv)[0], np.asarray(page_table), f(diff_lambda)[0], f(diff_subnorm)[0])
    yp, ys = run_post(h1p, h1s, op1, os1, f(diff_w_out)[0], f(norm_mlp)[1], f(mlp_w_up)[1], f(mlp_w_down)[1], f(norm_final), True)
    kv6 = lambda a: np.ascontiguousarray(a).reshape(a.shape[:2] + (2, 2, 64))[None]
    cmp_p, sel_p, win_p = kv6(zp[:, :, 1024:1280]), kv6(zp[:, :, 1280:1536]), kv6(zp[:, SEQ - WIN:, 1536:1792])
    cmp_s, sel_s = kv6(zs[:, :, 1024:1280]), kv6(zs[:, :, 1280:1536])
    win_s = win_s5[None]
    dkv = lambda z1: np.ascontiguousarray(z1[:, :, 1024:3072]).reshape(z1.shape[:2] + (2, 8, 128))[None]
    return (yp, ys, cmp_p, cmp_s, sel_p, sel_s, win_p, win_s, dkv(z1p), dkv(z1s))
```

```python
import numpy as np
from contextlib import ExitStack
import concourse.bass as bass
import concourse.mybir as mybir
from concourse.bass_utils import run_bass_kernel_spmd

F32 = mybir.dt.float32
BF16 = mybir.dt.bfloat16
I32 = mybir.dt.int32
AF = mybir.ActivationFunctionType
ALU = mybir.AluOpType

NCORES = 8
D = 1024
SEQ = 8192
PAST = 8192
DEC_B, DEC_T = 32, 8
NSA_IN = 1840
DIFF_IN = 3072
DFF = 4096
WIN = 512
EPS = 1e-6
NPT = 16
ROWS = NPT * 128 + 32
THETA = 10000.0
NEGB = -30000.0
PIPE = 2


def core_tiles(c):
    out = []
    for b in range(2):
        for u in sorted([c, 15 - c, 16 + c, 31 - c]):
            for s in range(2):
                out.append((b, u * 256 + s * 128))
    return out


def tile_rows():
    return [(i * 128, 128) for i in range(NPT)] + [(NPT * 128, 32)]


class Prog:
    ENG = ("sync", "scalar", "vector", "tensor", "gpsimd")

    def __init__(self):
        self.items = []
        self.cnt = {}
        self.lastw = {}
        self.readers = {}
        self.waited = {e: {} for e in self.ENG}

    def emit(self, eng, fn, reads=(), writes=(), inc=1, chan=None):
        key = (eng, chan)
        deps = {}

        def need(tok):
            if tok is None:
                return
            k, c = tok
            if k == ("tensor", None) and eng == "tensor":
                return
            if self.waited[eng].get(k, 0) >= c:
                return
            deps[k] = max(deps.get(k, 0), c)

        for b in reads:
            need(self.lastw.get(b))
        for b in writes:
            need(self.lastw.get(b))
            for k, c in self.readers.get(b, {}).items():
                need((k, c))
        for k, c in deps.items():
            self.waited[eng][k] = c
        c = self.cnt.get(key, 0) + inc
        self.cnt[key] = c
        for b in reads:
            self.readers.setdefault(b, {})[key] = c
        for b in writes:
            self.lastw[b] = (key, c)
            self.readers[b] = {}
        self.items.append((eng, deps, fn, key, inc))

    def dma(self, eng, fn, reads=(), writes=(), chan=None):
        assert chan is not None
        self.emit(eng, fn, reads=reads, writes=writes, inc=16, chan="d_" + chan)

    def replay(self, nc, es):
        sems = {k: es.enter_context(nc.semaphore("s%d" % i)) for i, k in enumerate(self.cnt)}
        block = es.enter_context(nc.Block())

        def make(name):
            def body(e):
                for eng, deps, fn, key, inc in self.items:
                    if eng != name:
                        continue
                    for k, c in deps.items():
                        e.wait_ge(sems[k], c)
                    fn(e).then_inc(sems[key], inc)
                if name == "sync":
                    for k, c in self.cnt.items():
                        e.wait_ge(sems[k], c)
            return body
        for name in self.ENG:
            getattr(block, name)(make(name))


def bc(ap, shape, axis):
    return ap.unsqueeze(axis).to_broadcast(list(shape))


def make_ident(P, nc, es):
    identf = es.enter_context(nc.sbuf_tensor("identf", [128, 128], F32))
    ident = es.enter_context(nc.sbuf_tensor("ident", [128, 128], BF16))
    P.emit("gpsimd", lambda e: e.memset(identf[:], 0.0), writes=["identf"])
    P.emit("gpsimd", lambda e: e.affine_select(out=identf[:], in_=identf[:], pattern=[[-1, 128]],
                                               compare_op=ALU.not_equal, fill=1.0, base=0, channel_multiplier=1),
           reads=["identf"], writes=["identf"])
    P.emit("vector", lambda e: e.tensor_copy(out=ident[:], in_=identf[:]), reads=["identf"], writes=["ident"])
    return ident


def emit_rmsnorm(P, xt, rows, gt, hn, ss, rstd, junk, tag):
    P.emit("vector", lambda e: e.memset(ss[rows, :], 0.0), writes=["ss" + tag])
    P.emit("scalar", lambda e: e.activation(out=junk[rows, :], in_=xt[rows, :], func=AF.Square, accum_out=ss[rows, 0:1]),
           reads=["xt" + tag, "ss" + tag], writes=["junk" + tag, "ss" + tag])
    P.emit("vector", lambda e: e.tensor_scalar(out=rstd[rows, :], in0=ss[rows, :], scalar1=1.0 / D, scalar2=EPS,
                                               op0=ALU.mult, op1=ALU.add), reads=["ss" + tag], writes=["rstd" + tag])
    P.emit("scalar", lambda e: e.activation(out=rstd[rows, :], in_=rstd[rows, :], func=AF.Sqrt),
           reads=["rstd" + tag], writes=["rstd" + tag])
    P.emit("vector", lambda e: e.reciprocal(out=rstd[rows, :], in_=rstd[rows, :]), reads=["rstd" + tag], writes=["rstd" + tag])
    P.emit("vector", lambda e: e.scalar_tensor_tensor(out=hn[rows, :], in0=xt[rows, :], scalar=rstd[rows, 0:1],
                                                      in1=gt[rows, :], op0=ALU.mult, op1=ALU.mult),
           reads=["xt" + tag, "rstd" + tag, "gt"], writes=["hn" + tag])


def build_proj(ncols, rope_sets, sig_cols):
    nc = bass.Bass("TRN2", target_bir_lowering=False)
    x = nc.dram_tensor("x", [ROWS, D], F32, kind="ExternalInput").ap()
    cs = nc.dram_tensor("cs", [ROWS, 32], F32, kind="ExternalInput").ap()
    sn = nc.dram_tensor("sn", [ROWS, 32], F32, kind="ExternalInput").ap()
    w = nc.dram_tensor("w", [D, ncols], F32, kind="ExternalInput").ap()
    g_rep = nc.dram_tensor("g_rep", [128, D], F32, kind="ExternalInput").ap()
    z_out = nc.dram_tensor("z_out", [ROWS, ncols], F32, kind="ExternalOutput").ap()
    P = Prog()
    with ExitStack() as es:
        sb = lambda n, s, d: es.enter_context(nc.sbuf_tensor(n, s, d))
        wb = sb("wb", [128, 8, ncols], BF16)
        gt = sb("gt", [128, D], F32)
        NB = 2
        xt = [sb(f"xt{i}", [128, D], F32) for i in range(NB)]
        junk = [sb(f"junk{i}", [128, D], F32) for i in range(NB)]
        hn = [sb(f"hn{i}", [128, D], BF16) for i in range(NB)]
        hnT = [sb(f"hnT{i}", [128, 8, 128], BF16) for i in range(NB)]
        cst = [sb(f"cst{i}", [128, 32], F32) for i in range(NB)]
        snt = [sb(f"snt{i}", [128, 32], F32) for i in range(NB)]
        ss = [sb(f"ss{i}", [128, 1], F32) for i in range(NB)]
        rstd = [sb(f"rstd{i}", [128, 1], F32) for i in range(NB)]
        zsb = [sb(f"zsb{i}", [128, ncols], F32) for i in range(NB)]
        tmp = [sb(f"tmp{i}", [128, 1024], F32) for i in range(4)]
        tp = [es.enter_context(nc.psum_tensor(f"tp{i}", [128, 8, 128], BF16)) for i in range(NB)]
        zp = [es.enter_context(nc.psum_tensor(f"zp{i}", [128, 512], F32)) for i in range(4)]
        ident = make_ident(P, nc, es)
        for k in range(8):
            P.dma("gpsimd", lambda e, k=k: e.dma_start(out=wb[:, k, :], in_=w[k * 128:(k + 1) * 128, :]), writes=["wb"], chan="wb")
        P.dma("sync", lambda e: e.dma_start(out=gt[:], in_=g_rep[:, :]), writes=["gt"], chan="gt")
        chunks = [(c0, min(512, ncols - c0)) for c0 in range(0, ncols, 512)]
        zi = 0
        for ti, (r0, R) in enumerate(tile_rows()):
            bi = ti % NB
            t = str(bi)
            rows = slice(0, R)
            P.dma("sync", lambda e, bi=bi, r0=r0, R=R: e.dma_start(out=xt[bi][0:R, :], in_=x[r0:r0 + R, :]), writes=["xt" + t], chan="xt" + t)
            P.dma("sync", lambda e, bi=bi, r0=r0, R=R: e.dma_start(out=cst[bi][0:R, :], in_=cs[r0:r0 + R, :]), writes=["cst" + t], chan="cst" + t)
            P.dma("sync", lambda e, bi=bi, r0=r0, R=R: e.dma_start(out=snt[bi][0:R, :], in_=sn[r0:r0 + R, :]), writes=["snt" + t], chan="snt" + t)
            emit_rmsnorm(P, xt[bi], rows, gt, hn[bi], ss[bi], rstd[bi], junk[bi], t)
            for k in range(8):
                P.emit("tensor", lambda e, k=k, R=R, bi=bi: e.transpose(out=tp[bi][:, k, 0:R], in_=hn[bi][0:R, k * 128:(k + 1) * 128],
                                                                        identity=ident[0:R, 0:R]),
                       reads=["hn" + t, "ident"], writes=["tp" + t])
            P.emit("scalar", lambda e, R=R, bi=bi: e.copy(out=hnT[bi][:, :, 0:R], in_=tp[bi][:, :, 0:R]), reads=["tp" + t], writes=["hnT" + t])
            for (c0, wd) in chunks:
                zb = zi % 4
                zi += 1
                for k in range(8):
                    P.emit("tensor", lambda e, zb=zb, k=k, c0=c0, wd=wd, R=R, bi=bi: e.matmul(
                        zp[zb][0:R, 0:wd], lhsT=hnT[bi][:, k, 0:R], rhs=wb[:, k, c0:c0 + wd], start=(k == 0), stop=(k == 7)),
                        reads=["hnT" + t, "wb"], writes=[f"zp{zb}"])
                P.emit("scalar", lambda e, zb=zb, c0=c0, wd=wd, rows=rows, bi=bi: e.copy(out=zsb[bi][rows, c0:c0 + wd], in_=zp[zb][rows, 0:wd]),
                       reads=[f"zp{zb}"], writes=["zsb" + t])
            for (col0, no, ostride, ni) in rope_sets:
                def views(bi=bi, col0=col0, no=no, ostride=ostride, ni=ni, R=R):
                    a1 = zsb[bi][0:R, col0:col0 + 1]
                    a2 = zsb[bi][0:R, col0 + 32:col0 + 33]
                    pat = [[ostride if no > 1 else 64 * ni, no], [64, ni], [1, 32]]
                    x1 = bass.AP(a1.tensor, a1.offset, [list(a1.ap[0])] + pat)
                    x2 = bass.AP(a2.tensor, a2.offset, [list(a2.ap[0])] + pat)
                    return x1, x2
                x1, x2 = views()
                ng = no * ni
                shp = [R, no, ni, 32]
                c4 = cst[bi][0:R, :].unsqueeze(1).unsqueeze(1).to_broadcast(shp)
                s4 = snt[bi][0:R, :].unsqueeze(1).unsqueeze(1).to_broadcast(shp)
                a = [tmp[i][0:R, 0:ng * 32].rearrange("p (o i f) -> p o i f", o=no, i=ni, f=32) for i in range(4)]
                P.emit("vector", lambda e, a=a, x1=x1, c4=c4: e.tensor_tensor(out=a[0], in0=x1, in1=c4, op=ALU.mult), reads=["zsb" + t, "cst" + t], writes=["tmp0"])
                P.emit("gpsimd", lambda e, a=a, x2=x2, s4=s4: e.tensor_tensor(out=a[1], in0=x2, in1=s4, op=ALU.mult), reads=["zsb" + t, "snt" + t], writes=["tmp1"])
                P.emit("vector", lambda e, a=a, x2=x2, c4=c4: e.tensor_tensor(out=a[2], in0=x2, in1=c4, op=ALU.mult), reads=["zsb" + t, "cst" + t], writes=["tmp2"])
                P.emit("gpsimd", lambda e, a=a, x1=x1, s4=s4: e.tensor_tensor(out=a[3], in0=x1, in1=s4, op=ALU.mult), reads=["zsb" + t, "snt" + t], writes=["tmp3"])
                P.emit("vector", lambda e, a=a, x1=x1: e.tensor_tensor(out=x1, in0=a[0], in1=a[1], op=ALU.subtract), reads=["tmp0", "tmp1"], writes=["zsb" + t])
                P.emit("gpsimd", lambda e, a=a, x2=x2: e.tensor_tensor(out=x2, in0=a[2], in1=a[3], op=ALU.add), reads=["tmp2", "tmp3"], writes=["zsb" + t])
            if sig_cols is not None:
                s0, s1 = sig_cols
                P.emit("scalar", lambda e, bi=bi, rows=rows, s0=s0, s1=s1: e.activation(out=zsb[bi][rows, s0:s1], in_=zsb[bi][rows, s0:s1], func=AF.Sigmoid),
                       reads=["zsb" + t], writes=["zsb" + t])
            P.dma("sync", lambda e, bi=bi, r0=r0, R=R: e.dma_start(out=z_out[r0:r0 + R, :], in_=zsb[bi][0:R, :]), reads=["zsb" + t], chan="zsb" + t)
        P.replay(nc, es)
    return nc


def build_post(final):
    nc = bass.Bass("TRN2", target_bir_lowering=False)
    x = nc.dram_tensor("x", [ROWS, D], F32, kind="ExternalInput").ap()
    o = nc.dram_tensor("o", [ROWS, D], F32, kind="ExternalInput").ap()
    w_out = nc.dram_tensor("w_out", [D, D], F32, kind="ExternalInput").ap()
    w_up = nc.dram_tensor("w_up", [D, DFF], F32, kind="ExternalInput").ap()
    w_down = nc.dram_tensor("w_down", [DFF, D], F32, kind="ExternalInput").ap()
    g_rep = nc.dram_tensor("g_rep", [128, D], F32, kind="ExternalInput").ap()
    gf_rep = nc.dram_tensor("gf_rep", [128, D], F32, kind="ExternalInput").ap()
    h_out = nc.dram_tensor("h_out", [ROWS, D], F32, kind="ExternalOutput").ap()
    P = Prog()
    with ExitStack() as es:
        sb = lambda n, s, d: es.enter_context(nc.sbuf_tensor(n, s, d))
        wo = sb("wo", [128, 8, D], BF16)
        wu = sb("wu", [128, 8, DFF], BF16)
        wd_ = sb("wd", [128, 32, D], BF16)
        gt = sb("gt", [128, D], F32)
        gf = sb("gf", [128, D], F32)
        xh = sb("xh", [128, 2, D], F32)
        ost = sb("ost", [128, D], F32)
        ob = sb("ob", [128, 2, D], BF16)
        aT = sb("aT", [128, 8, 256], BF16)
        uT = sb("uT", [128, 32, 256], BF16)
        rt = [sb(f"rt{i}", [128, 512], F32) for i in range(2)]
        ss = sb("ss", [128, 1], F32)
        rstd = sb("rstd", [128, 1], F32)
        tp = [es.enter_context(nc.psum_tensor(f"tp{i}", [128, 8, 128], BF16)) for i in range(2)]
        mp = [es.enter_context(nc.psum_tensor(f"mp{i}", [128, 512], F32)) for i in range(4)]
        ident = make_ident(P, nc, es)
        for k in range(8):
            P.dma("gpsimd", lambda e, k=k: e.dma_start(out=wo[:, k, :], in_=w_out[k * 128:(k + 1) * 128, :]), writes=["wo"], chan="wo")
        for k in range(8):
            for hh in range(2):
                P.dma("gpsimd", lambda e, k=k, hh=hh: e.dma_start(out=wu[:, k, hh * 2048:(hh + 1) * 2048], in_=w_up[k * 128:(k + 1) * 128, hh * 2048:(hh + 1) * 2048]),
                      writes=["wu"], chan="wu")
        for f in range(32):
            P.dma("gpsimd", lambda e, f=f: e.dma_start(out=wd_[:, f, :], in_=w_down[f * 128:(f + 1) * 128, :]), writes=["wd"], chan="wd")
        P.dma("sync", lambda e: e.dma_start(out=gt[:], in_=g_rep[:, :]), writes=["gt"], chan="gt")
        P.dma("sync", lambda e: e.dma_start(out=gf[:], in_=gf_rep[:, :]), writes=["gf"], chan="gf")
        groups = [[(2 * gi) * 128, 128, (2 * gi + 1) * 128, 128] for gi in range(NPT // 2)] + [[NPT * 128, 32]]
        mi = 0
        for grp in groups:
            tl = [(grp[i], grp[i + 1]) for i in range(0, len(grp), 2)]
            NT = sum(R for _, R in tl)
            for j, (r0, R) in enumerate(tl):
                P.dma("sync", lambda e, j=j, r0=r0, R=R: e.dma_start(out=xh[0:R, j, :], in_=x[r0:r0 + R, :]), writes=["xh"], chan="xh")
                P.dma("sync", lambda e, r0=r0, R=R: e.dma_start(out=ost[0:R, :], in_=o[r0:r0 + R, :]), writes=["ost"], chan="ost")
                P.emit("scalar", lambda e, j=j, R=R: e.copy(out=ob[0:R, j, :], in_=ost[0:R, :]), reads=["ost"], writes=["ob"])
                tb = j % 2
                for k in range(8):
                    P.emit("tensor", lambda e, k=k, R=R, j=j, tb=tb: e.transpose(out=tp[tb][:, k, 0:R], in_=ob[0:R, j, k * 128:(k + 1) * 128], identity=ident[0:R, 0:R]),
                           reads=["ob", "ident"], writes=[f"tp{tb}"])
                P.emit("scalar", lambda e, j=j, R=R, tb=tb: e.copy(out=aT[:, :, j * 128:j * 128 + R], in_=tp[tb][:, :, 0:R]), reads=[f"tp{tb}"], writes=["aT"])
            for j, (r0, R) in enumerate(tl):
                for hf in range(2):
                    mb = mi % 4
                    mi += 1
                    for k in range(8):
                        P.emit("tensor", lambda e, mb=mb, k=k, j=j, R=R, hf=hf: e.matmul(mp[mb][0:R, :], lhsT=aT[:, k, j * 128:j * 128 + R], rhs=wo[:, k, hf * 512:(hf + 1) * 512],
                                                                                     start=(k == 0), stop=(k == 7)), reads=["aT", "wo"], writes=[f"mp{mb}"])
                    P.emit("vector", lambda e, mb=mb, j=j, R=R, hf=hf: e.tensor_tensor(out=xh[0:R, j, hf * 512:(hf + 1) * 512], in0=mp[mb][0:R, :], in1=xh[0:R, j, hf * 512:(hf + 1) * 512], op=ALU.add),
                           reads=[f"mp{mb}", "xh"], writes=["xh"])
            for j, (r0, R) in enumerate(tl):
                rows = slice(0, R)
                P.emit("vector", lambda e, rows=rows: e.memset(ss[rows, :], 0.0), writes=["ss"])
                P.emit("scalar", lambda e, rows=rows, j=j: e.activation(out=ost[rows, :], in_=xh[rows, j, :], func=AF.Square, accum_out=ss[rows, 0:1]),
                       reads=["xh", "ss"], writes=["ost", "ss"])
                P.emit("vector", lambda e, rows=rows: e.tensor_scalar(out=rstd[rows, :], in0=ss[rows, :], scalar1=1.0 / D, scalar2=EPS, op0=ALU.mult, op1=ALU.add),
                       reads=["ss"], writes=["rstd"])
                P.emit("scalar", lambda e, rows=rows: e.activation(out=rstd[rows, :], in_=rstd[rows, :], func=AF.Sqrt), reads=["rstd"], writes=["rstd"])
                P.emit("vector", lambda e, rows=rows: e.reciprocal(out=rstd[rows, :], in_=rstd[rows, :]), reads=["rstd"], writes=["rstd"])
                P.emit("vector", lambda e, rows=rows, j=j: e.scalar_tensor_tensor(out=ob[rows, j, :], in0=xh[rows, j, :], scalar=rstd[rows, 0:1], in1=gt[rows, :], op0=ALU.mult, op1=ALU.mult),
                       reads=["xh", "rstd", "gt"], writes=["ob"])
                tb = j % 2
                for k in range(8):
                    P.emit("tensor", lambda e, k=k, R=R, j=j, tb=tb: e.transpose(out=tp[tb][:, k, 0:R], in_=ob[0:R, j, k * 128:(k + 1) * 128], identity=ident[0:R, 0:R]),
                           reads=["ob", "ident"], writes=[f"tp{tb}"])
                P.emit("scalar", lambda e, j=j, R=R, tb=tb: e.copy(out=aT[:, :, j * 128:j * 128 + R], in_=tp[tb][:, :, 0:R]), reads=[f"tp{tb}"], writes=["aT"])
            for fc in range(32):
                mb = mi % 4
                mi += 1
                for k in range(8):
                    P.emit("tensor", lambda e, mb=mb, k=k, fc=fc, NT=NT: e.matmul(mp[mb][:, 0:NT], lhsT=wu[:, k, fc * 128:(fc + 1) * 128], rhs=aT[:, k, 0:NT], start=(k == 0), stop=(k == 7)),
                           reads=["aT", "wu"], writes=[f"mp{mb}"])
                rb = fc % 2
                P.emit("scalar", lambda e, mb=mb, rb=rb, NT=NT: e.activation(out=rt[rb][:, 0:NT], in_=mp[mb][:, 0:NT], func=AF.Relu), reads=[f"mp{mb}"], writes=[f"rt{rb}"])
                P.emit("vector", lambda e, rb=rb, fc=fc, NT=NT: e.tensor_tensor(out=uT[:, fc, 0:NT], in0=rt[rb][:, 0:NT], in1=rt[rb][:, 0:NT], op=ALU.mult), reads=[f"rt{rb}"], writes=["uT"])
            for j, (r0, R) in enumerate(tl):
                for hf in range(2):
                    mb = mi % 4
                    mi += 1
                    for fc in range(32):
                        P.emit("tensor", lambda e, mb=mb, fc=fc, j=j, R=R, hf=hf: e.matmul(mp[mb][0:R, :], lhsT=uT[:, fc, j * 128:j * 128 + R], rhs=wd_[:, fc, hf * 512:(hf + 1) * 512],
                                                                                       start=(fc == 0), stop=(fc == 31)), reads=["uT", "wd"], writes=[f"mp{mb}"])
                    P.emit("vector", lambda e, mb=mb, j=j, R=R, hf=hf: e.tensor_tensor(out=xh[0:R, j, hf * 512:(hf + 1) * 512], in0=mp[mb][0:R, :], in1=xh[0:R, j, hf * 512:(hf + 1) * 512], op=ALU.add),
                           reads=[f"mp{mb}", "xh"], writes=["xh"])
                if final:
                    rows = slice(0, R)
                    P.emit("vector", lambda e, rows=rows: e.memset(ss[rows, :], 0.0), writes=["ss"])
                    P.emit("scalar", lambda e, rows=rows, j=j: e.activation(out=ost[rows, :], in_=xh[rows, j, :], func=AF.Square, accum_out=ss[rows, 0:1]),
                           reads=["xh", "ss"], writes=["ost", "ss"])
                    P.emit("vector", lambda e, rows=rows: e.tensor_scalar(out=rstd[rows, :], in0=ss[rows, :], scalar1=1.0 / D, scalar2=EPS, op0=ALU.mult, op1=ALU.add),
                           reads=["ss"], writes=["rstd"])
                    P.emit("scalar", lambda e, rows=rows: e.activation(out=rstd[rows, :], in_=rstd[rows, :], func=AF.Sqrt), reads=["rstd"], writes=["rstd"])
                    P.emit("vector", lambda e, rows=rows: e.reciprocal(out=rstd[rows, :], in_=rstd[rows, :]), reads=["rstd"], writes=["rstd"])
                    P.emit("vector", lambda e, rows=rows, j=j: e.scalar_tensor_tensor(out=xh[rows, j, :], in0=xh[rows, j, :], scalar=rstd[rows, 0:1], in1=gf[rows, :], op0=ALU.mult, op1=ALU.mult),
                           reads=["xh", "rstd", "gf"], writes=["xh"])
                P.dma("sync", lambda e, j=j, r0=r0, R=R: e.dma_start(out=h_out[r0:r0 + R, :], in_=xh[0:R, j, :]), reads=["xh"], chan="xh")
        P.replay(nc, es)
    return nc


def fview(t, rows, start, step, count):
    a = t[rows, start:start + 1]
    return bass.AP(a.tensor, a.offset, [list(a.ap[0]), [step, count]])


def slot_range(slot):
    k, s = (slot % 8) // 2, slot % 2
    return slot // 8, 16 * k + s, 16 * k + 14 + s


def build_nsa():
    nc = bass.Bass("TRN2", target_bir_lowering=False)
    dt_in = lambda n, s: nc.dram_tensor(n, s, F32, kind="ExternalInput").ap()
    qT = dt_in("qT", [NPT, 2, 64, 1024])
    gates = dt_in("gates", [ROWS, 48])
    selKT = dt_in("selKT", [2, 2, 64, SEQ])
    selV = dt_in("selV", [2, 2, SEQ, 64])
    winKT = dt_in("winKT", [NPT, 2, 64, 640])
    winV = dt_in("winV", [NPT, 2, 640, 64])
    kbias = dt_in("kbias", [NPT, 128, 15, 128])
    wbias = dt_in("wbias", [NPT, 128, 5, 128])
    cmpA = dt_in("cmpA", [2, 2, 2, 128, SEQ])
    EK = dt_in("EK", [64, SEQ])
    topA = dt_in("topA", [NPT, 128, 128])
    topB = dt_in("topB", [NPT, 128, 128])
    cmpbias = dt_in("cmpbias", [NPT, 4, 128, 128])
    ovl = dt_in("ovl", [4, 128, 128])
    ropeC = dt_in("ropeC", [64, 512])
    ropeS = dt_in("ropeS", [64, 512])
    w1 = dt_in("w1", [2, 2048, 256])
    w2 = dt_in("w2", [2, 256, 64])
    peT = dt_in("peT", [2, 128, 16])
    o_out = nc.dram_tensor("o_out", [NPT * 128, D], F32, kind="ExternalOutput").ap()
    P = Prog()
    with ExitStack() as es:
        sb = lambda n, s, d: es.enter_context(nc.sbuf_tensor(n, s, d))
        ps = lambda n, s, d: es.enter_context(nc.psum_tensor(n, s, d))
        ident = make_ident(P, nc, es)
        w1b = sb("w1b", [128, 2, 16, 256], BF16)
        w2b = sb("w2b", [128, 2, 2, 64], BF16)
        w2R = sb("w2R", [128, 2, 64], BF16)
        peb = sb("peb", [128, 2, 16], BF16)
        rC = sb("rC", [64, 512], F32)
        rS = sb("rS", [64, 512], F32)
        kb1 = sb("kb1", [128, 15, 128], BF16)
        wb1 = sb("wb1", [128, 5, 128], BF16)
        kbr = [sb(f"kbr{i}", [128, 15, 4, 128], BF16) for i in range(2)]
        wbr = [sb(f"wbr{i}", [128, 5, 4, 128], BF16) for i in range(2)]
        b1 = sb("b1", [128, 2, 2], F32)
        kcT = sb("kcT", [64, 4, 512], BF16)
        rhsC = sb("rhsC", [128, 4, 4, 193], BF16)
        Acmp = sb("Acmp", [128, SEQ], BF16)
        hid = sb("hid", [128, 2, 512], BF16)
        t64 = [sb(f"t64_{i}", [64, 512], F32) for i in range(2)]
        KTa = [sb(f"KTa{i}", [128, SEQ], BF16) for i in range(2)]
        KwT = [sb(f"KwT{i}", [64, 640], BF16) for i in range(2)]
        Vs = [sb(f"Vs{i}", [128, 64, 65], BF16) for i in range(2)]
        Vw = [sb(f"Vw{i}", [128, 5, 65], BF16) for i in range(2)]
        QTa = [sb(f"QTa{i}", [128, 2, 1024], BF16) for i in range(2)]
        gts = [sb(f"gts{i}", [128, 48], F32) for i in range(2)]
        tA = [sb(f"tA{i}", [128, 128], F32) for i in range(2)]
        tB = [sb(f"tB{i}", [128, 128], F32) for i in range(2)]
        cb1 = [sb(f"cb1_{i}", [128, 128], BF16) for i in range(2)]
        cbr = [sb(f"cbr{i}", [128, 8, 128], BF16) for i in range(2)]
        pT = [sb(f"pT{i}", [128, 512], BF16) for i in range(3)]
        den = sb("den", [128, 8], F32)
        rden = sb("rden", [128, 8], F32)
        fsc = sb("fsc", [128, 8], F32)
        imp = sb("imp", [128, 128], F32)
        wk = sb("wk", [128, 128], F32)
        mx1 = sb("mx1", [128, 8], F32)
        mx2 = sb("mx2", [128, 8], F32)
        selb2 = sb("selb2", [128, 2, 128], BF16)
        otmp = sb("otmp", [128, 8, 64], F32)
        oacc = [sb(f"oacc{i}", [128, 8, 64], F32) for i in range(2)]
        sp = [ps(f"sp{i}", [128, 512], F32) for i in range(3)]
        acc = [ps(f"acc{i}", [128, 512], F32) for i in range(4)]
        tps = ps("tps", [128, 128], BF16)
        cp = ps

        for kv in range(2):
            for j in range(16):
                P.dma("gpsimd", lambda e, kv=kv, j=j: e.dma_start(out=w1b[:, kv, j, :], in_=w1[kv, j * 128:(j + 1) * 128, :]), writes=["w1b"], chan="w1b")
            for hc in range(2):
                P.dma("gpsimd", lambda e, kv=kv, hc=hc: e.dma_start(out=w2b[:, kv, hc, :], in_=w2[kv, hc * 128:(hc + 1) * 128, :]), writes=["w2b"], chan="w2b")
            P.dma("gpsimd", lambda e, kv=kv: e.dma_start(out=peb[:, kv, :], in_=peT[kv, :, :]), writes=["peb"], chan="peb")
        P.dma("sync", lambda e: e.dma_start(out=rC[:], in_=ropeC[:, :]), writes=["rC"], chan="rC")
        P.dma("sync", lambda e: e.dma_start(out=rS[:], in_=ropeS[:, :]), writes=["rS"], chan="rS")
        P.emit("vector", lambda e: e.tensor_scalar(out=w2R[:, :, 0:32], in0=w2b[:, 0, :, 32:64], scalar1=-1.0, scalar2=None, op0=ALU.mult), reads=["w2b"], writes=["w2R"])
        P.emit("vector", lambda e: e.tensor_copy(out=w2R[:, :, 32:64], in_=w2b[:, 0, :, 0:32]), reads=["w2b"], writes=["w2R"])
        P.emit("vector", lambda e: e.memset(kcT[:], 0.0), writes=["kcT"])
        P.emit("gpsimd", lambda e: e.memset(rhsC[:], 0.0), writes=["rhsC"])
        P.emit("vector", lambda e: e.memset(selb2[:], 0.0), writes=["selb2"])
        for i in range(2):
            P.emit("gpsimd", lambda e, i=i: e.memset(Vs[i][:, :, 64:65], 1.0), writes=[f"Vs{i}"])
            P.emit("gpsimd", lambda e, i=i: e.memset(Vw[i][:, :, 64:65], 1.0), writes=[f"Vw{i}"])
            for q4 in range(4):
                P.dma("gpsimd", lambda e, i=i, q4=q4: e.dma_start(out=KTa[i][64:128, q4 * 2048:(q4 + 1) * 2048], in_=EK[:, q4 * 2048:(q4 + 1) * 2048]), writes=[f"KTa{i}"], chan=f"KTa{i}")

        NCMP = 511
        spi = [0]

        def nsp():
            spi[0] += 1
            return spi[0] % 3

        for b in range(2):
            for kvh in range(2):
                combo = b * 2 + kvh
                for kv in range(2):
                    for q4 in range(4):
                        P.dma("gpsimd", lambda e, b=b, kvh=kvh, kv=kv, q4=q4: e.dma_start(out=Acmp[:, q4 * 2048:(q4 + 1) * 2048], in_=cmpA[b, kvh, kv, :, q4 * 2048:(q4 + 1) * 2048]),
                              writes=["Acmp"], chan="Acmp")
                    for hc in range(2):
                        s = nsp()
                        for j in range(16):
                            P.emit("tensor", lambda e, s=s, kv=kv, j=j, hc=hc: e.matmul(sp[s][:, 0:1], lhsT=w1b[:, kv, j, hc * 128:(hc + 1) * 128], rhs=peb[:, kv, j:j + 1], start=(j == 0), stop=(j == 15)),
                                   reads=["w1b", "peb"], writes=[f"sp{s}"])
                        P.emit("vector", lambda e, s=s, kv=kv, hc=hc: e.tensor_copy(out=b1[:, kv, hc:hc + 1], in_=sp[s][:, 0:1]), reads=[f"sp{s}"], writes=["b1"])
                        s = nsp()
                        for j in range(16):
                            P.emit("tensor", lambda e, s=s, kv=kv, j=j, hc=hc: e.matmul(sp[s][:, 0:NCMP], lhsT=w1b[:, kv, j, hc * 128:(hc + 1) * 128], rhs=fview(Acmp, slice(0, 128), 2 * j, 16, NCMP),
                                                                                     start=(j == 0), stop=(j == 15)), reads=["w1b", "Acmp"], writes=[f"sp{s}"])
                        P.emit("scalar", lambda e, s=s, kv=kv, hc=hc: e.activation(out=hid[:, hc, 0:NCMP], in_=sp[s][:, 0:NCMP], func=AF.Silu, bias=b1[:, kv, hc:hc + 1]),
                               reads=[f"sp{s}", "b1"], writes=["hid"])
                    if kv == 0:
                        s0, s1 = nsp(), nsp()
                        for hc in range(2):
                            P.emit("tensor", lambda e, s0=s0, hc=hc: e.matmul(sp[s0][0:64, 0:NCMP], lhsT=w2b[:, 0, hc, :], rhs=hid[:, hc, 0:NCMP], start=(hc == 0), stop=(hc == 1)),
                                   reads=["w2b", "hid"], writes=[f"sp{s0}"])
                        for hc in range(2):
                            P.emit("tensor", lambda e, s1=s1, hc=hc: e.matmul(sp[s1][0:64, 0:NCMP], lhsT=w2R[:, hc, :], rhs=hid[:, hc, 0:NCMP], start=(hc == 0), stop=(hc == 1)),
                                   reads=["w2R", "hid"], writes=[f"sp{s1}"])
                        P.emit("vector", lambda e, s0=s0: e.tensor_tensor(out=t64[0][:, 0:NCMP], in0=sp[s0][0:64, 0:NCMP], in1=rC[:, 0:NCMP], op=ALU.mult), reads=[f"sp{s0}", "rC"], writes=["t64_0"])
                        P.emit("vector", lambda e, s1=s1: e.tensor_tensor(out=t64[1][:, 0:NCMP], in0=sp[s1][0:64, 0:NCMP], in1=rS[:, 0:NCMP], op=ALU.mult), reads=[f"sp{s1}", "rS"], writes=["t64_1"])
                        P.emit("vector", lambda e, combo=combo: e.tensor_tensor(out=kcT[:, combo, 0:NCMP], in0=t64[0][:, 0:NCMP], in1=t64[1][:, 0:NCMP], op=ALU.add), reads=["t64_0", "t64_1"], writes=["kcT"])
                    else:
                        for ct in range(4):
                            nb = min(128, NCMP - ct * 128)
                            s = nsp()
                            for hc in range(2):
                                P.emit("tensor", lambda e, s=s, hc=hc, ct=ct, nb=nb: e.matmul(sp[s][0:nb, 0:64], lhsT=hid[:, hc, ct * 128:ct * 128 + nb], rhs=w2b[:, 1, hc, :], start=(hc == 0), stop=(hc == 1)),
                                       reads=["w2b", "hid"], writes=[f"sp{s}"])
                            P.emit("vector", lambda e, s=s, ct=ct, nb=nb, combo=combo: e.tensor_copy(out=rhsC[0:nb, combo, ct, 0:64], in_=sp[s][0:nb, 0:64]), reads=[f"sp{s}"], writes=["rhsC"])
                            P.emit("gpsimd", lambda e, ct=ct, nb=nb, combo=combo: e.memset(rhsC[0:nb, combo, ct, 64:65], 1.0), reads=[], writes=["rhsC"])
                            P.dma("gpsimd", lambda e, ct=ct, combo=combo: e.dma_start(out=rhsC[:, combo, ct, 65:193], in_=ovl[ct, :, :]), writes=["rhsC"], chan="rhsC")

        pti = [0]

        def npt():
            pti[0] += 1
            return pti[0] % 3

        def score_unit(lhsT, rhs, kparts, bias, kbuf, qbuf, biasname):
            s = nsp()
            P.emit("tensor", lambda e: e.matmul(sp[s][:, :], lhsT=lhsT, rhs=rhs, start=True, stop=(bias is None)), reads=[kbuf, qbuf], writes=[f"sp{s}"])
            if bias is not None:
                P.emit("tensor", lambda e: e.matmul(sp[s][:, :], lhsT=ident[:], rhs=bias, start=False, stop=True), reads=["ident", biasname], writes=[f"sp{s}"])
            p = npt()
            P.emit("scalar", lambda e: e.activation(out=pT[p][:], in_=sp[s][:], func=AF.Exp, scale=0.125), reads=[f"sp{s}"], writes=[f"pT{p}"])
            return p

        it = 0
        for b in range(2):
            for kvh in range(2):
                combo = b * 2 + kvh
                cb = combo % 2
                for q4 in range(4):
                    cs_ = slice(q4 * 2048, (q4 + 1) * 2048)
                    P.dma("gpsimd", lambda e, cb=cb, b=b, kvh=kvh, cs_=cs_: e.dma_start(out=KTa[cb][0:64, cs_], in_=selKT[b, kvh, :, cs_]), writes=[f"KTa{cb}"], chan=f"KTa{cb}")
                    ks = slice(q4 * 16, (q4 + 1) * 16)
                    P.dma("gpsimd", lambda e, cb=cb, b=b, kvh=kvh, ks=ks, q4=q4: e.dma_start(out=Vs[cb][:, ks, 0:64], in_=selV[b, kvh, q4 * 2048:(q4 + 1) * 2048, :].rearrange("(k p) d -> p k d", p=128)),
                          writes=[f"Vs{cb}"], chan=f"Vs{cb}")
                slots = [ti for ti in range(NPT) if slot_range(ti)[0] == b]

                def slot_loads(ti, ib, kvh=kvh):
                    r0 = ti * 128
                    for h in range(2):
                        P.dma("gpsimd", lambda e, ib=ib, ti=ti, kvh=kvh, h=h: e.dma_start(out=QTa[ib][0:64, h, :], in_=qT[ti, kvh, :, :]), writes=[f"QTa{ib}"], chan=f"QTa{ib}")
                    P.dma("sync", lambda e, ib=ib, r0=r0: e.dma_start(out=gts[ib][:], in_=gates[r0:r0 + 128, :]), writes=[f"gts{ib}"], chan=f"gts{ib}")
                    P.dma("sync", lambda e, ib=ib, ti=ti: e.dma_start(out=tA[ib][:], in_=topA[ti, :, :]), writes=[f"tA{ib}"], chan=f"tA{ib}")
                    P.dma("sync", lambda e, ib=ib, ti=ti: e.dma_start(out=tB[ib][:], in_=topB[ti, :, :]), writes=[f"tB{ib}"], chan=f"tB{ib}")
                    P.dma("gpsimd", lambda e, ti=ti: e.dma_start(out=kb1[:], in_=kbias[ti, :, :, :]), writes=["kb1"], chan="kb1")
                    P.emit("gpsimd", lambda e, ib=ib: e.tensor_copy(out=kbr[ib][:], in_=bc(kb1[:], [128, 15, 4, 128], 2)), reads=["kb1"], writes=[f"kbr{ib}"])
                    P.dma("gpsimd", lambda e, ti=ti: e.dma_start(out=wb1[:], in_=wbias[ti, :, :, :]), writes=["wb1"], chan="wb1")
                    P.emit("gpsimd", lambda e, ib=ib: e.tensor_copy(out=wbr[ib][:], in_=bc(wb1[:], [128, 5, 4, 128], 2)), reads=["wb1"], writes=[f"wbr{ib}"])
                    P.dma("gpsimd", lambda e, ib=ib, ti=ti, kvh=kvh: e.dma_start(out=KwT[ib][:, :], in_=winKT[ti, kvh, :, :]), writes=[f"KwT{ib}"], chan=f"KwT{ib}")
                    P.dma("gpsimd", lambda e, ib=ib, ti=ti, kvh=kvh: e.dma_start(out=Vw[ib][:, :, 0:64], in_=winV[ti, kvh, :, :].rearrange("(k p) d -> p k d", p=128)),
                          writes=[f"Vw{ib}"], chan=f"Vw{ib}")
                    return None

                def slot_compute(ti, ib, it, kvh=kvh, combo=combo, cb=cb):
                    tb_, qmin, qmax = slot_range(ti)
                    r0 = ti * 128
                    gv = gts[ib][:, kvh * 24:(kvh + 1) * 24].rearrange("p (g c) -> p g c", c=3)
                    nct = 4
                    pend = []

                    def pv_cmp(p, ct, hf, combo=combo, nct=nct):
                        for g4 in range(4):
                            g = hf * 4 + g4
                            P.emit("tensor", lambda e, p=p, g=g, g4=g4, ct=ct: e.matmul(
                                acc[g // 2][:, (g % 2) * 193:(g % 2) * 193 + 193], lhsT=pT[p][:, g4 * 128:(g4 + 1) * 128], rhs=rhsC[:, combo, ct, :],
                                start=(ct == 0 and g % 2 == 0), stop=(ct == nct - 1)), reads=[f"pT{p}", "rhsC"], writes=[f"acc{g // 2}"])

                    for ct in range(nct):
                        cbi = ct % 2
                        P.dma("gpsimd", lambda e, cbi=cbi, ti=ti, ct=ct: e.dma_start(out=cb1[cbi][:], in_=cmpbias[ti, ct, :, :]), writes=[f"cb1_{cbi}"], chan=f"cb1_{cbi}")
                        P.emit("gpsimd", lambda e, cbi=cbi: e.tensor_copy(out=cbr[cbi][:], in_=bc(cb1[cbi][:], [128, 8, 128], 1)), reads=[f"cb1_{cbi}"], writes=[f"cbr{cbi}"])
                        for hf in range(2):
                            p = score_unit(kcT[:, combo, ct * 128:(ct + 1) * 128], QTa[ib][0:64, 0, hf * 512:(hf + 1) * 512], 64,
                                           cbr[cbi][:, hf * 4:(hf + 1) * 4, :].rearrange("p g t -> p (g t)"), "kcT", f"QTa{ib}", f"cbr{cbi}")
                            pend.append((p, ct, hf))
                            if len(pend) > PIPE:
                                pv_cmp(*pend.pop(0))
                    while pend:
                        pv_cmp(*pend.pop(0))
                    for a in range(4):
                        P.emit("vector", lambda e, a=a: e.tensor_copy(out=den[:, 2 * a:2 * a + 2], in_=fview(acc[a], slice(0, 128), 64, 193, 2)), reads=[f"acc{a}"], writes=["den"])
                    P.emit("vector", lambda e: e.tensor_scalar(out=rden[:], in0=den[:], scalar1=1e-30, scalar2=None, op0=ALU.max), reads=["den"], writes=["rden"])
                    P.emit("vector", lambda e: e.reciprocal(out=rden[:], in_=rden[:]), reads=["rden"], writes=["rden"])
                    for g in range(8):
                        U = acc[g // 2][:, (g % 2) * 193 + 65:(g % 2) * 193 + 193]
                        if g == 0:
                            P.emit("vector", lambda e, U=U: e.tensor_scalar(out=imp[:], in0=U, scalar1=rden[:, 0:1], scalar2=None, op0=ALU.mult), reads=["acc0", "rden"], writes=["imp"])
                        else:
                            P.emit("vector", lambda e, U=U, g=g: e.scalar_tensor_tensor(out=imp[:], in0=U, scalar=rden[:, g:g + 1], in1=imp[:], op0=ALU.mult, op1=ALU.add),
                                   reads=[f"acc{g // 2}", "rden", "imp"], writes=["imp"])
                    ob_ = it % 2
                    P.emit("vector", lambda e, gv=gv: e.tensor_tensor(out=fsc[:], in0=rden[:], in1=gv[:, :, 0], op=ALU.mult), reads=["rden", f"gts{ib}"], writes=["fsc"])
                    for a in range(4):
                        ov = acc[a][:, 0:386].rearrange("p (g c) -> p g c", c=193)[:, :, 0:64]
                        P.emit("vector", lambda e, a=a, ov=ov, ob_=ob_: e.tensor_tensor(out=oacc[ob_][:, 2 * a:2 * a + 2, :], in0=ov, in1=bc(fsc[:, 2 * a:2 * a + 2], [128, 2, 64], 2), op=ALU.mult),
                               reads=[f"acc{a}", "fsc"], writes=[f"oacc{ob_}"])
                    pend = []

                    def pv_win(p, kt, hf, ib=ib):
                        for g4 in range(4):
                            P.emit("tensor", lambda e, p=p, g4=g4, hf=hf, kt=kt: e.matmul(acc[2 + hf][:, g4 * 65:(g4 + 1) * 65], lhsT=pT[p][:, g4 * 128:(g4 + 1) * 128], rhs=Vw[ib][:, kt, :],
                                                                                    start=(kt == 0 and g4 == 0), stop=(kt == 4)), reads=[f"pT{p}", f"Vw{ib}"], writes=[f"acc{2 + hf}"])

                    for kt in range(5):
                        for hf in range(2):
                            bias = wbr[ib][:, kt, :, :].rearrange("p g t -> p (g t)")
                            p = score_unit(KwT[ib][:, kt * 128:(kt + 1) * 128], QTa[ib][0:64, 0, hf * 512:(hf + 1) * 512], 64, bias, f"KwT{ib}", f"QTa{ib}", f"wbr{ib}")
                            pend.append((p, kt, hf))
                            if len(pend) > PIPE:
                                pv_win(*pend.pop(0))
                    while pend:
                        pv_win(*pend.pop(0))
                    P.emit("vector", lambda e, ib=ib: e.tensor_tensor(out=imp[:], in0=imp[:], in1=tA[ib][:], op=ALU.mult), reads=["imp", f"tA{ib}"], writes=["imp"])
                    P.emit("vector", lambda e, ib=ib: e.tensor_tensor(out=imp[:], in0=imp[:], in1=tB[ib][:], op=ALU.add), reads=["imp", f"tB{ib}"], writes=["imp"])
                    P.emit("vector", lambda e: e.max(out=mx1[:], in_=imp[:]), reads=["imp"], writes=["mx1"])
                    P.emit("vector", lambda e: e.match_replace(out=wk[:], in_to_replace=mx1[:], in_values=imp[:], imm_value=-1e30), reads=["imp", "mx1"], writes=["wk"])
                    P.emit("vector", lambda e: e.max(out=mx2[:], in_=wk[:]), reads=["wk"], writes=["mx2"])
                    P.emit("vector", lambda e: e.tensor_scalar(out=selb2[:, :, 64:128], in0=imp[:].rearrange("p (h j) -> p h j", h=2), scalar1=mx2[:, 7:8], scalar2=NEGB,
                                                               op0=ALU.is_lt, op1=ALU.mult), reads=["imp", "mx2"], writes=["selb2"])
                    for h in range(2):
                        P.emit("tensor", lambda e, h=h: e.transpose(out=tps[:], in_=selb2[:, h, :], identity=ident[:]), reads=["selb2", "ident"], writes=["tps"])
                        P.emit("scalar", lambda e, h=h, ib=ib: e.copy(out=QTa[ib][64:128, h, :].rearrange("p (g t) -> p g t", g=8), in_=bc(tps[64:128, :], [64, 8, 128], 1)),
                               reads=["tps"], writes=[f"QTa{ib}"])
                    pend = []

                    def pv_sel(p, kt, hf, qmax=qmax, cb=cb):
                        for g4 in range(4):
                            P.emit("tensor", lambda e, p=p, g4=g4, hf=hf, kt=kt: e.matmul(acc[hf][:, g4 * 65:(g4 + 1) * 65], lhsT=pT[p][:, g4 * 128:(g4 + 1) * 128], rhs=Vs[cb][:, kt, :],
                                                                                    start=(kt == 0 and g4 == 0), stop=(kt == qmax)), reads=[f"pT{p}", f"Vs{cb}"], writes=[f"acc{hf}"])

                    for kt in range(qmax + 1):
                        half = kt // 32
                        for hf in range(2):
                            bias = kbr[ib][:, kt - qmin, :, :].rearrange("p g t -> p (g t)") if kt >= qmin else None
                            p = score_unit(KTa[cb][:, kt * 128:(kt + 1) * 128], QTa[ib][:, half, hf * 512:(hf + 1) * 512], 128, bias, f"KTa{cb}", f"QTa{ib}", f"kbr{ib}")
                            pend.append((p, kt, hf))
                            if len(pend) > PIPE:
                                pv_sel(*pend.pop(0))
                    while pend:
                        pv_sel(*pend.pop(0))
                    for br, a0 in ((1, 0), (2, 2)):
                        for hf in range(2):
                            a = a0 + hf
                            P.emit("vector", lambda e, a=a, hf=hf: e.tensor_copy(out=den[:, hf * 4:(hf + 1) * 4], in_=fview(acc[a], slice(0, 128), 64, 65, 4)), reads=[f"acc{a}"], writes=["den"])
                        P.emit("vector", lambda e: e.tensor_scalar(out=rden[:], in0=den[:], scalar1=1e-30, scalar2=None, op0=ALU.max), reads=["den"], writes=["rden"])
                        P.emit("vector", lambda e: e.reciprocal(out=rden[:], in_=rden[:]), reads=["rden"], writes=["rden"])
                        P.emit("vector", lambda e, gv=gv, br=br: e.tensor_tensor(out=fsc[:], in0=rden[:], in1=gv[:, :, br], op=ALU.mult), reads=["rden", f"gts{ib}"], writes=["fsc"])
                        for hf in range(2):
                            a = a0 + hf
                            ov = acc[a][:, 0:260].rearrange("p (g c) -> p g c", c=65)[:, :, 0:64]
                            P.emit("vector", lambda e, a=a, hf=hf, ov=ov: e.tensor_tensor(out=otmp[:, hf * 4:(hf + 1) * 4, :], in0=ov, in1=bc(fsc[:, hf * 4:(hf + 1) * 4], [128, 4, 64], 2), op=ALU.mult),
                                   reads=[f"acc{a}", "fsc"], writes=["otmp"])
                        P.emit("gpsimd", lambda e, ob_=ob_: e.tensor_tensor(out=oacc[ob_][:], in0=oacc[ob_][:], in1=otmp[:], op=ALU.add), reads=[f"oacc{ob_}", "otmp"], writes=[f"oacc{ob_}"])
                    P.dma("sync", lambda e, ob_=ob_, r0=r0, kvh=kvh: e.dma_start(out=o_out[r0:r0 + 128, kvh * 512:(kvh + 1) * 512], in_=oacc[ob_][:].rearrange("p g d -> p (g d)")),
                          reads=[f"oacc{ob_}"], chan=f"oacc{ob_}")

                ibs = [(it + n) % 2 for n in range(len(slots))]
                slot_loads(slots[0], ibs[0])
                for n, ti in enumerate(slots):
                    if n + 1 < len(slots):
                        slot_loads(slots[n + 1], ibs[n + 1])
                    slot_compute(ti, ibs[n], it + n + 1)
                it += len(slots)
        P.replay(nc, es)
    return nc


def nsa_consts(core):
    tiles = core_tiles(core)
    tok = np.arange(128)
    j = np.arange(128)
    tA = np.zeros((NPT, 128, 128), np.float32)
    tB = np.zeros((NPT, 128, 128), np.float32)
    cbias = np.full((NPT, 4, 128, 128), NEGB, np.float32)
    for ti, (_, p0) in enumerate(tiles):
        qp = (p0 + tok)[:, None]
        cur = qp // 64
        forced = (j[None, :] == 0) | (j[None, :] == cur) | (j[None, :] == cur - 1)
        causal = j[None, :] * 64 <= qp
        tA[ti] = (causal & ~forced).astype(np.float32)
        tB[ti] = np.where(forced, 1e6, np.where(causal, 0.0, -1.0)).astype(np.float32)
        for ct in range(4):
            i = ct * 128 + np.arange(128)
            ok = (i[:, None] * 16 + 31 <= (p0 + tok)[None, :]) & (i[:, None] < 511)
            cbias[ti, ct] = np.where(ok, 0.0, NEGB)
    i = np.arange(512)
    ov = ((i[:, None] * 16 < (j[None, :] + 1) * 64) & (i[:, None] * 16 + 32 > j[None, :] * 64) & (i[:, None] < 511)).astype(np.float32)
    key = np.arange(SEQ)
    EK = ((key[None, :] // 64) % 64 == np.arange(64)[:, None]).astype(np.float32)
    inv = (THETA ** (-np.arange(32, dtype=np.float32) / 32)).astype(np.float32)
    ang = (np.arange(512) * 16).astype(np.float32)[None, :] * np.tile(inv, 2)[:, None]
    kk = np.arange(128)
    kb = np.zeros((NPT, 128, 15, 128), np.float32)
    wb = np.zeros((NPT, 128, 5, 128), np.float32)
    for ti, (_, p0) in enumerate(tiles):
        _, qmin, qmax = slot_range(ti)
        qp = (p0 + tok)[None, :]
        for jj in range(15):
            kp = ((qmin + jj) * 128 + kk)[:, None]
            kb[ti, :, jj, :] = np.where(kp <= qp, 0.0, NEGB)
        for r in range(5):
            kp = (p0 - 512 + r * 128 + kk)[:, None]
            wb[ti, :, r, :] = np.where((kp <= qp) & (kp > qp - WIN) & (kp >= 0), 0.0, NEGB)
    return {"topA": tA, "topB": tB, "cmpbias": cbias, "ovl": np.ascontiguousarray(ov.reshape(4, 128, 128)), "EK": EK,
            "ropeC": np.cos(ang).astype(np.float32), "ropeS": np.sin(ang).astype(np.float32), "kbias": kb, "wbias": wb}


def run_nsa_prompt(zp, w1, w2, pe):
    T = lambda a: np.ascontiguousarray(a)
    kvp = zp[:, :, 1024:1792].reshape(2, SEQ, 3, 2, 2, 64)
    selKT = T(kvp[:, :, 1, 0].transpose(0, 2, 3, 1))
    selV = T(kvp[:, :, 1, 1].transpose(0, 2, 1, 3))
    wpad = np.concatenate([np.zeros((2, WIN, 2, 2, 64), np.float32), kvp[:, :, 2]], axis=1)
    ck = kvp[:, :, 0].transpose(0, 3, 2, 4, 1)
    cmpA = np.zeros((2, 2, 2, 128, SEQ), np.float32)
    cmpA[:, :, :, 0:64, :] = ck
    cmpA[:, :, :, 64:128, :SEQ - 1] = ck[..., 1:]
    peT = T(pe.reshape(2, 16, 2, 64).transpose(0, 2, 3, 1).reshape(2, 128, 16))
    in_maps, ncs = [], []
    for c in range(NCORES):
        tiles = core_tiles(c)
        qT = np.stack([zp[b, p0:p0 + 128, 0:1024].reshape(128, 2, 8, 64).transpose(1, 3, 2, 0).reshape(2, 64, 1024) for b, p0 in tiles])
        gates = np.zeros((ROWS, 48), np.float32)
        for ti, (b, p0) in enumerate(tiles):
            gates[ti * 128:(ti + 1) * 128] = zp[b, p0:p0 + 128, 1792:1840]
        winKT = T(np.stack([wpad[b, p0:p0 + 640, 0].transpose(1, 2, 0) for b, p0 in tiles]))
        winV = T(np.stack([wpad[b, p0:p0 + 640, 1].transpose(1, 0, 2) for b, p0 in tiles]))
        m = {"qT": T(qT), "gates": gates, "selKT": selKT, "selV": selV, "winKT": winKT, "winV": winV, "cmpA": cmpA,
             "w1": T(w1), "w2": T(w2), "peT": peT}
        m.update(nsa_consts(c))
        in_maps.append(m)
    if "nsa" not in _NC_CACHE:
        _NC_CACHE["nsa"] = build_nsa()
    res = run_bass_kernel_spmd(_NC_CACHE["nsa"], in_maps, core_ids=list(range(NCORES))).results
    op = np.zeros((2, SEQ, D), np.float32)
    for c in range(NCORES):
        for ti, (b_, p0) in enumerate(core_tiles(c)):
            op[b_, p0:p0 + 128] = res[c]["o_out"][ti * 128:(ti + 1) * 128]
    return op


NPOOL = 2560
NPG = PAST // 128


def build_nsa_s():
    nc = bass.Bass("TRN2", target_bir_lowering=False)
    dt_in = lambda n, s, d=F32: nc.dram_tensor(n, s, d, kind="ExternalInput").ap()
    pool_cmp = dt_in("pool_cmp", [NPOOL, 128, 128])
    pool_sel = dt_in("pool_sel", [NPOOL, 128, 128])
    ptl = dt_in("ptl", [128, 8 * 8], I32)
    pmod = dt_in("pmod", [128, 1], I32)
    qT = dt_in("qT", [8, 64, 64])
    gates = dt_in("gates", [8, 8, 24])
    knewT = dt_in("knewT", [8, 64, 8])
    vnew = dt_in("vnew", [8, 8, 64])
    wKT = dt_in("wKT", [8, 64, 520])
    wV = dt_in("wV", [8, 520, 64])
    wrows = dt_in("wrows", [8, 520, 128])
    EK = dt_in("EK", [64, SEQ])
    topA = dt_in("topA", [8, 128])
    topB = dt_in("topB", [8, 128])
    ovl = dt_in("ovl", [4, 128, 128])
    ropeC = dt_in("ropeC", [64, 512])
    ropeS = dt_in("ropeS", [64, 512])
    causb = dt_in("causb", [8, 64])
    wbias = dt_in("wbias", [128, 5, 64])
    w1s = dt_in("w1s", [128, 32, 256])
    w2 = dt_in("w2", [2, 256, 64])
    pes = dt_in("pes", [128, 32])
    o_out = nc.dram_tensor("o_out", [64, 512], F32, kind="ExternalOutput").ap()
    wins_out = nc.dram_tensor("wins_out", [8, WIN, 128], F32, kind="ExternalOutput").ap()
    P = Prog()
    with ExitStack() as es:
        sb = lambda n, s, d: es.enter_context(nc.sbuf_tensor(n, s, d))
        ps = lambda n, s, d: es.enter_context(nc.psum_tensor(n, s, d))
        ident = make_ident(P, nc, es)
        ptb = sb("ptb", [128, 8 * 8], I32)
        pidx = sb("pidx", [128, 8 * 8], I32)
        piota = sb("piota", [128, 1], I32)
        w1b = sb("w1b", [128, 32, 256], BF16)
        w2b = sb("w2b", [128, 2, 2, 64], BF16)
        w2R = sb("w2R", [128, 2, 64], BF16)
        peb = sb("peb", [128, 32], BF16)
        rC = sb("rC", [64, 512], F32)
        rS = sb("rS", [64, 512], F32)
        tA = sb("tA", [8, 128], F32)
        tB = sb("tB", [8, 128], F32)
        cbs = sb("cbs", [8, 64], BF16)
        wbs = sb("wbs", [128, 5, 64], BF16)
        b1 = sb("b1", [128, 2, 2], F32)
        kcT = sb("kcT", [64, 512], BF16)
        rhsC = sb("rhsC", [128, 4, 193], BF16)
        cTok = [sb(f"cTok{i}", [128, NPG, 128], BF16) for i in range(2)]
        sTok = [sb(f"sTok{i}", [128, NPG, 128], BF16) for i in range(2)]
        AT = sb("AT", [128, SEQ], BF16)
        KTa = sb("KTa", [128, SEQ], BF16)
        Vs = sb("Vs", [128, NPG, 65], BF16)
        hid = sb("hid", [128, 2, 512], BF16)
        t64 = [sb(f"t64_{i}", [64, 512], F32) for i in range(2)]
        QTa = sb("QTa", [128, 2, 64], BF16)
        gts = sb("gts", [8, 24], F32)
        knT = sb("knT", [64, 8], BF16)
        vnA = sb("vnA", [8, 65], BF16)
        KwT = sb("KwT", [64, 640], BF16)
        Vw = sb("Vw", [128, 5, 65], BF16)
        pT = [sb(f"pT{i}", [128, 256], BF16) for i in range(3)]
        den = sb("den", [8, 8], F32)
        rden = sb("rden", [8, 8], F32)
        fsc = sb("fsc", [8, 8], F32)
        imp = sb("imp", [8, 128], F32)
        wk = sb("wk", [8, 128], F32)
        mx1 = sb("mx1", [8, 8], F32)
        mx2 = sb("mx2", [8, 8], F32)
        selb2 = sb("selb2", [8, 2, 128], BF16)
        otmp = sb("otmp", [8, 8, 64], F32)
        oacc = [sb(f"oacc{i}", [8, 8, 64], F32) for i in range(2)]
        tpp = [ps(f"tpp{i}", [128, 4, 128], BF16) for i in range(2)]
        sp = [ps(f"sp{i}", [128, 512], F32) for i in range(2)]
        acc = [ps(f"acc{i}", [128, 512], F32) for i in range(4)]

        P.dma("gpsimd", lambda e: e.dma_start(out=ptb[:], in_=ptl[:, :]), writes=["ptb"], chan="ptb")
        P.dma("gpsimd", lambda e: e.dma_start(out=piota[:], in_=pmod[:, :]), writes=["piota"], chan="piota")
        P.emit("gpsimd", lambda e: e.tensor_scalar(out=pidx[:], in0=ptb[:], scalar1=16, scalar2=None, op0=ALU.mult), reads=["ptb"], writes=["pt"])
        P.emit("gpsimd", lambda e: e.tensor_tensor(out=pidx[:], in0=pidx[:], in1=piota[:].to_broadcast([128, 8 * 8]), op=ALU.add), reads=["pt", "piota"], writes=["pt"])
        rows_cmp = pool_cmp.rearrange("n (g r) c -> (n g) (r c)", r=8)
        rows_sel = pool_sel.rearrange("n (g r) c -> (n g) (r c)", r=8)
        for r4 in range(4):
            P.dma("gpsimd", lambda e, r4=r4: e.dma_start(out=w1b[:, r4 * 8:(r4 + 1) * 8, :], in_=w1s[:, r4 * 8:(r4 + 1) * 8, :]), writes=["w1b"], chan="w1b")
        for kv in range(2):
            for hc in range(2):
                P.dma("gpsimd", lambda e, kv=kv, hc=hc: e.dma_start(out=w2b[:, kv, hc, :], in_=w2[kv, hc * 128:(hc + 1) * 128, :]), writes=["w2b"], chan="w2b")
        P.dma("gpsimd", lambda e: e.dma_start(out=peb[:], in_=pes[:, :]), writes=["peb"], chan="peb")
        P.dma("gpsimd", lambda e: e.dma_start(out=cbs[:], in_=causb[:, :]), writes=["cbs"], chan="cbs")
        P.dma("gpsimd", lambda e: e.dma_start(out=wbs[:], in_=wbias[:, :, :]), writes=["wbs"], chan="wbs")
        P.dma("sync", lambda e: e.dma_start(out=rC[:], in_=ropeC[:, :]), writes=["rC"], chan="rC")
        P.dma("sync", lambda e: e.dma_start(out=rS[:], in_=ropeS[:, :]), writes=["rS"], chan="rS")
        P.dma("sync", lambda e: e.dma_start(out=tA[:], in_=topA[:, :]), writes=["tA"], chan="tA")
        P.dma("sync", lambda e: e.dma_start(out=tB[:], in_=topB[:, :]), writes=["tB"], chan="tB")
        P.emit("vector", lambda e: e.tensor_scalar(out=w2R[:, :, 0:32], in0=w2b[:, 0, :, 32:64], scalar1=-1.0, scalar2=None, op0=ALU.mult), reads=["w2b"], writes=["w2R"])
        P.emit("vector", lambda e: e.tensor_copy(out=w2R[:, :, 32:64], in_=w2b[:, 0, :, 0:32]), reads=["w2b"], writes=["w2R"])
        P.emit("vector", lambda e: e.memset(kcT[:], 0.0), writes=["kcT"])
        P.emit("gpsimd", lambda e: e.memset(rhsC[:], 0.0), writes=["rhsC"])
        P.emit("vector", lambda e: e.memset(selb2[:], 0.0), writes=["selb2"])
        P.emit("gpsimd", lambda e: e.memset(Vs[:, :, 64:65], 1.0), writes=["Vs"])
        P.emit("gpsimd", lambda e: e.memset(Vw[:, :, 64:65], 1.0), writes=["Vw"])
        P.emit("gpsimd", lambda e: e.memset(vnA[:, 64:65], 1.0), writes=["vnA"])
        for ct in range(4):
            P.dma("gpsimd", lambda e, ct=ct: e.dma_start(out=rhsC[:, ct, 65:193], in_=ovl[ct, :, :]), writes=["rhsC"], chan="rhsC")
        for q4 in range(4):
            P.dma("gpsimd", lambda e, q4=q4: e.dma_start(out=KTa[64:128, q4 * 2048:(q4 + 1) * 2048], in_=EK[:, q4 * 2048:(q4 + 1) * 2048]), writes=["KTa"], chan="KTa")
        NCMP = 511
        cnt = {"sp": 0, "pt": 0, "tp": 0, "cp": 0}

        def nxt(k, n):
            cnt[k] += 1
            return cnt[k] % n

        def gather(j, bi):
            for d in range(8):
                for (prow, dst, nm) in ((rows_cmp, cTok, "cTok"), (rows_sel, sTok, "sTok")):
                    def g(e, j=j, d=d, prow=prow, dst=dst, bi=bi):
                        return e.indirect_dma_start(out=dst[bi][:, d * 8:(d + 1) * 8, :].rearrange("p r c -> p (r c)"), out_offset=None, in_=prow[:, :],
                                                    in_offset=bass.IndirectOffsetOnAxis(ap=pidx[:, j * 8 + d:j * 8 + d + 1], axis=0))
                    P.dma("gpsimd", g, reads=["pt"], writes=[f"{nm}{bi}"], chan=f"{nm}{bi}")

        gather(0, 0)
        for j in range(8):
            bi = j % 2
            if j + 1 < 8:
                gather(j + 1, 1 - bi)
            for (src_, dstT, nm, rows) in ((cTok, AT, "AT", slice(0, 128)), (sTok, KTa, "KTa", slice(0, 64))):
                for p4 in range(NPG // 4):
                    tb = nxt("tp", 2)
                    for q in range(4):
                        P.emit("tensor", lambda e, tb=tb, q=q, p4=p4, src_=src_, bi=bi: e.transpose(out=tpp[tb][:, q, :], in_=src_[bi][:, p4 * 4 + q, :], identity=ident[:]),
                               reads=[f"{'cTok' if nm == 'AT' else 'sTok'}{bi}", "ident"], writes=[f"tpp{tb}"])
                    eng = "scalar" if nxt("cp", 2) else "vector"
                    if nm == "AT":
                        d_, r0 = (p4 * 4) // 8, (p4 * 4) % 8
                        a_ = dstT[rows, d_ * 1024 + r0:d_ * 1024 + r0 + 1]
                        o_ap = bass.AP(a_.tensor, a_.offset, [list(a_.ap[0]), [1, 4], [128, 8], [8, 16]])
                        i_ap = tpp[tb][rows, :, :].rearrange("p q (s g) -> p q s g", s=8)
                    else:
                        o_ap = dstT[rows, p4 * 512:(p4 + 1) * 512]
                        i_ap = tpp[tb][rows, :, :].rearrange("p q t -> p (q t)")
                    if eng == "scalar":
                        P.emit("scalar", lambda e, o_ap=o_ap, i_ap=i_ap: e.copy(out=o_ap, in_=i_ap), reads=[f"tpp{tb}"], writes=[nm])
                    else:
                        P.emit("vector", lambda e, o_ap=o_ap, i_ap=i_ap: e.tensor_copy(out=o_ap, in_=i_ap), reads=[f"tpp{tb}"], writes=[nm])
            P.emit("gpsimd", lambda e, bi=bi: e.tensor_copy(out=Vs[:, :, 0:64], in_=sTok[bi][:, :, 64:128]), reads=[f"sTok{bi}"], writes=["Vs"])
            for kv in range(2):
                pr = slice(kv * 64, (kv + 1) * 64)
                for hc in range(2):
                    s = nxt("sp", 2)
                    for r in range(32):
                        P.emit("tensor", lambda e, s=s, r=r, hc=hc, pr=pr: e.matmul(sp[s][:, 0:1], lhsT=w1b[pr, r, hc * 128:(hc + 1) * 128], rhs=peb[pr, r:r + 1], start=(r == 0), stop=(r == 31)),
                               reads=["w1b", "peb"], writes=[f"sp{s}"])
                    P.emit("vector", lambda e, s=s, kv=kv, hc=hc: e.tensor_copy(out=b1[:, kv, hc:hc + 1], in_=sp[s][:, 0:1]), reads=[f"sp{s}"], writes=["b1"])
                    s = nxt("sp", 2)
                    for r in range(32):
                        P.emit("tensor", lambda e, s=s, r=r, hc=hc, pr=pr: e.matmul(sp[s][:, 0:NCMP], lhsT=w1b[pr, r, hc * 128:(hc + 1) * 128], rhs=fview(AT, pr, r, 16, NCMP), start=(r == 0), stop=(r == 31)),
                               reads=["w1b", "AT"], writes=[f"sp{s}"])
                    P.emit("scalar", lambda e, s=s, kv=kv, hc=hc: e.activation(out=hid[:, hc, 0:NCMP], in_=sp[s][:, 0:NCMP], func=AF.Silu, bias=b1[:, kv, hc:hc + 1]), reads=[f"sp{s}", "b1"], writes=["hid"])
                if kv == 0:
                    for hc in range(2):
                        P.emit("tensor", lambda e, hc=hc: e.matmul(sp[0][0:64, 0:NCMP], lhsT=w2b[:, 0, hc, :], rhs=hid[:, hc, 0:NCMP], start=(hc == 0), stop=(hc == 1)), reads=["w2b", "hid"], writes=["sp0"])
                    for hc in range(2):
                        P.emit("tensor", lambda e, hc=hc: e.matmul(sp[1][0:64, 0:NCMP], lhsT=w2R[:, hc, :], rhs=hid[:, hc, 0:NCMP], start=(hc == 0), stop=(hc == 1)), reads=["w2R", "hid"], writes=["sp1"])
                    P.emit("vector", lambda e: e.tensor_tensor(out=t64[0][:, 0:NCMP], in0=sp[0][0:64, 0:NCMP], in1=rC[:, 0:NCMP], op=ALU.mult), reads=["sp0", "rC"], writes=["t64_0"])
                    P.emit("vector", lambda e: e.tensor_tensor(out=t64[1][:, 0:NCMP], in0=sp[1][0:64, 0:NCMP], in1=rS[:, 0:NCMP], op=ALU.mult), reads=["sp1", "rS"], writes=["t64_1"])
                    P.emit("vector", lambda e: e.tensor_tensor(out=kcT[:, 0:NCMP], in0=t64[0][:, 0:NCMP], in1=t64[1][:, 0:NCMP], op=ALU.add), reads=["t64_0", "t64_1"], writes=["kcT"])
                else:
                    for ct in range(4):
                        nb = min(128, NCMP - ct * 128)
                        s = nxt("sp", 2)
                        for hc in range(2):
                            P.emit("tensor", lambda e, s=s, hc=hc, ct=ct, nb=nb: e.matmul(sp[s][0:nb, 0:64], lhsT=hid[:, hc, ct * 128:ct * 128 + nb], rhs=w2b[:, 1, hc, :], start=(hc == 0), stop=(hc == 1)),
                                   reads=["w2b", "hid"], writes=[f"sp{s}"])
                        P.emit("vector", lambda e, s=s, ct=ct, nb=nb: e.tensor_copy(out=rhsC[0:nb, ct, 0:64], in_=sp[s][0:nb, 0:64]), reads=[f"sp{s}"], writes=["rhsC"])
                        P.emit("gpsimd", lambda e, ct=ct, nb=nb: e.memset(rhsC[0:nb, ct, 64:65], 1.0), writes=["rhsC"])
            for h in range(2):
                P.dma("gpsimd", lambda e, j=j, h=h: e.dma_start(out=QTa[0:64, h, :], in_=qT[j, :, :]), writes=["QTa"], chan="QTa")
            P.dma("sync", lambda e, j=j: e.dma_start(out=gts[:], in_=gates[j, :, :]), writes=["gts"], chan="gts")
            P.dma("gpsimd", lambda e, j=j: e.dma_start(out=knT[:], in_=knewT[j, :, :]), writes=["knT"], chan="knT")
            P.dma("gpsimd", lambda e, j=j: e.dma_start(out=vnA[:, 0:64], in_=vnew[j, :, :]), writes=["vnA"], chan="vnA")
            P.emit("gpsimd", lambda e: e.memset(KwT[:], 0.0), writes=["KwT"])
            P.emit("gpsimd", lambda e: e.memset(Vw[:, 4, 0:64], 0.0), writes=["Vw"])
            P.dma("gpsimd", lambda e, j=j: e.dma_start(out=KwT[:, 0:520], in_=wKT[j, :, :]), writes=["KwT"], chan="KwT")
            P.dma("gpsimd", lambda e, j=j: e.dma_start(out=Vw[:, 0:4, 0:64], in_=wV[j, 0:512, :].rearrange("(k p) d -> p k d", p=128)), writes=["Vw"], chan="Vw")
            P.dma("gpsimd", lambda e, j=j: e.dma_start(out=Vw[0:8, 4, 0:64], in_=wV[j, 512:520, :]), writes=["Vw"], chan="Vw")
            P.dma("sync", lambda e, j=j: e.dma_start(out=wins_out[j, :, :], in_=wrows[j, DEC_T:DEC_T + WIN, :]), chan="wins")
            gv = gts[:, :].rearrange("p (g c) -> p g c", c=3)
            for ct in range(4):
                s = nxt("sp", 2)
                P.emit("tensor", lambda e, s=s, ct=ct: e.matmul(sp[s][:, 0:64], lhsT=kcT[:, ct * 128:(ct + 1) * 128], rhs=QTa[0:64, 0, :], start=True, stop=True), reads=["kcT", "QTa"], writes=[f"sp{s}"])
                p = nxt("pt", 3)
                P.emit("scalar", lambda e, s=s, p=p: e.activation(out=pT[p][:, 0:64], in_=sp[s][:, 0:64], func=AF.Exp, scale=0.125), reads=[f"sp{s}"], writes=[f"pT{p}"])
                for g in range(8):
                    P.emit("tensor", lambda e, p=p, g=g, ct=ct: e.matmul(acc[g // 2][0:8, (g % 2) * 193:(g % 2) * 193 + 193], lhsT=pT[p][:, g * 8:(g + 1) * 8], rhs=rhsC[:, ct, :],
                                                                   start=(ct == 0 and g % 2 == 0), stop=(ct == 3)), reads=[f"pT{p}", "rhsC"], writes=[f"acc{g // 2}"])
            r8 = slice(0, 8)
            for a in range(4):
                P.emit("vector", lambda e, a=a: e.tensor_copy(out=den[:, 2 * a:2 * a + 2], in_=fview(acc[a], r8, 64, 193, 2)), reads=[f"acc{a}"], writes=["den"])
            P.emit("vector", lambda e: e.tensor_scalar(out=rden[:], in0=den[:], scalar1=1e-30, scalar2=None, op0=ALU.max), reads=["den"], writes=["rden"])
            P.emit("vector", lambda e: e.reciprocal(out=rden[:], in_=rden[:]), reads=["rden"], writes=["rden"])
            for g in range(8):
                U = acc[g // 2][0:8, (g % 2) * 193 + 65:(g % 2) * 193 + 193]
                if g == 0:
                    P.emit("vector", lambda e, U=U: e.tensor_scalar(out=imp[:], in0=U, scalar1=rden[:, 0:1], scalar2=None, op0=ALU.mult), reads=["acc0", "rden"], writes=["imp"])
                else:
                    P.emit("vector", lambda e, U=U, g=g: e.scalar_tensor_tensor(out=imp[:], in0=U, scalar=rden[:, g:g + 1], in1=imp[:], op0=ALU.mult, op1=ALU.add), reads=[f"acc{g // 2}", "rden", "imp"], writes=["imp"])
            ob_ = j % 2
            P.emit("vector", lambda e, gv=gv: e.tensor_tensor(out=fsc[:], in0=rden[:], in1=gv[:, :, 0], op=ALU.mult), reads=["rden", "gts"], writes=["fsc"])
            for a in range(4):
                ov = acc[a][0:8, 0:386].rearrange("p (g c) -> p g c", c=193)[:, :, 0:64]
                P.emit("vector", lambda e, a=a, ov=ov, ob_=ob_: e.tensor_tensor(out=oacc[ob_][:, 2 * a:2 * a + 2, :], in0=ov, in1=bc(fsc[:, 2 * a:2 * a + 2], [8, 2, 64], 2), op=ALU.mult), reads=[f"acc{a}", "fsc"], writes=[f"oacc{ob_}"])
            P.emit("vector", lambda e: e.tensor_tensor(out=imp[:], in0=imp[:], in1=tA[:], op=ALU.mult), reads=["imp", "tA"], writes=["imp"])
            P.emit("vector", lambda e: e.tensor_tensor(out=imp[:], in0=imp[:], in1=tB[:], op=ALU.add), reads=["imp", "tB"], writes=["imp"])
            P.emit("vector", lambda e: e.max(out=mx1[:], in_=imp[:]), reads=["imp"], writes=["mx1"])
            P.emit("vector", lambda e: e.match_replace(out=wk[:], in_to_replace=mx1[:], in_values=imp[:], imm_value=-1e30), reads=["imp", "mx1"], writes=["wk"])
            P.emit("vector", lambda e: e.max(out=mx2[:], in_=wk[:]), reads=["wk"], writes=["mx2"])
            P.emit("vector", lambda e: e.tensor_scalar(out=selb2[:, :, 64:128], in0=imp[:].rearrange("p (h j) -> p h j", h=2), scalar1=mx2[:, 6:7], scalar2=NEGB, op0=ALU.is_lt, op1=ALU.mult),
                   reads=["imp", "mx2"], writes=["selb2"])
            for h in range(2):
                tb = nxt("tp", 2)
                P.emit("tensor", lambda e, h=h, tb=tb: e.transpose(out=tpp[tb][:, 0, 0:8], in_=selb2[:, h, :], identity=ident[0:8, 0:8]), reads=["selb2", "ident"], writes=[f"tpp{tb}"])
                P.emit("scalar", lambda e, h=h, tb=tb: e.copy(out=QTa[64:128, h, :].rearrange("p (g t) -> p g t", g=8), in_=bc(tpp[tb][64:128, 0, 0:8], [64, 8, 8], 1)), reads=[f"tpp{tb}"], writes=["QTa"])
            for k4 in range(NPG // 4):
                s = nxt("sp", 2)
                for q in range(4):
                    kt = k4 * 4 + q
                    P.emit("tensor", lambda e, s=s, q=q, kt=kt: e.matmul(sp[s][:, q * 64:(q + 1) * 64], lhsT=KTa[:, kt * 128:(kt + 1) * 128], rhs=QTa[:, kt // 32, :], start=True, stop=True),
                           reads=["KTa", "QTa"], writes=[f"sp{s}"])
                p = nxt("pt", 3)
                P.emit("scalar", lambda e, s=s, p=p: e.activation(out=pT[p][:, :], in_=sp[s][:, 0:256], func=AF.Exp, scale=0.125), reads=[f"sp{s}"], writes=[f"pT{p}"])
                for q in range(4):
                    kt = k4 * 4 + q
                    for g in range(8):
                        P.emit("tensor", lambda e, p=p, q=q, g=g, kt=kt: e.matmul(acc[g // 4][0:8, (g % 4) * 65:(g % 4) * 65 + 65], lhsT=pT[p][:, q * 64 + g * 8:q * 64 + (g + 1) * 8], rhs=Vs[:, kt, :],
                                                                            start=(kt == 0 and g % 4 == 0), stop=False), reads=[f"pT{p}", "Vs"], writes=[f"acc{g // 4}"])
            s = nxt("sp", 2)
            P.emit("tensor", lambda e, s=s: e.matmul(sp[s][0:8, 0:64], lhsT=knT[:, :], rhs=QTa[0:64, 0, :], start=True, stop=False), reads=["knT", "QTa"], writes=[f"sp{s}"])
            P.emit("tensor", lambda e, s=s: e.matmul(sp[s][0:8, 0:64], lhsT=ident[0:8, 0:8], rhs=cbs[:, :], start=False, stop=True), reads=["ident", "cbs"], writes=[f"sp{s}"])
            p = nxt("pt", 3)
            P.emit("scalar", lambda e, s=s, p=p: e.activation(out=pT[p][0:8, 0:64], in_=sp[s][0:8, 0:64], func=AF.Exp, scale=0.125), reads=[f"sp{s}"], writes=[f"pT{p}"])
            for g in range(8):
                P.emit("tensor", lambda e, p=p, g=g: e.matmul(acc[g // 4][0:8, (g % 4) * 65:(g % 4) * 65 + 65], lhsT=pT[p][0:8, g * 8:(g + 1) * 8], rhs=vnA[:, :], start=False, stop=True),
                       reads=[f"pT{p}", "vnA"], writes=[f"acc{g // 4}"])
            for kt in range(5):
                s = nxt("sp", 2)
                P.emit("tensor", lambda e, s=s, kt=kt: e.matmul(sp[s][:, 0:64], lhsT=KwT[:, kt * 128:(kt + 1) * 128], rhs=QTa[0:64, 0, :], start=True, stop=False), reads=["KwT", "QTa"], writes=[f"sp{s}"])
                P.emit("tensor", lambda e, s=s, kt=kt: e.matmul(sp[s][:, 0:64], lhsT=ident[:], rhs=wbs[:, kt, :], start=False, stop=True), reads=["ident", "wbs"], writes=[f"sp{s}"])
                p = nxt("pt", 3)
                P.emit("scalar", lambda e, s=s, p=p: e.activation(out=pT[p][:, 0:64], in_=sp[s][:, 0:64], func=AF.Exp, scale=0.125), reads=[f"sp{s}"], writes=[f"pT{p}"])
                for g in range(8):
                    P.emit("tensor", lambda e, p=p, g=g, kt=kt: e.matmul(acc[2 + g // 4][0:8, (g % 4) * 65:(g % 4) * 65 + 65], lhsT=pT[p][:, g * 8:(g + 1) * 8], rhs=Vw[:, kt, :],
                                                                   start=(kt == 0 and g % 4 == 0), stop=(kt == 4)), reads=[f"pT{p}", "Vw"], writes=[f"acc{2 + g // 4}"])
            for br, a0 in ((1, 0), (2, 2)):
                for hf in range(2):
                    a = a0 + hf
                    P.emit("vector", lambda e, a=a, hf=hf: e.tensor_copy(out=den[:, hf * 4:(hf + 1) * 4], in_=fview(acc[a], r8, 64, 65, 4)), reads=[f"acc{a}"], writes=["den"])
                P.emit("vector", lambda e: e.tensor_scalar(out=rden[:], in0=den[:], scalar1=1e-30, scalar2=None, op0=ALU.max), reads=["den"], writes=["rden"])
                P.emit("vector", lambda e: e.reciprocal(out=rden[:], in_=rden[:]), reads=["rden"], writes=["rden"])
                P.emit("vector", lambda e, gv=gv, br=br: e.tensor_tensor(out=fsc[:], in0=rden[:], in1=gv[:, :, br], op=ALU.mult), reads=["rden", "gts"], writes=["fsc"])
                for hf in range(2):
                    a = a0 + hf
                    ov = acc[a][0:8, 0:260].rearrange("p (g c) -> p g c", c=65)[:, :, 0:64]
                    P.emit("vector", lambda e, a=a, hf=hf, ov=ov: e.tensor_tensor(out=otmp[:, hf * 4:(hf + 1) * 4, :], in0=ov, in1=bc(fsc[:, hf * 4:(hf + 1) * 4], [8, 4, 64], 2), op=ALU.mult), reads=[f"acc{a}", "fsc"], writes=["otmp"])
                P.emit("gpsimd", lambda e, ob_=ob_: e.tensor_tensor(out=oacc[ob_][:], in0=oacc[ob_][:], in1=otmp[:], op=ALU.add), reads=[f"oacc{ob_}", "otmp"], writes=[f"oacc{ob_}"])
            P.dma("sync", lambda e, ob_=ob_, j=j: e.dma_start(out=o_out[j * 8:(j + 1) * 8, :], in_=oacc[ob_][:].rearrange("p g d -> p (g d)")), reads=[f"oacc{ob_}"], chan=f"oacc{ob_}")
        P.replay(nc, es)
    return nc


def run_nsa_sample(zs, cache_cmp, cache_sel, state_win, page_table, w1, w2, pe):
    T = lambda a: np.ascontiguousarray(a)
    base = nsa_consts(0)
    tok = np.arange(DEC_T)
    qp = (PAST + tok)[:, None]
    j = np.arange(128)[None, :]
    cur = qp // 64
    forced = (j == 0) | (j == cur) | (j == cur - 1)
    tA = (~forced).astype(np.float32)
    tB = np.where(forced, 1e6, 0.0).astype(np.float32)
    causb = np.tile(np.where(np.arange(8)[:, None] <= tok[None, :], 0.0, NEGB).astype(np.float32), (1, 8))
    wb = np.zeros((128, 5, 64), np.float32)
    for r in range(5):
        kp = (PAST - WIN + r * 128 + np.arange(128))[:, None]
        ok = (kp <= qp.T) & (kp > qp.T - WIN) & (kp < PAST + DEC_T)
        wb[:, r, :] = np.tile(np.where(ok, 0.0, NEGB), (1, 8))
    cc_ = np.arange(SEQ)
    kt_, p_ = cc_ // 128, cc_ % 128
    tokc = ((kt_ // 8) * 8 + p_ // 16) * 128 + (p_ % 16) * 8 + (kt_ % 8)
    EKp = (((tokc // 64) % 64)[None, :] == np.arange(64)[:, None]).astype(np.float32)
    assert ((tokc // 64) // 64 == kt_ // 32).all() and sorted(tokc) == list(range(SEQ))
    w1s = T(w1.reshape(2, 32, 64, 256).transpose(0, 2, 1, 3).reshape(128, 32, 256))
    pes = T(pe.transpose(0, 2, 1).reshape(128, 32))
    kvs = zs[:, :, 1024:1792].reshape(DEC_B, DEC_T, 3, 2, 2, 64)
    in_maps = []
    for c in range(NCORES):
        kvh, s0 = c % 2, (c // 2) * 8
        sq = slice(s0, s0 + 8)
        wall = np.concatenate([state_win[sq, :, :, kvh, :], kvs[sq, :, 2, :, kvh, :]], axis=1)
        in_maps.append({
            "pool_cmp": T(cache_cmp[:, :, :, kvh, :].reshape(NPOOL, 128, 128)),
            "pool_sel": T(cache_sel[:, :, :, kvh, :].reshape(NPOOL, 128, 128)),
            "ptl": T(np.stack([page_table[sq][:, d * 8 + np.arange(128) // 16] for d in range(8)], axis=1).transpose(2, 0, 1).reshape(128, 64).astype(np.int32)),
            "pmod": T((np.arange(128) % 16).astype(np.int32).reshape(128, 1)),
            "qT": T(zs[sq, :, 0:1024].reshape(8, DEC_T, 2, 8, 64)[:, :, kvh].transpose(0, 3, 2, 1).reshape(8, 64, 64)),
            "gates": T(zs[sq, :, 1792:1840].reshape(8, DEC_T, 2, 24)[:, :, kvh]),
            "knewT": T(kvs[sq, :, 1, 0, kvh].transpose(0, 2, 1)), "vnew": T(kvs[sq, :, 1, 1, kvh]),
            "wKT": T(wall[:, :, 0].transpose(0, 2, 1)), "wV": T(wall[:, :, 1]), "wrows": T(wall.reshape(8, 520, 128)),
            "EK": EKp, "topA": tA, "topB": tB, "ovl": base["ovl"], "ropeC": base["ropeC"], "ropeS": base["ropeS"],
            "causb": causb, "wbias": wb, "w1s": w1s, "w2": T(w2), "pes": pes})
    if "nsa_s" not in _NC_CACHE:
        _NC_CACHE["nsa_s"] = build_nsa_s()
    res = run_bass_kernel_spmd(_NC_CACHE["nsa_s"], in_maps, core_ids=list(range(NCORES))).results
    os_ = np.zeros((DEC_B, DEC_T, 2, 512), np.float32)
    win_s = np.zeros((DEC_B, WIN, 2, 2, 64), np.float32)
    for c in range(NCORES):
        kvh, s0 = c % 2, (c // 2) * 8
        os_[s0:s0 + 8, :, kvh] = res[c]["o_out"].reshape(8, DEC_T, 512)
        win_s[s0:s0 + 8, :, :, kvh, :] = res[c]["wins_out"].reshape(8, WIN, 2, 64)
    return os_.reshape(DEC_B, DEC_T, D), win_s


LAM_INIT = 0.8 - 0.6 * float(np.exp(-0.3 * 1))


def build_diff():
    nc = bass.Bass("TRN2", target_bir_lowering=False)
    dt_in = lambda n, s: nc.dram_tensor(n, s, F32, kind="ExternalInput").ap()
    qT = dt_in("qT", [NPT, 8, 128, 128])
    KTd = dt_in("KTd", [2, 8, 128, SEQ])
    Vd = dt_in("Vd", [2, 8, SEQ, 128])
    kbias = dt_in("kbias", [NPT, 128, 15, 128])
    lam_rep = dt_in("lam_rep", [128, 256])
    sub_rep = dt_in("sub_rep", [128, 128])
    o_out = nc.dram_tensor("o_out", [NPT * 128, D], F32, kind="ExternalOutput").ap()
    P = Prog()
    with ExitStack() as es:
        sb = lambda n, s, d: es.enter_context(nc.sbuf_tensor(n, s, d))
        ps = lambda n, s, d: es.enter_context(nc.psum_tensor(n, s, d))
        ident = make_ident(P, nc, es)
        KT = [sb(f"KT{i}", [128, SEQ], BF16) for i in range(2)]
        Va = [sb(f"Va{i}", [128, 64, 129], BF16) for i in range(2)]
        QB = [sb(f"QB{i}", [128, 2, 128], BF16) for i in range(2)]
        kb1 = sb("kb1", [128, 15, 128], BF16)
        kbr = [sb(f"kbr{i}", [128, 15, 2, 128], BF16) for i in range(2)]
        pT = [sb(f"pT{i}", [128, 256], BF16) for i in range(3)]
        lamt = sb("lamt", [128, 256], F32)
        lprod = sb("lprod", [128, 128], F32)
        lsum = sb("lsum", [128, 2], F32)
        nlam = sb("nlam", [128, 1], F32)
        gsub = sb("gsub", [128, 128], F32)
        den = sb("den", [128, 2], F32)
        rd = sb("rd", [128, 2], F32)
        t0 = sb("t0", [128, 128], F32)
        o1 = sb("o1", [128, 128], F32)
        junk = sb("junk", [128, 128], F32)
        ss = sb("ss", [128, 1], F32)
        rstd = sb("rstd", [128, 1], F32)
        ot = [sb(f"ot{i}", [128, 128], F32) for i in range(2)]
        sp = [ps(f"sp{i}", [128, 256], F32) for i in range(3)]
        acc = [ps(f"acc{i}", [128, 258], F32) for i in range(2)]
        P.dma("sync", lambda e: e.dma_start(out=lamt[:], in_=lam_rep[:, :]), writes=["lamt"], chan="lamt")
        P.dma("sync", lambda e: e.dma_start(out=gsub[:], in_=sub_rep[:, :]), writes=["gsub"], chan="gsub")
        P.emit("vector", lambda e: e.tensor_tensor(out=lprod[:].rearrange("p (a d) -> p a d", a=2), in0=lamt[:].rearrange("p (a two d) -> p a two d", a=2, two=2)[:, :, 0, :],
                                                   in1=lamt[:].rearrange("p (a two d) -> p a two d", a=2, two=2)[:, :, 1, :], op=ALU.mult), reads=["lamt"], writes=["lprod"])
        P.emit("vector", lambda e: e.reduce_sum(out=lsum[:], in_=lprod[:].rearrange("p (a d) -> p a d", a=2), axis=mybir.AxisListType.X), reads=["lprod"], writes=["lsum"])
        P.emit("scalar", lambda e: e.activation(out=lsum[:], in_=lsum[:], func=AF.Exp), reads=["lsum"], writes=["lsum"])
        P.emit("vector", lambda e: e.tensor_tensor(out=nlam[:], in0=lsum[:, 1:2], in1=lsum[:, 0:1], op=ALU.subtract), reads=["lsum"], writes=["nlam"])
        P.emit("vector", lambda e: e.tensor_scalar(out=nlam[:], in0=nlam[:], scalar1=-LAM_INIT, scalar2=None, op0=ALU.add), reads=["nlam"], writes=["nlam"])
        P.emit("vector", lambda e: e.tensor_scalar(out=gsub[:], in0=gsub[:], scalar1=1.0 - LAM_INIT, scalar2=None, op0=ALU.mult), reads=["gsub"], writes=["gsub"])
        for i in range(2):
            P.emit("gpsimd", lambda e, i=i: e.memset(Va[i][:, :, 128:129], 1.0), writes=[f"Va{i}"])
            P.emit("gpsimd", lambda e, i=i: e.memset(QB[i][:], 0.0), writes=[f"QB{i}"])
        spi, pti, it = [0], [0], 0
        for b in range(2):
            for h in range(8):
                cb = (b * 8 + h) % 2
                for q4 in range(4):
                    cs_ = slice(q4 * 2048, (q4 + 1) * 2048)
                    P.dma("gpsimd", lambda e, cb=cb, b=b, h=h, cs_=cs_: e.dma_start(out=KT[cb][:, cs_], in_=KTd[b, h, :, cs_]), writes=[f"KT{cb}"], chan=f"KT{cb}")
                    P.dma("gpsimd", lambda e, cb=cb, b=b, h=h, q4=q4: e.dma_start(out=Va[cb][:, q4 * 16:(q4 + 1) * 16, 0:128], in_=Vd[b, h, q4 * 2048:(q4 + 1) * 2048, :].rearrange("(k p) d -> p k d", p=128)),
                          writes=[f"Va{cb}"], chan=f"Va{cb}")
                for ti in range(NPT):
                    tb_, qmin, qmax = slot_range(ti)
                    if tb_ != b:
                        continue
                    ib = it % 2
                    it += 1
                    r0 = ti * 128
                    P.dma("gpsimd", lambda e, ib=ib, ti=ti, h=h: e.dma_start(out=QB[ib][0:64, 0, :], in_=qT[ti, h, 0:64, :]), writes=[f"QB{ib}"], chan=f"QB{ib}")
                    P.dma("gpsimd", lambda e, ib=ib, ti=ti, h=h: e.dma_start(out=QB[ib][64:128, 1, :], in_=qT[ti, h, 64:128, :]), writes=[f"QB{ib}"], chan=f"QB{ib}")
                    P.dma("gpsimd", lambda e, ti=ti: e.dma_start(out=kb1[:], in_=kbias[ti, :, :, :]), writes=["kb1"], chan="kb1")
                    P.emit("gpsimd", lambda e, ib=ib: e.tensor_copy(out=kbr[ib][:], in_=bc(kb1[:], [128, 15, 2, 128], 2)), reads=["kb1"], writes=[f"kbr{ib}"])
                    ab = it % 2
                    pend = []

                    def pv_diff(p, kt, qmax, cb, ab):
                        for c in range(2):
                            P.emit("tensor", lambda e, p=p, c=c, kt=kt: e.matmul(acc[ab][:, c * 129:(c + 1) * 129], lhsT=pT[p][:, c * 128:(c + 1) * 128], rhs=Va[cb][:, kt, :],
                                                                           start=(kt == 0 and c == 0), stop=(kt == qmax)), reads=[f"pT{p}", f"Va{cb}"], writes=[f"acc{ab}"])

                    for kt in range(qmax + 1):
                        spi[0] += 1
                        s = spi[0] % 3
                        hasb = kt >= qmin
                        P.emit("tensor", lambda e, s=s, cb=cb, ib=ib, kt=kt, hasb=hasb: e.matmul(sp[s][:, :], lhsT=KT[cb][:, kt * 128:(kt + 1) * 128], rhs=QB[ib][:].rearrange("p c t -> p (c t)"),
                                                                                           start=True, stop=(not hasb)), reads=[f"KT{cb}", f"QB{ib}"], writes=[f"sp{s}"])
                        if hasb:
                            P.emit("tensor", lambda e, s=s, ib=ib, kt=kt, qmin=qmin: e.matmul(sp[s][:, :], lhsT=ident[:], rhs=kbr[ib][:, kt - qmin, :, :].rearrange("p c t -> p (c t)"), start=False, stop=True),
                                   reads=["ident", f"kbr{ib}"], writes=[f"sp{s}"])
                        pti[0] += 1
                        p = pti[0] % 3
                        P.emit("scalar", lambda e, s=s, p=p: e.activation(out=pT[p][:], in_=sp[s][:], func=AF.Exp, scale=0.125), reads=[f"sp{s}"], writes=[f"pT{p}"])
                        pend.append((p, kt, qmax, cb, ab))
                        if len(pend) > PIPE:
                            pv_diff(*pend.pop(0))
                    while pend:
                        pv_diff(*pend.pop(0))
                    A = f"acc{ab}"
                    P.emit("vector", lambda e, ab=ab: e.tensor_copy(out=den[:], in_=fview(acc[ab], slice(0, 128), 128, 129, 2)), reads=[A], writes=["den"])
                    P.emit("vector", lambda e: e.tensor_scalar(out=rd[:], in0=den[:], scalar1=1e-30, scalar2=None, op0=ALU.max), reads=["den"], writes=["rd"])
                    P.emit("vector", lambda e: e.reciprocal(out=rd[:], in_=rd[:]), reads=["rd"], writes=["rd"])
                    P.emit("vector", lambda e: e.tensor_tensor(out=rd[:, 1:2], in0=rd[:, 1:2], in1=nlam[:], op=ALU.mult), reads=["rd", "nlam"], writes=["rd"])
                    P.emit("vector", lambda e, ab=ab: e.tensor_scalar(out=t0[:], in0=acc[ab][:, 0:128], scalar1=rd[:, 0:1], scalar2=None, op0=ALU.mult), reads=[A, "rd"], writes=["t0"])
                    P.emit("vector", lambda e, ab=ab: e.scalar_tensor_tensor(out=o1[:], in0=acc[ab][:, 129:257], scalar=rd[:, 1:2], in1=t0[:], op0=ALU.mult, op1=ALU.add), reads=[A, "rd", "t0"], writes=["o1"])
                    P.emit("vector", lambda e: e.memset(ss[:], 0.0), writes=["ss"])
                    P.emit("scalar", lambda e: e.activation(out=junk[:], in_=o1[:], func=AF.Square, accum_out=ss[:, 0:1]), reads=["o1", "ss"], writes=["junk", "ss"])
                    P.emit("vector", lambda e: e.tensor_scalar(out=rstd[:], in0=ss[:], scalar1=1.0 / 128, scalar2=EPS, op0=ALU.mult, op1=ALU.add), reads=["ss"], writes=["rstd"])
                    P.emit("scalar", lambda e: e.activation(out=rstd[:], in_=rstd[:], func=AF.Sqrt), reads=["rstd"], writes=["rstd"])
                    P.emit("vector", lambda e: e.reciprocal(out=rstd[:], in_=rstd[:]), reads=["rstd"], writes=["rstd"])
                    ob_ = it % 2
                    P.emit("vector", lambda e, ob_=ob_: e.scalar_tensor_tensor(out=ot[ob_][:], in0=o1[:], scalar=rstd[:, 0:1], in1=gsub[:], op0=ALU.mult, op1=ALU.mult), reads=["o1", "rstd", "gsub"], writes=[f"ot{ob_}"])
                    P.dma("sync", lambda e, ob_=ob_, r0=r0, h=h: e.dma_start(out=o_out[r0:r0 + 128, h * 128:(h + 1) * 128], in_=ot[ob_][:]), reads=[f"ot{ob_}"], chan=f"ot{ob_}")
        P.replay(nc, es)
    return nc


def run_diff_prompt(z1p, lam_p, subnorm):
    T = lambda a: np.ascontiguousarray(a)
    KTd = T(z1p[:, :, 1024:2048].reshape(2, SEQ, 8, 128).transpose(0, 2, 3, 1))
    Vd = T(z1p[:, :, 2048:3072].reshape(2, SEQ, 8, 128).transpose(0, 2, 1, 3))
    lam_rep = T(np.broadcast_to(np.asarray(lam_p, np.float32).reshape(1, 256), (128, 256)))
    sub_rep = T(np.broadcast_to(np.asarray(subnorm, np.float32).reshape(1, 128), (128, 128)))
    in_maps = []
    for c in range(NCORES):
        tiles = core_tiles(c)
        qT = T(np.stack([z1p[b, p0:p0 + 128, 0:1024].reshape(128, 8, 128).transpose(1, 2, 0) for b, p0 in tiles]))
        in_maps.append({"qT": qT, "KTd": KTd, "Vd": Vd, "kbias": nsa_consts(c)["kbias"], "lam_rep": lam_rep, "sub_rep": sub_rep})
    if "diff" not in _NC_CACHE:
        _NC_CACHE["diff"] = build_diff()
    res = run_bass_kernel_spmd(_NC_CACHE["diff"], in_maps, core_ids=list(range(NCORES))).results
    op = np.zeros((2, SEQ, D), np.float32)
    for c in range(NCORES):
        for ti, (b_, p0) in enumerate(core_tiles(c)):
            op[b_, p0:p0 + 128] = res[c]["o_out"][ti * 128:(ti + 1) * 128]
    return op


def build_diff_s():
    nc = bass.Bass("TRN2", target_bir_lowering=False)
    dt_in = lambda n, s, d=F32: nc.dram_tensor(n, s, d, kind="ExternalInput").ap()
    pool = dt_in("pool", [NPOOL, 128, 256])
    ptl = dt_in("ptl", [128, DEC_B * 8], I32)
    pmod = dt_in("pmod", [128, 1], I32)
    qT = dt_in("qT", [DEC_B, 128, 8])
    knewT = dt_in("knewT", [DEC_B, 128, 8])
    vnew = dt_in("vnew", [DEC_B, 8, 128])
    causb = dt_in("causb", [8, 16])
    lam_rep = dt_in("lam_rep", [128, 256])
    sub_rep = dt_in("sub_rep", [128, 128])
    o_out = nc.dram_tensor("o_out", [DEC_B * DEC_T, 128], F32, kind="ExternalOutput").ap()
    P = Prog()
    with ExitStack() as es:
        sb = lambda n, s, d: es.enter_context(nc.sbuf_tensor(n, s, d))
        ps = lambda n, s, d: es.enter_context(nc.psum_tensor(n, s, d))
        ident = make_ident(P, nc, es)
        ptb = sb("ptb", [128, DEC_B * 8], I32)
        pidx = sb("pidx", [128, DEC_B * 8], I32)
        piota = sb("piota", [128, 1], I32)
        dTok = [sb(f"dTok{i}", [128, NPG, 256], BF16) for i in range(2)]
        KT = sb("KT", [128, SEQ], BF16)
        Va = sb("Va", [128, NPG, 129], BF16)
        QB = sb("QB", [128, 2, 8], BF16)
        knT = sb("knT", [128, 8], BF16)
        vnA = sb("vnA", [8, 129], BF16)
        cbs = sb("cbs", [8, 16], BF16)
        pT = [sb(f"pT{i}", [128, 128], BF16) for i in range(3)]
        lamt = sb("lamt", [128, 256], F32)
        lprod = sb("lprod", [128, 128], F32)
        lsum = sb("lsum", [128, 2], F32)
        nlam = sb("nlam", [128, 1], F32)
        gsub = sb("gsub", [128, 128], F32)
        den = sb("den", [8, 2], F32)
        rd = sb("rd", [8, 2], F32)
        t0 = sb("t0", [8, 128], F32)
        o1 = sb("o1", [8, 128], F32)
        junk = sb("junk", [8, 128], F32)
        ss = sb("ss", [8, 1], F32)
        rstd = sb("rstd", [8, 1], F32)
        ot = [sb(f"ot{i}", [8, 128], F32) for i in range(2)]
        tpp = [ps(f"tpp{i}", [128, 4, 128], BF16) for i in range(2)]
        sp = [ps(f"sp{i}", [128, 128], F32) for i in range(2)]
        acc = [ps(f"acc{i}", [128, 258], F32) for i in range(2)]
        P.dma("gpsimd", lambda e: e.dma_start(out=ptb[:], in_=ptl[:, :]), writes=["ptb"], chan="ptb")
        P.dma("gpsimd", lambda e: e.dma_start(out=piota[:], in_=pmod[:, :]), writes=["piota"], chan="piota")
        P.emit("gpsimd", lambda e: e.tensor_scalar(out=pidx[:], in0=ptb[:], scalar1=16, scalar2=None, op0=ALU.mult), reads=["ptb"], writes=["pt"])
        P.emit("gpsimd", lambda e: e.tensor_tensor(out=pidx[:], in0=pidx[:], in1=piota[:].to_broadcast([128, DEC_B * 8]), op=ALU.add), reads=["pt", "piota"], writes=["pt"])
        prow = pool.rearrange("n (g r) c -> (n g) (r c)", r=8)
        P.dma("sync", lambda e: e.dma_start(out=lamt[:], in_=lam_rep[:, :]), writes=["lamt"], chan="lamt")
        P.dma("sync", lambda e: e.dma_start(out=gsub[:], in_=sub_rep[:, :]), writes=["gsub"], chan="gsub")
        P.dma("gpsimd", lambda e: e.dma_start(out=cbs[:], in_=causb[:, :]), writes=["cbs"], chan="cbs")
        P.emit("vector", lambda e: e.tensor_tensor(out=lprod[:].rearrange("p (a d) -> p a d", a=2), in0=lamt[:].rearrange("p (a two d) -> p a two d", a=2, two=2)[:, :, 0, :],
                                                   in1=lamt[:].rearrange("p (a two d) -> p a two d", a=2, two=2)[:, :, 1, :], op=ALU.mult), reads=["lamt"], writes=["lprod"])
        P.emit("vector", lambda e: e.reduce_sum(out=lsum[:], in_=lprod[:].rearrange("p (a d) -> p a d", a=2), axis=mybir.AxisListType.X), reads=["lprod"], writes=["lsum"])
        P.emit("scalar", lambda e: e.activation(out=lsum[:], in_=lsum[:], func=AF.Exp), reads=["lsum"], writes=["lsum"])
        P.emit("vector", lambda e: e.tensor_tensor(out=nlam[:], in0=lsum[:, 1:2], in1=lsum[:, 0:1], op=ALU.subtract), reads=["lsum"], writes=["nlam"])
        P.emit("vector", lambda e: e.tensor_scalar(out=nlam[:], in0=nlam[:], scalar1=-LAM_INIT, scalar2=None, op0=ALU.add), reads=["nlam"], writes=["nlam"])
        P.emit("vector", lambda e: e.tensor_scalar(out=gsub[:], in0=gsub[:], scalar1=1.0 - LAM_INIT, scalar2=None, op0=ALU.mult), reads=["gsub"], writes=["gsub"])
        P.emit("gpsimd", lambda e: e.memset(Va[:, :, 128:129], 1.0), writes=["Va"])
        P.emit("gpsimd", lambda e: e.memset(vnA[:, 128:129], 1.0), writes=["vnA"])
        P.emit("gpsimd", lambda e: e.memset(QB[:], 0.0), writes=["QB"])
        cnt = {"sp": 0, "pt": 0, "tp": 0, "cp": 0}

        def nxt(k, n):
            cnt[k] += 1
            return cnt[k] % n

        def gather(j, bi):
            for d in range(8):
                def g(e, j=j, d=d, bi=bi):
                    return e.indirect_dma_start(out=dTok[bi][:, d * 8:(d + 1) * 8, :].rearrange("p r c -> p (r c)"), out_offset=None, in_=prow[:, :],
                                                in_offset=bass.IndirectOffsetOnAxis(ap=pidx[:, j * 8 + d:j * 8 + d + 1], axis=0))
                P.dma("gpsimd", g, reads=["pt"], writes=[f"dTok{bi}"], chan=f"dTok{bi}")

        gather(0, 0)
        r8 = slice(0, 8)
        for j in range(DEC_B):
            bi = j % 2
            if j + 1 < DEC_B:
                gather(j + 1, 1 - bi)
            for p4 in range(NPG // 4):
                tb = nxt("tp", 2)
                for q in range(4):
                    P.emit("tensor", lambda e, tb=tb, q=q, p4=p4, bi=bi: e.transpose(out=tpp[tb][:, q, :], in_=dTok[bi][:, p4 * 4 + q, 0:128], identity=ident[:]),
                           reads=[f"dTok{bi}", "ident"], writes=[f"tpp{tb}"])
                if nxt("cp", 2):
                    P.emit("scalar", lambda e, tb=tb, p4=p4: e.copy(out=KT[:, p4 * 512:(p4 + 1) * 512], in_=tpp[tb][:, :, :].rearrange("p q t -> p (q t)")), reads=[f"tpp{tb}"], writes=["KT"])
                else:
                    P.emit("vector", lambda e, tb=tb, p4=p4: e.tensor_copy(out=KT[:, p4 * 512:(p4 + 1) * 512], in_=tpp[tb][:, :, :].rearrange("p q t -> p (q t)")), reads=[f"tpp{tb}"], writes=["KT"])
            P.emit("gpsimd", lambda e, bi=bi: e.tensor_copy(out=Va[:, :, 0:128], in_=dTok[bi][:, :, 128:256]), reads=[f"dTok{bi}"], writes=["Va"])
            P.dma("gpsimd", lambda e, j=j: e.dma_start(out=QB[0:64, 0, :], in_=qT[j, 0:64, :]), writes=["QB"], chan="QB")
            P.dma("gpsimd", lambda e, j=j: e.dma_start(out=QB[64:128, 1, :], in_=qT[j, 64:128, :]), writes=["QB"], chan="QB")
            P.dma("gpsimd", lambda e, j=j: e.dma_start(out=knT[:], in_=knewT[j, :, :]), writes=["knT"], chan="knT")
            P.dma("gpsimd", lambda e, j=j: e.dma_start(out=vnA[:, 0:128], in_=vnew[j, :, :]), writes=["vnA"], chan="vnA")
            ab = j % 2
            for k8 in range(NPG // 8):
                s = nxt("sp", 2)
                for q in range(8):
                    kt = k8 * 8 + q
                    P.emit("tensor", lambda e, s=s, q=q, kt=kt: e.matmul(sp[s][:, q * 16:(q + 1) * 16], lhsT=KT[:, kt * 128:(kt + 1) * 128], rhs=QB[:].rearrange("p c t -> p (c t)"), start=True, stop=True),
                           reads=["KT", "QB"], writes=[f"sp{s}"])
                p = nxt("pt", 3)
                P.emit("scalar", lambda e, s=s, p=p: e.activation(out=pT[p][:, :], in_=sp[s][:, :], func=AF.Exp, scale=0.125), reads=[f"sp{s}"], writes=[f"pT{p}"])
                for q in range(8):
                    kt = k8 * 8 + q
                    for c in range(2):
                        P.emit("tensor", lambda e, p=p, q=q, c=c, kt=kt, ab=ab: e.matmul(acc[ab][0:8, c * 129:(c + 1) * 129], lhsT=pT[p][:, q * 16 + c * 8:q * 16 + (c + 1) * 8], rhs=Va[:, kt, :],
                                                                                   start=(kt == 0 and c == 0), stop=False), reads=[f"pT{p}", "Va"], writes=[f"acc{ab}"])
            s = nxt("sp", 2)
            P.emit("tensor", lambda e, s=s: e.matmul(sp[s][0:8, 0:16], lhsT=knT[:, :], rhs=QB[:].rearrange("p c t -> p (c t)"), start=True, stop=False), reads=["knT", "QB"], writes=[f"sp{s}"])
            P.emit("tensor", lambda e, s=s: e.matmul(sp[s][0:8, 0:16], lhsT=ident[0:8, 0:8], rhs=cbs[:, :], start=False, stop=True), reads=["ident", "cbs"], writes=[f"sp{s}"])
            p = nxt("pt", 3)
            P.emit("scalar", lambda e, s=s, p=p: e.activation(out=pT[p][0:8, 0:16], in_=sp[s][0:8, 0:16], func=AF.Exp, scale=0.125), reads=[f"sp{s}"], writes=[f"pT{p}"])
            for c in range(2):
                P.emit("tensor", lambda e, p=p, c=c, ab=ab: e.matmul(acc[ab][0:8, c * 129:(c + 1) * 129], lhsT=pT[p][0:8, c * 8:(c + 1) * 8], rhs=vnA[:, :], start=False, stop=True),
                       reads=[f"pT{p}", "vnA"], writes=[f"acc{ab}"])
            A = f"acc{ab}"
            P.emit("vector", lambda e, ab=ab: e.tensor_copy(out=den[:], in_=fview(acc[ab], r8, 128, 129, 2)), reads=[A], writes=["den"])
            P.emit("vector", lambda e: e.tensor_scalar(out=rd[:], in0=den[:], scalar1=1e-30, scalar2=None, op0=ALU.max), reads=["den"], writes=["rd"])
            P.emit("vector", lambda e: e.reciprocal(out=rd[:], in_=rd[:]), reads=["rd"], writes=["rd"])
            P.emit("vector", lambda e: e.tensor_tensor(out=rd[:, 1:2], in0=rd[:, 1:2], in1=nlam[r8, :], op=ALU.mult), reads=["rd", "nlam"], writes=["rd"])
            P.emit("vector", lambda e, ab=ab: e.tensor_scalar(out=t0[:], in0=acc[ab][0:8, 0:128], scalar1=rd[:, 0:1], scalar2=None, op0=ALU.mult), reads=[A, "rd"], writes=["t0"])
            P.emit("vector", lambda e, ab=ab: e.scalar_tensor_tensor(out=o1[:], in0=acc[ab][0:8, 129:257], scalar=rd[:, 1:2], in1=t0[:], op0=ALU.mult, op1=ALU.add), reads=[A, "rd", "t0"], writes=["o1"])
            P.emit("vector", lambda e: e.memset(ss[:], 0.0), writes=["ss"])
            P.emit("scalar", lambda e: e.activation(out=junk[:], in_=o1[:], func=AF.Square, accum_out=ss[:, 0:1]), reads=["o1", "ss"], writes=["junk", "ss"])
            P.emit("vector", lambda e: e.tensor_scalar(out=rstd[:], in0=ss[:], scalar1=1.0 / 128, scalar2=EPS, op0=ALU.mult, op1=ALU.add), reads=["ss"], writes=["rstd"])
            P.emit("scalar", lambda e: e.activation(out=rstd[:], in_=rstd[:], func=AF.Sqrt), reads=["rstd"], writes=["rstd"])
            P.emit("vector", lambda e: e.reciprocal(out=rstd[:], in_=rstd[:]), reads=["rstd"], writes=["rstd"])
            ob_ = j % 2
            P.emit("vector", lambda e, ob_=ob_: e.scalar_tensor_tensor(out=ot[ob_][:], in0=o1[:], scalar=rstd[:, 0:1], in1=gsub[r8, :], op0=ALU.mult, op1=ALU.mult), reads=["o1", "rstd", "gsub"], writes=[f"ot{ob_}"])
            P.dma("sync", lambda e, ob_=ob_, j=j: e.dma_start(out=o_out[j * 8:(j + 1) * 8, :], in_=ot[ob_][:]), reads=[f"ot{ob_}"], chan=f"ot{ob_}")
        P.replay(nc, es)
    return nc


def run_diff_sample(z1s, cache_kv, page_table, lam_p, subnorm):
    T = lambda a: np.ascontiguousarray(a)
    lam_rep = T(np.broadcast_to(np.asarray(lam_p, np.float32).reshape(1, 256), (128, 256)))
    sub_rep = T(np.broadcast_to(np.asarray(subnorm, np.float32).reshape(1, 128), (128, 128)))
    tok = np.arange(DEC_T)
    causb = np.tile(np.where(np.arange(8)[:, None] <= tok[None, :], 0.0, NEGB).astype(np.float32), (1, 2))
    pslot = np.arange(128) // 16
    ptl = T(np.stack([page_table[:, d * 8 + pslot] for d in range(8)], axis=1).transpose(2, 0, 1).reshape(128, DEC_B * 8).astype(np.int32))
    pmod = T((np.arange(128) % 16).astype(np.int32).reshape(128, 1))
    in_maps = []
    for h in range(NCORES):
        in_maps.append({
            "pool": T(cache_kv[:, :, :, h, :].reshape(NPOOL, 128, 256)), "ptl": ptl, "pmod": pmod,
            "qT": T(z1s[:, :, h * 128:(h + 1) * 128].transpose(0, 2, 1)),
            "knewT": T(z1s[:, :, 1024 + h * 128:1024 + (h + 1) * 128].transpose(0, 2, 1)),
            "vnew": T(z1s[:, :, 2048 + h * 128:2048 + (h + 1) * 128]),
            "causb": causb, "lam_rep": lam_rep, "sub_rep": sub_rep})
    if "diff_s" not in _NC_CACHE:
        _NC_CACHE["diff_s"] = build_diff_s()
    res = run_bass_kernel_spmd(_NC_CACHE["diff_s"], in_maps, core_ids=list(range(NCORES))).results
    os_ = np.zeros((DEC_B, DEC_T, 8, 128), np.float32)
    for h in range(NCORES):
        os_[:, :, h, :] = res[h]["o_out"].reshape(DEC_B, DEC_T, 128)
    return os_.reshape(DEC_B, DEC_T, D)


def rope_tables(pos):
    half = 32
    inv = (THETA ** (-np.arange(half, dtype=np.float32) / half)).astype(np.float32)
    ang = pos.astype(np.float32)[:, None] * inv[None, :]
    return np.cos(ang).astype(np.float32), np.sin(ang).astype(np.float32)


def core_positions(c):
    return np.concatenate([np.arange(p0, p0 + 128) for _, p0 in core_tiles(c)] + [np.tile(PAST + np.arange(DEC_T), 4)])


def shard_tokens(xp, xs, c):
    parts = [xp[b, p0:p0 + 128] for b, p0 in core_tiles(c)]
    parts.append(xs[4 * c:4 * c + 4].reshape(32, -1))
    return np.ascontiguousarray(np.concatenate(parts, axis=0))


def unshard_tokens(per_core, F):
    xp = np.zeros((2, SEQ, F), np.float32)
    xs = np.zeros((DEC_B, DEC_T, F), np.float32)
    for c in range(NCORES):
        r = per_core[c]
        for ti, (b, p0) in enumerate(core_tiles(c)):
            xp[b, p0:p0 + 128] = r[ti * 128:(ti + 1) * 128]
        xs[4 * c:4 * c + 4] = r[NPT * 128:].reshape(4, DEC_T, F)
    return xp, xs


def rep128(v):
    return np.ascontiguousarray(np.broadcast_to(np.asarray(v, np.float32)[None, :], (128, D)))


_NC_CACHE = {}


def run_proj(hp, hs, g, w, ncols, rope_sets, sig_cols):
    key = ("proj", ncols)
    if key not in _NC_CACHE:
        _NC_CACHE[key] = build_proj(ncols, rope_sets, sig_cols)
    in_maps = []
    for c in range(NCORES):
        cos, sin = rope_tables(core_positions(c))
        in_maps.append({"x": shard_tokens(hp, hs, c), "cs": cos, "sn": sin,
                        "w": np.ascontiguousarray(w, np.float32), "g_rep": rep128(g)})
    res = run_bass_kernel_spmd(_NC_CACHE[key], in_maps, core_ids=list(range(NCORES))).results
    return unshard_tokens([r["z_out"] for r in res], ncols)


def run_post(hp, hs, op, os_, w_out, g_mlp, w_up, w_down, g_fin, final):
    key = ("post", final)
    if key not in _NC_CACHE:
        _NC_CACHE[key] = build_post(final)
    in_maps = []
    for c in range(NCORES):
        in_maps.append({"x": shard_tokens(hp, hs, c), "o": shard_tokens(op, os_, c),
                        "w_out": np.ascontiguousarray(w_out, np.float32), "w_up": np.ascontiguousarray(w_up, np.float32),
                        "w_down": np.ascontiguousarray(w_down, np.float32), "g_rep": rep128(g_mlp), "gf_rep": rep128(g_fin)})
    res = run_bass_kernel_spmd(_NC_CACHE[key], in_maps, core_ids=list(range(NCORES))).results
    return unshard_tokens([r["h_out"] for r in res], D)


NSA_ROPE = [(0, 1, 0, 16), (1280, 2, 256, 2)]
DIFF_ROPE = [(0, 1, 0, 16), (1024, 1, 0, 16)]


def kernel(x_prompt, x_sample, cache_nsa_cmp, cache_nsa_sel, state_nsa_win, cache_diff_kv, page_table,
           norm_mix, norm_mlp, norm_final, nsa_w_in, nsa_cmp_pe, nsa_cmp_w1, nsa_cmp_w2, nsa_w_out,
           diff_w_in, diff_lambda, diff_subnorm, diff_w_out, mlp_w_up, mlp_w_down):
    f = lambda a: np.asarray(a, np.float32)
    x_prompt, x_sample = f(x_prompt), f(x_sample)
    zp, zs = run_proj(x_prompt, x_sample, f(norm_mix)[0], f(nsa_w_in)[0], NSA_IN, NSA_ROPE, (1792, 1840))
    op0 = run_nsa_prompt(zp, f(nsa_cmp_w1)[0], f(nsa_cmp_w2)[0], f(nsa_cmp_pe)[0])
    os0, win_s5 = run_nsa_sample(zs, f(cache_nsa_cmp)[0], f(cache_nsa_sel)[0], f(state_nsa_win)[0], np.asarray(page_table),
                                 f(nsa_cmp_w1)[0], f(nsa_cmp_w2)[0], f(nsa_cmp_pe)[0])
    h1p, h1s = run_post(x_prompt, x_sample, op0, os0, f(nsa_w_out)[0], f(norm_mlp)[0], f(mlp_w_up)[0], f(mlp_w_down)[0], f(norm_final), False)
    z1p, z1s = run_proj(h1p, h1s, f(norm_mix)[1], f(diff_w_in)[0], DIFF_IN, DIFF_ROPE, None)
    op1 = run_diff_prompt(z1p, f(diff_lambda)[0], f(diff_subnorm)[0])
    os1 = run_diff_sample(z1s, f(cache_diff_kv)[0], np.asarray(page_table), f(diff_lambda)[0], f(diff_subnorm)[0])
    yp, ys = run_post(h1p, h1s, op1, os1, f(diff_w_out)[0], f(norm_mlp)[1], f(mlp_w_up)[1], f(mlp_w_down)[1], f(norm_final), True)
    kv6 = lambda a: np.ascontiguousarray(a).reshape(a.shape[:2] + (2, 2, 64))[None]
    cmp_p, sel_p, win_p = kv6(zp[:, :, 1024:1280]), kv6(zp[:, :, 1280:1536]), kv6(zp[:, SEQ - WIN:, 1536:1792])
    cmp_s, sel_s = kv6(zs[:, :, 1024:1280]), kv6(zs[:, :, 1280:1536])
    win_s = win_s5[None]
    dkv = lambda z1: np.ascontiguousarray(z1[:, :, 1024:3072]).reshape(z1.shape[:2] + (2, 8, 128))[None]
    return (yp, ys, cmp_p, cmp_s, sel_p, sel_s, win_p, win_s, dkv(z1p), dkv(z1s))
```
